# Optimizing a Trainium2 kernel written in Bass

```python
import jax, jax.numpy as jnp
from jax import lax
import numpy as np

D_MODEL = 1024
BATCH = 2
SEQ = 8192
DEPTH = 4

GRID_W = 64
CTX_LEN = 256
EPS = 1e-6
NEG_INF = -1e30

NA_HEADS = 8
NA_HEAD_DIM = 64
NA_WIDTH = NA_HEADS * NA_HEAD_DIM
NA_KH = 8
NA_KW = 16
SGU_GROUPS = 8
SGU_WIDTH = D_MODEL - NA_WIDTH
SGU_GROUP_DIM = SGU_WIDTH // SGU_GROUPS
SGU_CHUNK = 128
HYB_IN = 3 * NA_WIDTH + 2 * SGU_WIDTH

RET_HEADS = 8
RET_QK_DIM = D_MODEL // RET_HEADS
RET_V_DIM = 2 * RET_QK_DIM
RET_QK_WIDTH = RET_HEADS * RET_QK_DIM
RET_V_WIDTH = RET_HEADS * RET_V_DIM
RET_CHUNK = 128
RET_IN = 2 * RET_QK_WIDTH + 2 * RET_V_WIDTH
ROPE_BASE = 10000.0

D_FF = 2816
CONV_W = 3

N_EVEN = (DEPTH + 1) // 2
N_ODD = DEPTH // 2

kernel_name = "hybrid_na_sgu_retention_dit_trunk"


def rmsnorm(x, g):
    xf = x.astype(jnp.float32)
    y = xf * lax.rsqrt(jnp.mean(xf * xf, axis=-1, keepdims=True) + EPS)
    return (y * g.astype(jnp.float32)).astype(x.dtype)


def modulate(x, shift, scale):
    return x * (1 + scale) + shift


def axial_rope(n_tokens, head_dim):
    t = jnp.arange(n_tokens)
    row = (t // GRID_W).astype(jnp.float32)
    col = (t % GRID_W).astype(jnp.float32)
    n_freq = head_dim // 4
    inv = ROPE_BASE ** (-jnp.arange(n_freq, dtype=jnp.float32) / n_freq)
    ang = jnp.concatenate([row[:, None] * inv, col[:, None] * inv], axis=-1)
    return jnp.cos(ang), jnp.sin(ang)


def apply_rope(x, cos, sin):
    x1, x2 = jnp.split(x, 2, axis=-1)
    c = cos[None, :, None, :]
    s = sin[None, :, None, :]
    return jnp.concatenate([x1 * c - x2 * s, x1 * s + x2 * c], axis=-1).astype(x.dtype)


def neighbourhood_attention(q, k, v, k_ctx, v_ctx, rpb):
    B, T, H, dh = q.shape
    rows = T // GRID_W
    kh = min(NA_KH, rows)
    kw = NA_KW
    scale = dh ** -0.5
    qg = q.reshape(B, rows, GRID_W, H, dh)
    kg = k.reshape(B, rows, GRID_W, H, dh)
    vg = v.reshape(B, rows, GRID_W, H, dh)
    r = jnp.arange(rows)
    r0 = jnp.clip(r - kh // 2, 0, rows - kh)
    key_rows = r0[:, None] + jnp.arange(kh)[None, :]
    k_blk = kg[:, key_rows].reshape(B, rows, kh * GRID_W, H, dh)
    v_blk = vg[:, key_rows].reshape(B, rows, kh * GRID_W, H, dh)
    s_nb = jnp.einsum('brqhd,brkhd->bhrqk', qg, k_blk,
                      preferred_element_type=jnp.float32) * scale
    cidx = jnp.arange(GRID_W)
    c0 = jnp.clip(cidx - kw // 2, 0, GRID_W - kw)
    in_win = (cidx[None, :] >= c0[:, None]) & (cidx[None, :] < c0[:, None] + kw)
    mask = jnp.broadcast_to(in_win[:, None, :], (GRID_W, kh, GRID_W)).reshape(GRID_W, kh * GRID_W)
    dr = key_rows - r[:, None] + (NA_KH - 1)
    dc = jnp.clip(cidx[None, :] - cidx[:, None], -(kw - 1), kw - 1) + (kw - 1)
    bias = rpb[:, dr[:, None, :, None], dc[None, :, None, :]]
    bias = bias.reshape(H, rows, GRID_W, kh * GRID_W).astype(jnp.float32)
    s_nb = jnp.where(mask, s_nb + bias[None], NEG_INF)
    s_cx = jnp.einsum('brqhd,bchd->bhrqc', qg, k_ctx,
                      preferred_element_type=jnp.float32) * scale
    p = jax.nn.softmax(jnp.concatenate([s_nb, s_cx], axis=-1), axis=-1)
    n_nb = kh * GRID_W
    p_nb = p[..., :n_nb].astype(v.dtype)
    p_cx = p[..., n_nb:].astype(v.dtype)
    o = (jnp.einsum('bhrqk,brkhd->brqhd', p_nb, v_blk)
         + jnp.einsum('bhrqc,bchd->brqhd', p_cx, v_ctx))
    return o.reshape(B, T, H * dh).astype(q.dtype)


def context_attention(q, k, v):
    B, L, H, dh = q.shape
    s = jnp.einsum('bqhd,bkhd->bhqk', q, k, preferred_element_type=jnp.float32) * dh ** -0.5
    p = jax.nn.softmax(s, axis=-1).astype(v.dtype)
    return jnp.einsum('bhqk,bkhd->bqhd', p, v).reshape(B, L, H * dh).astype(q.dtype)


def spatial_gating(u, v, w_s, b_s):
    B, N, _ = v.shape
    vf = v.astype(jnp.float32)
    mu = jnp.mean(vf, axis=-1, keepdims=True)
    var = jnp.mean(jnp.square(vf - mu), axis=-1, keepdims=True)
    vn = ((vf - mu) * lax.rsqrt(var + EPS)).astype(v.dtype)
    vc = vn.reshape(B, N // SGU_CHUNK, SGU_CHUNK, SGU_GROUPS, SGU_GROUP_DIM)
    mixed = jnp.einsum('gpq,bnqgc->bnpgc', w_s, vc) + b_s.T[None, None, :, :, None]
    return u * mixed.reshape(B, N, SGU_WIDTH)


def hybrid_na_sgu(h_ctx, h_lat, w_in, rpb, w_s, b_s, w_out, ctx_out):
    heads = lambda a: a.reshape(a.shape[0], a.shape[1], NA_HEADS, NA_HEAD_DIM)
    splits = [NA_WIDTH, 2 * NA_WIDTH, 3 * NA_WIDTH, 3 * NA_WIDTH + SGU_WIDTH]
    q, k, v, u, g = jnp.split(h_lat @ w_in, splits, axis=-1)
    if ctx_out:
        qc, kc, vc, uc, gc = jnp.split(h_ctx @ w_in, splits, axis=-1)
    else:
        kc, vc = jnp.split(h_ctx @ w_in[:, NA_WIDTH:3 * NA_WIDTH], 2, axis=-1)
    a_lat = neighbourhood_attention(heads(q), heads(k), heads(v), heads(kc), heads(vc), rpb)
    s_lat = spatial_gating(jax.nn.gelu(u), jax.nn.gelu(g), w_s, b_s)
    y_lat = jnp.concatenate([a_lat, s_lat], axis=-1) @ w_out
    y_ctx = None
    if ctx_out:
        a_ctx = context_attention(heads(qc), heads(kc), heads(vc))
        s_ctx = spatial_gating(jax.nn.gelu(uc), jax.nn.gelu(gc), w_s, b_s)
        y_ctx = jnp.concatenate([a_ctx, s_ctx], axis=-1) @ w_out
    return y_ctx, y_lat


def retention_chunked(q, k, v, lg, state0):
    B, N, H, dk = q.shape
    dv = v.shape[-1]
    C = RET_CHUNK
    n = N // C
    to_chunks = lambda a: a.reshape(B, n, C, H, a.shape[-1]).transpose(1, 0, 3, 2, 4)
    pos = jnp.arange(C, dtype=jnp.float32)
    diff = pos[:, None] - pos[None, :]
    lower = diff >= 0
    decay_intra = jnp.where(lower[None], jnp.exp(lg[:, None, None] * jnp.where(lower, diff, 0.0)[None]), 0.0)
    decay_q = jnp.exp(lg[:, None] * (pos + 1.0))[None, :, :, None]
    decay_k = jnp.exp(lg[:, None] * (C - 1.0 - pos))[None, :, :, None]
    decay_chunk = jnp.exp(lg * C)[None, :, None, None]

    def step(S, blk):
        qb, kb, vb = blk
        qf = qb.astype(jnp.float32)
        kf = kb.astype(jnp.float32)
        vf = vb.astype(jnp.float32)
        s = jnp.einsum('bhqd,bhkd->bhqk', qf, kf) * decay_intra
        o = jnp.einsum('bhqk,bhkv->bhqv', s, vf) + jnp.einsum('bhqd,bhdv->bhqv', qf * decay_q, S)
        S = S * decay_chunk + jnp.einsum('bhkd,bhkv->bhdv', kf * decay_k, vf)
        return S, o

    _, o = lax.scan(step, state0, (to_chunks(q), to_chunks(k), to_chunks(v)))
    return o.transpose(1, 0, 3, 2, 4).reshape(B, N, H, dv)


def retention_final_state(k, v, lg):
    L = k.shape[1]
    w = jnp.exp(lg[None, :] * (L - 1.0 - jnp.arange(L, dtype=jnp.float32))[:, None])
    return jnp.einsum('blhd,blhv->bhdv', k.astype(jnp.float32) * w[None, :, :, None], v.astype(jnp.float32))


def retention_output(o, g, gn_g, w_out):
    B, N = o.shape[0], o.shape[1]
    mu = jnp.mean(o, axis=-1, keepdims=True)
    var = jnp.mean(jnp.square(o - mu), axis=-1, keepdims=True)
    on = ((o - mu) * lax.rsqrt(var + EPS)).reshape(B, N, RET_V_WIDTH) * gn_g.astype(jnp.float32)
    return (jax.nn.silu(g) * on.astype(g.dtype)) @ w_out


def retention_mixer(h_ctx, h_lat, w_in, log_decay, gn_g, w_out, ctx_out):
    B, T, _ = h_lat.shape
    L = h_ctx.shape[1]
    H, dk, dv = RET_HEADS, RET_QK_DIM, RET_V_DIM
    splits = [RET_QK_WIDTH, 2 * RET_QK_WIDTH, 2 * RET_QK_WIDTH + RET_V_WIDTH]
    q, k, v, g = jnp.split(h_lat @ w_in, splits, axis=-1)
    cos, sin = axial_rope(T, dk)
    q = apply_rope(q.reshape(B, T, H, dk), cos, sin)
    k = apply_rope(k.reshape(B, T, H, dk) * dk ** -0.5, cos, sin)
    v = v.reshape(B, T, H, dv)
    if ctx_out:
        qc, kc, vc, gc = jnp.split(h_ctx @ w_in, splits, axis=-1)
        qc = qc.reshape(B, L, H, dk)
    else:
        kc, vc = jnp.split(h_ctx @ w_in[:, RET_QK_WIDTH:2 * RET_QK_WIDTH + RET_V_WIDTH], [RET_QK_WIDTH], axis=-1)
    kc = kc.reshape(B, L, H, dk) * dk ** -0.5
    vc = vc.reshape(B, L, H, dv)
    outs_lat = []
    outs_ctx = []
    for d in range(2):
        lg = log_decay[d].astype(jnp.float32)
        rev = (lambda a: a[:, ::-1]) if d == 1 else (lambda a: a)
        s_ctx = retention_final_state(rev(kc), rev(vc), lg)
        outs_lat.append(rev(retention_chunked(rev(q), rev(k), rev(v), lg, s_ctx)))
        if ctx_out:
            outs_ctx.append(rev(retention_chunked(rev(qc), rev(kc), rev(vc), lg, jnp.zeros_like(s_ctx))))
    y_lat = retention_output(outs_lat[0] + outs_lat[1], g, gn_g, w_out)
    y_ctx = None
    if ctx_out:
        y_ctx = retention_output(outs_ctx[0] + outs_ctx[1], gc, gn_g, w_out)
    return y_ctx, y_lat


def dwconv3(x, w, b):
    xp = jnp.pad(x, ((0, 0), (1, 1), (0, 0)))
    return xp[:, :-2] * w[0] + xp[:, 1:-1] * w[1] + xp[:, 2:] * w[2] + b


def conv_ffn(h, w_in, conv_w, conv_b, w_out):
    a = dwconv3(h @ w_in, conv_w, conv_b)
    gate, up = jnp.split(a, 2, axis=-1)
    return (jax.nn.silu(gate) * up) @ w_out


def setup_inputs(seed: int = 0) -> dict:
    key = jax.random.key(seed)
    ks = jax.random.split(key, 20)
    nrm = lambda k, shape, s: jax.random.normal(k, shape, jnp.float32) * s
    base_decay = np.log1p(-2.0 ** (-5.0 - np.arange(RET_HEADS))).astype(np.float32)
    return {
        "x": nrm(ks[0], (BATCH, SEQ, D_MODEL), 1.0),
        "c": nrm(ks[1], (BATCH, D_MODEL), 1.0),
        "ctx": nrm(ks[2], (BATCH, CTX_LEN, D_MODEL), 1.0),
        "c_ctx": nrm(ks[3], (D_MODEL,), 1.0),
        "ada_w": nrm(ks[4], (DEPTH, D_MODEL, 6 * D_MODEL), 0.5 * D_MODEL ** -0.5),
        "ada_b": nrm(ks[5], (DEPTH, 6 * D_MODEL), 0.02),
        "norm_g": 1.0 + nrm(ks[6], (DEPTH, 4, D_MODEL), 0.1),
        "hyb_w_in": nrm(ks[7], (N_EVEN, D_MODEL, HYB_IN), D_MODEL ** -0.5),
        "na_rpb": nrm(ks[8], (N_EVEN, NA_HEADS, 2 * NA_KH - 1, 2 * NA_KW - 1), 0.1),
        "sgu_w": nrm(ks[9], (N_EVEN, SGU_GROUPS, SGU_CHUNK, SGU_CHUNK), SGU_CHUNK ** -0.5),
        "sgu_b": 1.0 + nrm(ks[10], (N_EVEN, SGU_GROUPS, SGU_CHUNK), 0.1),
        "hyb_w_out": nrm(ks[11], (N_EVEN, D_MODEL, D_MODEL), D_MODEL ** -0.5),
        "ret_w_in": nrm(ks[12], (N_ODD, D_MODEL, RET_IN), D_MODEL ** -0.5),
        "ret_log_decay": jnp.asarray(base_decay)[None, None, :] * (1.0 + nrm(ks[13], (N_ODD, 2, RET_HEADS), 0.05)),
        "ret_gn_g": 1.0 + nrm(ks[14], (N_ODD, RET_V_WIDTH), 0.1),
        "ret_w_out": nrm(ks[15], (N_ODD, RET_V_WIDTH, D_MODEL), RET_V_WIDTH ** -0.5),
        "ffn_w_in": nrm(ks[16], (DEPTH, D_MODEL, 2 * D_FF), D_MODEL ** -0.5),
        "ffn_conv_w": nrm(ks[17], (DEPTH, CONV_W, 2 * D_FF), CONV_W ** -0.5),
        "ffn_conv_b": nrm(ks[18], (DEPTH, 2 * D_FF), 0.02),
        "ffn_w_out": nrm(ks[19], (DEPTH, D_FF, D_MODEL), D_FF ** -0.5),
    }


def reference(x, c, ctx, c_ctx, ada_w, ada_b, norm_g, hyb_w_in, na_rpb, sgu_w, sgu_b, hyb_w_out,
              ret_w_in, ret_log_decay, ret_gn_g, ret_w_out, ffn_w_in, ffn_conv_w, ffn_conv_b, ffn_w_out):
    xc = ctx
    silu_c = jax.nn.silu(c)
    silu_cc = jax.nn.silu(c_ctx)
    for i in range(DEPTH):
        ctx_out = i < DEPTH - 1
        j = i // 2
        mod_lat = (silu_c @ ada_w[i] + ada_b[i])[:, None, :]
        mod_ctx = (silu_cc @ ada_w[i] + ada_b[i])[None, None, :]
        sh1, sc1, g1, sh2, sc2, g2 = jnp.split(mod_lat, 6, axis=-1)
        csh1, csc1, cg1, csh2, csc2, cg2 = jnp.split(mod_ctx, 6, axis=-1)
        h_lat = modulate(rmsnorm(x, norm_g[i, 0]), sh1, sc1)
        h_ctx = modulate(rmsnorm(xc, norm_g[i, 0]), csh1, csc1)
        if i % 2 == 0:
            y_ctx, y_lat = hybrid_na_sgu(h_ctx, h_lat, hyb_w_in[j], na_rpb[j], sgu_w[j], sgu_b[j],
                                         hyb_w_out[j], ctx_out)
        else:
            y_ctx, y_lat = retention_mixer(h_ctx, h_lat, ret_w_in[j], ret_log_decay[j], ret_gn_g[j],
                                           ret_w_out[j], ctx_out)
        x = x + g1 * rmsnorm(y_lat, norm_g[i, 1])
        h = modulate(rmsnorm(x, norm_g[i, 2]), sh2, sc2)
        x = x + g2 * rmsnorm(conv_ffn(h, ffn_w_in[i], ffn_conv_w[i], ffn_conv_b[i], ffn_w_out[i]), norm_g[i, 3])
        if ctx_out:
            xc = xc + cg1 * rmsnorm(y_ctx, norm_g[i, 1])
            hc = modulate(rmsnorm(xc, norm_g[i, 2]), csh2, csc2)
            xc = xc + cg2 * rmsnorm(conv_ffn(hc, ffn_w_in[i], ffn_conv_w[i], ffn_conv_b[i], ffn_w_out[i]), norm_g[i, 3])
    return x
```

```python
import contextlib
import numpy as np
import concourse.bass as bass
import concourse.mybir as mybir
from concourse.bass_utils import run_bass_kernel_spmd

F32 = mybir.dt.float32
BF16 = mybir.dt.bfloat16
AF = mybir.ActivationFunctionType
ALU = mybir.AluOpType
AX = mybir.AxisListType


class Buf:
    def __init__(self, t=None, name=""):
        self.t = t
        self.name = name
        self.w = None
        self.r = []
        self.ld = None
        self.st = None


class SemCounter:
    def __init__(self, sem, key, shared=False):
        self.sem = sem
        self.key = key
        self.cnt = 0
        self.shared = shared


class Sched:
    def __init__(self, nc, strict_same=True):
        self.nc = nc
        self.es = contextlib.ExitStack()
        self.stack = [self.es]
        self.eng = {"pe": nc.tensor, "act": nc.scalar, "dve": nc.vector, "pool": nc.gpsimd, "sp": nc.sync}
        self.sem = {}
        self.cnt = {}
        for e in self.eng:
            self.sem[e] = self.es.enter_context(nc.semaphore("s_" + e))
            self.cnt[e] = 0
        self.waited = {e: {} for e in self.eng}
        self.strict_same = strict_same
        self.out_events = []
        self.nsem = 0
        self.n_ops = 0
        self.uid = 0

    def sb(self, name, shape, dtype):
        self.uid += 1
        return Buf(self.stack[-1].enter_context(self.nc.sbuf_tensor(f"sb_{name}_{self.uid}", list(shape), dtype)), name)

    def ps(self, name, shape, dtype=F32):
        self.uid += 1
        return Buf(self.stack[-1].enter_context(self.nc.psum_tensor(f"ps_{name}_{self.uid}", list(shape), dtype)), name)

    @contextlib.contextmanager
    def scope(self):
        es = contextlib.ExitStack()
        self.stack.append(es)
        try:
            yield es
        finally:
            self.stack.pop()
            es.close()

    def tok(self, name=""):
        return Buf(None, name)

    def new_sem(self, name):
        self.nsem += 1
        return self.es.enter_context(self.nc.semaphore(f"{name}_{self.nsem}"))

    def new_counter(self, name, shared=False):
        sem = self.new_sem(name)
        return SemCounter(sem, f"D{self.nsem}", shared)

    def shared_toks(self, name, n, k=8):
        cs = [self.new_counter(f"{name}{i}", shared=True) for i in range(k)]
        toks = []
        for i in range(n):
            t = Buf(None, f"{name}{i}")
            t.ld = cs[i % k]
            toks.append(t)
        return toks

    def _deps(self, reads, writes):
        ev = []
        for b in reads:
            if b.w is not None:
                ev.append(b.w)
        for b in writes:
            if b.w is not None:
                ev.append(b.w)
            ev.extend(b.r)
        return ev

    def _emit_waits(self, e, events):
        need = {}
        for (k, sem, v) in events:
            if k == e and not self.strict_same:
                continue
            if self.waited[e].get(k, 0) >= v:
                continue
            if k not in need or need[k][1] < v:
                need[k] = (sem, v)
        for k, (sem, v) in need.items():
            self.eng[e].wait_ge(sem, v)
            self.waited[e][k] = v

    def op(self, e, fn, reads=(), writes=()):
        self._emit_waits(e, self._deps(reads, writes))
        ins = fn()
        self.cnt[e] += 1
        ins.then_inc(self.sem[e], 1)
        evt = (e, self.sem[e], self.cnt[e])
        for b in writes:
            b.w = evt
            b.r = []
        for b in reads:
            if b not in writes:
                b.r.append(evt)
        self.n_ops += 1
        return ins

    def dma(self, q, out_ap, in_ap, reads=(), writes=(), out=False, **kw):
        if writes:
            b = writes[0]
            if b.ld is None:
                b.ld = self.new_counter("ld_" + b.name)
            sc = b.ld
        else:
            b = reads[0]
            if b.st is None:
                b.st = self.new_counter("st_" + b.name)
            sc = b.st
        ev = self._deps(reads, writes)
        if sc.shared and sc.cnt > 0:
            ev.append((sc.key, sc.sem, sc.cnt))
        self._emit_waits(q, ev)
        ins = self.eng[q].dma_start(out=out_ap, in_=in_ap, **kw)
        sc.cnt += 16
        ins.then_inc(sc.sem, 16)
        evt = (sc.key, sc.sem, sc.cnt)
        for b in writes:
            b.w = evt
            b.r = []
        for b in reads:
            if b not in writes:
                b.r.append(evt)
        if out:
            self.out_events.append(evt)
        return ins

    def barrier(self, bufs=()):
        ev = [(e, self.sem[e], self.cnt[e]) for e in self.eng if self.cnt[e] > 0]
        for b in bufs:
            if b.w is not None:
                ev.append(b.w)
            ev.extend(b.r)
        for e in self.eng:
            self._emit_waits(e, [x for x in ev if x[0] != e])

    def finish(self):
        self._emit_waits("sp", self.out_events)
        self.es.close()


D = 1024
EPS = 1e-6
D_FF = 2816
NFC = 22


class Rot:
    def __init__(self, bufs):
        self.bufs = bufs
        self.i = 0

    def next(self):
        b = self.bufs[self.i % len(self.bufs)]
        self.i += 1
        return b


def rot_sb(S, name, shape, dtype, n):
    return Rot([S.sb(f"{name}{i}", shape, dtype) for i in range(n)])


def rot_ps(S, name, shape, dtype, n):
    return Rot([S.ps(f"{name}{i}", shape, dtype) for i in range(n)])


class NormCtx:
    def __init__(self, S, ident, epsc, nrot=2):
        self.S = S
        self.ident = ident
        self.epsc = epsc
        self.sq = rot_sb(S, "nsq", [128, D], F32, 1)
        self.ss = rot_sb(S, "nss", [128, 1], F32, nrot)
        self.xn = rot_sb(S, "nxn", [128, D], BF16, nrot)
        self.pT = rot_ps(S, "npT", [128, 8, 128], BF16, 2)


def emit_rstd(S, nc, src_ap, src_buf, P, ss, sq, epsc, width=D):
    S.op("act", lambda: nc.scalar.activation(out=sq.t[:P, :width], in_=src_ap, func=AF.Square, accum_out=ss.t[:P, :]),
         reads=[src_buf], writes=[sq, ss])
    S.op("act", lambda: nc.scalar.activation(out=ss.t[:P, :], in_=ss.t[:P, :], func=AF.Sqrt, bias=epsc.t[:P, :], scale=1.0 / width),
         reads=[ss, epsc], writes=[ss])
    S.op("dve", lambda: nc.vector.reciprocal(ss.t[:P, :], ss.t[:P, :]), reads=[ss], writes=[ss])


def emit_norm_T(S, nc, N, xbuf, P, gain, shift, gi, hT, col0):
    ss = N.ss.next()
    sq = N.sq.next()
    xn = N.xn.next()
    pT = N.pT.next()
    emit_rstd(S, nc, xbuf.t[:P, :], xbuf, P, ss, sq, N.epsc)
    S.op("act", lambda: nc.scalar.activation(out=xn.t[:P, :], in_=xbuf.t[:P, :], func=AF.Copy, scale=ss.t[:P, :]),
         reads=[xbuf, ss], writes=[xn])
    for c in range(8):
        S.op("pe", lambda: nc.tensor.transpose(pT.t[:, c, :P], xn.t[:P, c * 128:(c + 1) * 128], N.ident.t[:P, :P]),
             reads=[xn, N.ident], writes=[pT])
    for c in range(8):
        e = "dve" if c % 2 == 0 else "pool_no"
        S.op("dve", lambda: nc.vector.tensor_scalar(hT.t[:, c, col0:col0 + P], pT.t[:, c, :P], gain.t[:, gi, c:c + 1], shift.t[:, gi, c:c + 1], ALU.mult, ALU.add),
             reads=[pT, gain, shift], writes=[hT])


def load_cast(S, nc, q, dst_ap, dst_buf, src_ap, stage, caste="pool"):
    S.dma(q, stage.t[:], src_ap, writes=[stage])
    eng = {"pool": nc.gpsimd, "dve": nc.vector, "act": nc.scalar}[caste]
    if caste == "act":
        S.op("act", lambda: nc.scalar.copy(dst_ap, stage.t[:]), reads=[stage], writes=[dst_buf])
    else:
        S.op(caste, lambda: eng.tensor_copy(dst_ap, stage.t[:]), reads=[stage], writes=[dst_buf])


def f_ntok(n_main, n_ctx):
    return n_main * 128 + 2 + n_ctx * 128


def build_F(KO, passes):
    nc = bass.Bass("TRN2", target_bir_lowering=False)
    KC = KO // 128
    dr = lambda name, shape, dt=F32, kind="ExternalInput": nc.dram_tensor(name, list(shape), dt, kind=kind).ap()
    w_o_d = dr("w_o", [KO, D])
    w_in_d = dr("w_in", [D, 2 * D_FF])
    w_out_d = dr("w_out", [D_FF, D])
    convw_d = dr("convw", [128, 2 * NFC, 3])
    convb_d = dr("convb", [128, 2 * NFC])
    rows_d = dr("rows", [6, 128, D])
    cols_d = dr("cols", [128, 5, 8])
    hmask_d = dr("hmask", [128, 2 * len(passes)])
    ident_d = dr("ident", [128, 128])
    S = Sched(nc)
    ident_f = S.sb("ident_f", [128, 128], F32)
    ident = S.sb("ident_b", [128, 128], BF16)
    S.dma("sp", ident_f.t[:], ident_d, writes=[ident_f])
    S.op("dve", lambda: nc.vector.tensor_copy(ident.t[:], ident_f.t[:]), reads=[ident_f], writes=[ident])
    epsc = S.sb("epsc", [128, 1], F32)
    S.op("dve", lambda: nc.vector.memset(epsc.t[:], EPS), writes=[epsc])
    cols = S.sb("cols", [128, 5, 8], F32)
    S.dma("sp", cols.t[:], cols_d, writes=[cols])
    gain = S.sb("gain", [128, 2, 8], F32)
    shift = S.sb("shift", [128, 2, 8], F32)
    for r in range(2):
        S.op("dve", lambda: nc.vector.scalar_tensor_tensor(out=gain.t[:, r, :], in0=cols.t[:, 1 + r, :], scalar=1.0, in1=cols.t[:, 0, :], op0=ALU.add, op1=ALU.mult),
             reads=[cols], writes=[gain])
        S.op("dve", lambda: nc.vector.tensor_copy(shift.t[:, r, :], cols.t[:, 3 + r, :]), reads=[cols], writes=[shift])
    convw = S.sb("convw", [128, 2 * NFC, 3], F32)
    convb = S.sb("convb", [128, 2 * NFC], F32)
    hmask = S.sb("hmask", [128, 2 * len(passes)], F32)
    S.dma("sp", convw.t[:], convw_d, writes=[convw])
    S.dma("sp", convb.t[:], convb_d, writes=[convb])
    S.dma("sp", hmask.t[:], hmask_d, writes=[hmask])
    GB = S.sb("GB", [128, 4, D], F32)
    with contextlib.ExitStack() as es0:
        tn = [Buf(es0.enter_context(nc.sbuf_tensor(f"sb_tn{i}", [128, D], F32)), f"tn{i}") for i in range(2)]
        for i in range(2):
            S.dma("sp", tn[i].t[:], rows_d[i], writes=[tn[i]])
        for k in range(4):
            S.dma("act", GB.t[:, k, :], rows_d[2 + k], writes=[GB])
        for k in range(4):
            S.op("dve", lambda: nc.vector.tensor_tensor(out=GB.t[:, k, :], in0=GB.t[:, k, :], in1=tn[k // 2].t[:], op=ALU.mult),
                 reads=[GB, tn[k // 2]], writes=[GB])
        S.barrier(tn + [GB])
    N = NormCtx(S, ident, epsc)
    stage = rot_sb(S, "stage", [128, 8, 256], F32, 2)
    pY = rot_ps(S, "pY", [128, D], F32, 2)
    pG = rot_ps(S, "pG", [128, 512], F32, 2)

    for pi, (n_main, n_ctx) in enumerate(passes):
        NTOK = f_ntok(n_main, n_ctx)
        NOUT = (n_main + n_ctx) * 128
        oT_d = dr(f"oT{pi}", [KO, NTOK], BF16)
        x_d = dr(f"x{pi}", [NTOK, D])
        xo_d = dr(f"xo{pi}", [NOUT, D], kind="ExternalOutput")
        xmid_d = dr(f"xmid{pi}", [NTOK, D], kind="Internal")
        xmid_tok = S.tok(f"xmid{pi}")
        hr = 1 + n_main * 128
        NA = n_main * 128 + 4 + n_ctx * 128
        tiles = [(t * 128, 128, 0, 1 + t * 128, t * 128) for t in range(n_main)]
        tiles.append((n_main * 128, 2, 0, None, None))
        tiles += [(n_main * 128 + 2 + j * 128, 128, 1, hr + 2 + j * 128, n_main * 128 + j * 128) for j in range(n_ctx)]
        with contextlib.ExitStack() as esP:
            sbp = lambda name, shape, dt: Buf(esP.enter_context(nc.sbuf_tensor(f"sb_{name}_{pi}", list(shape), dt)), f"{name}_{pi}")
            hid = sbp("hid", [128, NFC, NA], BF16)
            with contextlib.ExitStack() as es2:
                sb2 = lambda name, shape, dt: Buf(es2.enter_context(nc.sbuf_tensor(f"sb_{name}_{pi}", list(shape), dt)), f"{name}_{pi}")
                hT = sb2("hT", [128, 8, NTOK], BF16)
                with contextlib.ExitStack() as es1:
                    sb1 = lambda name, shape, dt: Buf(es1.enter_context(nc.sbuf_tensor(f"sb_{name}_{pi}", list(shape), dt)), f"{name}_{pi}")
                    oTs = sb1("oTs", [128, KC, NTOK], BF16)
                    oT_v = oT_d.rearrange("(c p) n -> p c n", p=128)
                    for c in range(KC):
                        S.dma("sp" if c % 2 == 0 else "act", oTs.t[:, c, :], oT_v[:, c, :], writes=[oTs])
                    wo = sb1("wo", [128, KC, D], BF16)
                    wo_v = w_o_d.rearrange("(c p) n -> p c n", p=128)
                    for c in range(KC):
                        for h4 in range(4):
                            st = stage.next()
                            S.dma("sp", st.t[:, 0, :], wo_v[:, c, h4 * 256:(h4 + 1) * 256], writes=[st])
                            S.op("pool", lambda: nc.gpsimd.tensor_copy(wo.t[:, c, h4 * 256:(h4 + 1) * 256], st.t[:, 0, :]), reads=[st], writes=[wo])
                    xts = [sb1(f"xt{i}", [128, D], F32) for i in range(2)]
                    xms = [sb1(f"xm{i}", [128, D], F32) for i in range(2)]
                    for ti, (col0, P, r, acol, orow) in enumerate(tiles):
                        xt = xts[ti % 2]
                        xm = xms[ti % 2]
                        S.dma("sp", xt.t[:P, :], x_d[col0:col0 + P, :], writes=[xt])
                        y = pY.next()
                        for half in range(2):
                            for c in range(KC):
                                S.op("pe", lambda: nc.tensor.matmul(y.t[:P, half * 512:(half + 1) * 512], lhsT=oTs.t[:, c, col0:col0 + P], rhs=wo.t[:, c, half * 512:(half + 1) * 512],
                                                                    start=(c == 0), stop=(c == KC - 1)), reads=[oTs, wo], writes=[y])
                        ss = N.ss.next()
                        sq = N.sq.next()
                        emit_rstd(S, nc, y.t[:P, :], y, P, ss, sq, epsc)
                        S.op("dve", lambda: nc.vector.scalar_tensor_tensor(out=xm.t[:P, :], in0=y.t[:P, :], scalar=ss.t[:P, 0:1], in1=GB.t[:P, r, :], op0=ALU.mult, op1=ALU.mult),
                             reads=[y, ss, GB], writes=[xm])
                        S.op("pool", lambda: nc.gpsimd.tensor_tensor(out=xm.t[:P, :], in0=xm.t[:P, :], in1=xt.t[:P, :], op=ALU.add), reads=[xm, xt], writes=[xm])
                        if acol is not None:
                            S.dma("act", xmid_d[col0:col0 + P, :], xm.t[:P, :], reads=[xm], writes=[xmid_tok])
                        emit_norm_T(S, nc, N, xm, P, gain, shift, r, hT, col0)
                    S.barrier([oTs, wo] + xts + xms)
                with contextlib.ExitStack() as es1:
                    sb1 = lambda name, shape, dt: Buf(es1.enter_context(nc.sbuf_tensor(f"sb_{name}_{pi}", list(shape), dt)), f"{name}_{pi}")
                    wblk = [sb1(f"wblk{i}", [128, 8, 256], BF16) for i in range(2)]
                    ab = [sb1("abg", [128, NA], F32), sb1("abu", [128, NA], F32)]
                    cg = sb1("cg", [128, NA], F32)
                    cu = sb1("cu", [128, NA], F32)
                    tp = sb1("tp", [128, NA], F32)
                    for a in ab:
                        S.op("pool", lambda: nc.gpsimd.memset(a.t[:], 0.0), writes=[a])
                    win_v = w_in_d.rearrange("(c p) n -> p c n", p=128)
                    groups = [(g * 512, 512, 1 + g * 512) for g in range(n_main // 4)]
                    if n_ctx:
                        groups.append((n_main * 128 + 2, n_ctx * 128, hr + 2))
                    wi = 0
                    for jb in range(NFC // 2):
                        blk = []
                        for which in range(2):
                            st = stage.next()
                            wb = wblk[which]
                            c0 = which * D_FF + jb * 256
                            S.dma("sp" if which == 0 else "act", st.t[:], win_v[:, :, c0:c0 + 256], writes=[st])
                            S.op("pool", lambda: nc.gpsimd.tensor_copy(wb.t[:], st.t[:]), reads=[st], writes=[wb])
                            blk.append(wb)
                        for jl in range(2):
                            j = jb * 2 + jl
                            for which in range(2):
                                wb = blk[which]
                                a = ab[which]
                                fc = which * NFC + j
                                for (tc0, n, ac0) in groups:
                                    pg = pG.next()
                                    for c in range(8):
                                        S.op("pe", lambda: nc.tensor.matmul(pg.t[:, :n], lhsT=wb.t[:, c, jl * 128:(jl + 1) * 128], rhs=hT.t[:, c, tc0:tc0 + n], start=(c == 0), stop=(c == 7)),
                                             reads=[wb, hT], writes=[pg])
                                    if wi % 2 == 0:
                                        S.op("act", lambda: nc.scalar.copy(a.t[:, ac0:ac0 + n], pg.t[:, :n]), reads=[pg], writes=[a])
                                    else:
                                        S.op("dve", lambda: nc.vector.tensor_copy(a.t[:, ac0:ac0 + n], pg.t[:, :n]), reads=[pg], writes=[a])
                                    wi += 1
                                pg = pG.next()
                                hc = n_main * 128
                                for c in range(8):
                                    S.op("pe", lambda: nc.tensor.matmul(pg.t[:, :2], lhsT=wb.t[:, c, jl * 128:(jl + 1) * 128], rhs=hT.t[:, c, hc:hc + 2], start=(c == 0), stop=(c == 7)),
                                         reads=[wb, hT], writes=[pg])
                                S.op("dve", lambda: nc.vector.tensor_tensor(out=a.t[:, 0:1], in0=pg.t[:, 0:1], in1=hmask.t[:, 2 * pi:2 * pi + 1], op=ALU.mult), reads=[pg, hmask], writes=[a])
                                S.op("dve", lambda: nc.vector.tensor_tensor(out=a.t[:, hr:hr + 1], in0=pg.t[:, 1:2], in1=hmask.t[:, 2 * pi + 1:2 * pi + 2], op=ALU.mult), reads=[pg, hmask], writes=[a])
                                cc = cg if which == 0 else cu
                                S.op("act", lambda: nc.scalar.activation(out=cc.t[:, 1:NA - 1], in_=a.t[:, 1:NA - 1], func=AF.Identity, bias=convb.t[:, fc:fc + 1], scale=convw.t[:, fc, 1:2]),
                                     reads=[a, convw, convb], writes=[cc])
                                S.op("dve", lambda: nc.vector.scalar_tensor_tensor(out=cc.t[:, 1:NA - 1], in0=a.t[:, 0:NA - 2], scalar=convw.t[:, fc, 0:1], in1=cc.t[:, 1:NA - 1], op0=ALU.mult, op1=ALU.add),
                                     reads=[a, convw, cc], writes=[cc])
                                if which == 0:
                                    S.op("dve", lambda: nc.vector.scalar_tensor_tensor(out=cc.t[:, 1:NA - 1], in0=a.t[:, 2:NA], scalar=convw.t[:, fc, 2:3], in1=cc.t[:, 1:NA - 1], op0=ALU.mult, op1=ALU.add),
                                         reads=[a, convw, cc], writes=[cc])
                                else:
                                    S.op("pool", lambda: nc.gpsimd.tensor_scalar(tp.t[:, 1:NA - 1], a.t[:, 2:NA], convw.t[:, fc, 2:3], None, ALU.mult), reads=[a, convw], writes=[tp])
                                    S.op("pool", lambda: nc.gpsimd.tensor_tensor(out=cc.t[:, 1:NA - 1], in0=cc.t[:, 1:NA - 1], in1=tp.t[:, 1:NA - 1], op=ALU.add), reads=[cc, tp], writes=[cc])
                            S.op("act", lambda: nc.scalar.activation(out=cg.t[:, 1:NA - 1], in_=cg.t[:, 1:NA - 1], func=AF.Silu), reads=[cg], writes=[cg])
                            S.op("pool", lambda: nc.gpsimd.tensor_tensor(out=hid.t[:, j, 1:NA - 1], in0=cg.t[:, 1:NA - 1], in1=cu.t[:, 1:NA - 1], op=ALU.mult), reads=[cg, cu], writes=[hid])
                    S.barrier([cg, cu, tp, hT] + ab + wblk)
            with contextlib.ExitStack() as es1:
                sb1 = lambda name, shape, dt: Buf(es1.enter_context(nc.sbuf_tensor(f"sb_{name}_{pi}", list(shape), dt)), f"{name}_{pi}")
                wout = sb1("wout", [128, NFC, D], BF16)
                wout_v = w_out_d.rearrange("(c p) n -> p c n", p=128)
                for j in range(NFC):
                    for h4 in range(4):
                        st = stage.next()
                        S.dma("sp" if h4 % 2 == 0 else "act", st.t[:, 0, :], wout_v[:, j, h4 * 256:(h4 + 1) * 256], writes=[st])
                        S.op("pool", lambda: nc.gpsimd.tensor_copy(wout.t[:, j, h4 * 256:(h4 + 1) * 256], st.t[:, 0, :]), reads=[st], writes=[wout])
                xms = [sb1(f"xm3{i}", [128, D], F32) for i in range(2)]
                xos = [sb1(f"xo3{i}", [128, D], F32) for i in range(2)]
                k3 = 0
                for (col0, P, r, acol, orow) in tiles:
                    if acol is None:
                        continue
                    xm = xms[k3 % 2]
                    xo = xos[k3 % 2]
                    k3 += 1
                    S.dma("sp", xm.t[:], xmid_d[col0:col0 + 128, :], reads=[xmid_tok], writes=[xm])
                    y = pY.next()
                    for half in range(2):
                        for j in range(NFC):
                            S.op("pe", lambda: nc.tensor.matmul(y.t[:, half * 512:(half + 1) * 512], lhsT=hid.t[:, j, acol:acol + 128], rhs=wout.t[:, j, half * 512:(half + 1) * 512],
                                                                start=(j == 0), stop=(j == NFC - 1)), reads=[hid, wout], writes=[y])
                    ss = N.ss.next()
                    sq = N.sq.next()
                    emit_rstd(S, nc, y.t[:, :], y, 128, ss, sq, epsc)
                    S.op("dve", lambda: nc.vector.scalar_tensor_tensor(out=xo.t[:], in0=y.t[:], scalar=ss.t[:, 0:1], in1=GB.t[:, 2 + r, :], op0=ALU.mult, op1=ALU.mult),
                         reads=[y, ss, GB], writes=[xo])
                    S.op("pool", lambda: nc.gpsimd.tensor_tensor(out=xo.t[:], in0=xo.t[:], in1=xm.t[:], op=ALU.add), reads=[xo, xm], writes=[xo])
                    S.dma("act", xo_d[orow:orow + 128, :], xo.t[:], reads=[xo], writes=[], out=True)
                S.barrier([wout, hid] + xms + xos)
    S.finish()
    return nc


GELU_C = 1.5957691216057308


def build_ME(n_lat=64, n_ctx=2, n_b=2):
    nc = bass.Bass("TRN2", target_bir_lowering=False)
    TPB = n_lat + n_ctx
    NT = n_b * TPB
    dr = lambda name, shape, dt=F32, kind="ExternalInput": nc.dram_tensor(name, list(shape), dt, kind=kind).ap()
    x_d = dr("x", [NT * 128, D])
    wqk_d = dr("wqk", [D, 128])
    wgu_d = dr("wgu", [D, 576])
    wv_d = dr("wv", [D, 64])
    wsT_d = dr("wsT", [128, 128])
    bs_d = dr("bs", [128, 1])
    bias_d = dr("bias", [5, 128, 576])
    cols_d = dr("cols", [128, 1 + 2 * (n_b + 1), 8])
    ident_d = dr("ident", [128, 128])
    out_d = dr("out", [NT * 128, 128], BF16, kind="ExternalOutput")
    S = Sched(nc)
    ident_f = S.sb("ident_f", [128, 128], F32)
    ident = S.sb("ident_b", [128, 128], BF16)
    S.dma("sp", ident_f.t[:], ident_d, writes=[ident_f])
    S.op("dve", lambda: nc.vector.tensor_copy(ident.t[:], ident_f.t[:]), reads=[ident_f], writes=[ident])
    epsc = S.sb("epsc", [128, 1], F32)
    S.op("dve", lambda: nc.vector.memset(epsc.t[:], EPS), writes=[epsc])
    NR = n_b + 1
    cols = S.sb("cols", [128, 1 + 2 * NR, 8], F32)
    S.dma("sp", cols.t[:], cols_d, writes=[cols])
    gain = S.sb("gain", [128, NR, 8], F32)
    shift = S.sb("shift", [128, NR, 8], F32)
    for r in range(NR):
        S.op("dve", lambda: nc.vector.scalar_tensor_tensor(out=gain.t[:, r, :], in0=cols.t[:, 1 + r, :], scalar=1.0, in1=cols.t[:, 0, :], op0=ALU.add, op1=ALU.mult),
             reads=[cols], writes=[gain])
        S.op("dve", lambda: nc.vector.tensor_copy(shift.t[:, r, :], cols.t[:, 1 + NR + r, :]), reads=[cols], writes=[shift])
    stage = S.sb("stage", [128, 8, 576], F32)
    wqk = S.sb("wqk", [128, 8, 128], BF16)
    wgu = S.sb("wgu", [128, 8, 576], BF16)
    wv = S.sb("wv", [128, 8, 64], BF16)
    for (wb, wd, n) in ((wqk, wqk_d, 128), (wgu, wgu_d, 576), (wv, wv_d, 64)):
        S.dma("sp", stage.t[:, :, :n], wd.rearrange("(c p) n -> p c n", p=128), writes=[stage])
        S.op("dve", lambda: nc.vector.tensor_copy(wb.t[:], stage.t[:, :, :n]), reads=[stage], writes=[wb])
    wsT = S.sb("wsT", [128, 128], BF16)
    S.dma("sp", stage.t[:, 0, :128], wsT_d, writes=[stage])
    S.op("dve", lambda: nc.vector.tensor_copy(wsT.t[:], stage.t[:, 0, :128]), reads=[stage], writes=[wsT])
    bs = S.sb("bs", [128, 1], F32)
    S.dma("sp", bs.t[:], bs_d, writes=[bs])
    biasT = S.sb("biasT", [128, 5, 576], F32)
    for k in range(5):
        S.dma("act", biasT.t[:, k, :], bias_d[k], writes=[biasT])
    qT = S.sb("qT", [64, NT * 128], BF16)
    kT = S.sb("kT", [64, NT * 128], BF16)
    vA = S.sb("vA", [128, NT, 64], BF16)
    oA = S.sb("oA", [128, NT, 128], BF16)
    with S.scope() as esA:
        def sbA(name, shape, dt):
            return Buf(esA.enter_context(nc.sbuf_tensor("sbA_" + name, list(shape), dt)), name)
        def psA(name, shape, dt=F32):
            return Buf(esA.enter_context(nc.psum_tensor("psA_" + name, list(shape), dt)), name)
        N = NormCtx(S, ident, epsc)
        xts = [sbA(f"xt{i}", [128, D], F32) for i in range(2)]
        hTs = [sbA(f"hT{i}", [128, 8, 128], BF16) for i in range(2)]
        pQK = psA("pQK", [64, 2, 128])
        pGU = psA("pGU", [128, 1024])
        pSG = psA("pSG", [128, 128])
        xg = sbA("xg", [128, 576], F32)
        t1 = sbA("t1", [128, 576], F32)
        gl = sbA("gl", [128, 576], F32)
        junk = sbA("junk", [128, 512], F32)
        st = sbA("st", [128, 4], F32)
        vn = sbA("vn", [128, 64], BF16)
        for ti in range(NT):
            b, tl = divmod(ti, TPB)
            r = b if tl < n_lat else n_b
            xt = xts[ti % 2]
            hT = hTs[ti % 2]
            S.dma("sp" if ti % 2 == 0 else "act", xt.t[:], x_d[ti * 128:(ti + 1) * 128, :], writes=[xt])
            emit_norm_T(S, nc, N, xt, 128, gain, shift, r, hT, 0)
            for w in range(2):
                for c in range(8):
                    S.op("pe", lambda: nc.tensor.matmul(pQK.t[:, w, :], lhsT=wqk.t[:, c, w * 64:(w + 1) * 64], rhs=hT.t[:, c, :], start=(c == 0), stop=(c == 7)), reads=[wqk, hT], writes=[pQK])
            S.op("act", lambda: nc.scalar.activation(out=qT.t[:, ti * 128:(ti + 1) * 128], in_=pQK.t[:, 0, :], func=AF.Copy, scale=0.125), reads=[pQK], writes=[qT])
            S.op("act", lambda: nc.scalar.copy(kT.t[:, ti * 128:(ti + 1) * 128], pQK.t[:, 1, :]), reads=[pQK], writes=[kT])
            for (o0, n, wb, wo0) in ((0, 512, wgu, 0), (512, 64, wgu, 512), (576, 64, wv, 0)):
                for c in range(8):
                    S.op("pe", lambda: nc.tensor.matmul(pGU.t[:, o0:o0 + n], lhsT=hT.t[:, c, :], rhs=wb.t[:, c, wo0:wo0 + n], start=(c == 0), stop=(c == 7)), reads=[wb, hT], writes=[pGU])
            S.op("act", lambda: nc.scalar.copy(vA.t[:, ti, :], pGU.t[:, 576:640]), reads=[pGU], writes=[vA])
            S.op("act", lambda: nc.scalar.copy(xg.t[:], pGU.t[:, 0:576]), reads=[pGU], writes=[xg])
            S.op("dve", lambda: nc.vector.tensor_tensor(out=t1.t[:], in0=xg.t[:], in1=xg.t[:], op=ALU.mult), reads=[xg], writes=[t1])
            S.op("dve", lambda: nc.vector.tensor_scalar(t1.t[:], t1.t[:], 0.044715, 1.0, ALU.mult, ALU.add), reads=[t1], writes=[t1])
            S.op("pool", lambda: nc.gpsimd.tensor_tensor(out=t1.t[:], in0=t1.t[:], in1=xg.t[:], op=ALU.mult), reads=[t1, xg], writes=[t1])
            S.op("act", lambda: nc.scalar.activation(out=t1.t[:], in_=t1.t[:], func=AF.Sigmoid, scale=GELU_C), reads=[t1], writes=[t1])
            S.op("pool", lambda: nc.gpsimd.tensor_tensor(out=gl.t[:], in0=t1.t[:], in1=xg.t[:], op=ALU.mult), reads=[t1, xg], writes=[gl])
            S.op("act", lambda: nc.scalar.activation(out=junk.t[:], in_=gl.t[:, 0:512], func=AF.Identity, accum_out=st.t[:, 0:1]), reads=[gl], writes=[junk, st])
            S.op("act", lambda: nc.scalar.activation(out=junk.t[:], in_=gl.t[:, 0:512], func=AF.Square, accum_out=st.t[:, 1:2]), reads=[gl], writes=[junk, st])
            S.op("dve", lambda: nc.vector.tensor_scalar(st.t[:, 0:2], st.t[:, 0:2], 1.0 / 512, None, ALU.mult), reads=[st], writes=[st])
            S.op("dve", lambda: nc.vector.tensor_tensor(out=st.t[:, 2:3], in0=st.t[:, 0:1], in1=st.t[:, 0:1], op=ALU.mult), reads=[st], writes=[st])
            S.op("dve", lambda: nc.vector.tensor_tensor(out=st.t[:, 2:3], in0=st.t[:, 1:2], in1=st.t[:, 2:3], op=ALU.subtract), reads=[st], writes=[st])
            S.op("act", lambda: nc.scalar.activation(out=st.t[:, 2:3], in_=st.t[:, 2:3], func=AF.Sqrt, bias=epsc.t[:], scale=1.0), reads=[st, epsc], writes=[st])
            S.op("dve", lambda: nc.vector.reciprocal(st.t[:, 2:3], st.t[:, 2:3]), reads=[st], writes=[st])
            S.op("dve", lambda: nc.vector.tensor_scalar(vn.t[:], gl.t[:, 0:64], st.t[:, 0:1], st.t[:, 2:3], ALU.subtract, ALU.mult), reads=[gl, st], writes=[vn])
            S.op("pe", lambda: nc.tensor.matmul(pSG.t[:, 0:64], lhsT=wsT.t[:], rhs=vn.t[:], start=True, stop=True), reads=[wsT, vn], writes=[pSG])
            S.op("dve", lambda: nc.vector.scalar_tensor_tensor(out=oA.t[:, ti, 64:128], in0=pSG.t[:, 0:64], scalar=bs.t[:, 0:1], in1=gl.t[:, 512:576], op0=ALU.add, op1=ALU.mult),
                 reads=[pSG, bs, gl], writes=[oA])
        S.barrier(xts + hTs + [xg, t1, gl, junk, st, vn, pQK, pGU, pSG] + N.sq.bufs + N.ss.bufs + N.xn.bufs + N.pT.bufs)
    with contextlib.ExitStack() as esB:
        def sbB(name, shape, dt):
            return Buf(esB.enter_context(nc.sbuf_tensor("sbB_" + name, list(shape), dt)), name)
        def psB(name, shape, dt=F32):
            return Buf(esB.enter_context(nc.psum_tensor("psB_" + name, list(shape), dt)), name)
        pSs = [psB(f"pS{i}", [128, 1024]) for i in range(2)]
        pPTs = [psB(f"pPT{i}", [128, 7, 128], BF16) for i in range(2)]
        pOs = [psB(f"pO{i}", [128, 64]) for i in range(2)]
        Ts = [sbB(f"T{i}", [128, 832], F32) for i in range(2)]
        Ps = [sbB(f"P{i}", [128, 832], BF16) for i in range(2)]
        PTs = [sbB(f"PT{i}", [128, 7, 128], BF16) for i in range(2)]
        mxs = [sbB(f"mx{i}", [128, 2], F32) for i in range(2)]
        step = 0
        for b in range(n_b):
            base = b * TPB
            ctx_tiles = [base + n_lat + j for j in range(n_ctx)]
            for tl in range(TPB):
                ti = base + tl
                pS, pPT, pO, T, P, PT, mx = [x[step % 2] for x in (pSs, pPTs, pOs, Ts, Ps, PTs, mxs)]
                step += 1
                if tl < n_lat:
                    tw = min(max(tl - 2, 0), n_lat - 4)
                    full = [base + tw + k for k in range(4)]
                    extra = (base + tw + 4) if (2 <= tl <= n_lat - 3) else None
                    cls = 0 if tl == 0 else 1 if tl == 1 else 3 if tl == n_lat - 2 else 4 if tl == n_lat - 1 else 2
                    nnb = 512 + (64 if extra is not None else 0)
                else:
                    full, extra, cls, nnb = [], None, None, 0
                ncx = n_ctx * 128
                W = nnb + ncx
                qs = qT.t[:, ti * 128:(ti + 1) * 128]
                if full:
                    S.op("pe", lambda: nc.tensor.matmul(pS.t[:, 0:512], lhsT=qs, rhs=kT.t[:, full[0] * 128:(full[0] + 4) * 128], start=True, stop=True), reads=[qT, kT], writes=[pS])
                    if extra is not None:
                        S.op("pe", lambda: nc.tensor.matmul(pS.t[:, 512:576], lhsT=qs, rhs=kT.t[:, extra * 128:extra * 128 + 64], start=True, stop=True), reads=[qT, kT], writes=[pS])
                c0 = 512 + (64 if extra is not None else 0) if full else 0
                S.op("pe", lambda: nc.tensor.matmul(pS.t[:, c0:c0 + ncx], lhsT=qs, rhs=kT.t[:, ctx_tiles[0] * 128:ctx_tiles[0] * 128 + ncx], start=True, stop=True), reads=[qT, kT], writes=[pS])
                if full:
                    S.op("dve", lambda: nc.vector.tensor_tensor(out=T.t[:, 0:nnb], in0=pS.t[:, 0:nnb], in1=biasT.t[:, cls, 0:nnb], op=ALU.add), reads=[pS, biasT], writes=[T])
                S.op("act", lambda: nc.scalar.copy(T.t[:, nnb:W], pS.t[:, c0:c0 + ncx]), reads=[pS], writes=[T])
                S.op("dve", lambda: nc.vector.tensor_reduce(out=mx.t[:, 0:1], in_=T.t[:, 0:W], axis=AX.X, op=ALU.max), reads=[T], writes=[mx])
                S.op("dve", lambda: nc.vector.tensor_scalar(mx.t[:, 0:1], mx.t[:, 0:1], -1.0, None, ALU.mult), reads=[mx], writes=[mx])
                S.op("act", lambda: nc.scalar.activation(out=P.t[:, 0:W], in_=T.t[:, 0:W], func=AF.Exp, bias=mx.t[:, 0:1], scale=1.0, accum_out=mx.t[:, 1:2]), reads=[T, mx], writes=[P, mx])
                S.op("dve", lambda: nc.vector.reciprocal(mx.t[:, 1:2], mx.t[:, 1:2]), reads=[mx], writes=[mx])
                chunks = [(k * 128, 128, full[k]) for k in range(len(full))]
                if extra is not None:
                    chunks.append((512, 64, extra))
                chunks += [(nnb + j * 128, 128, ctx_tiles[j]) for j in range(n_ctx)]
                for ci, (pc0, n, vt) in enumerate(chunks):
                    S.op("pe", lambda: nc.tensor.transpose(pPT.t[:n, ci, :], P.t[:, pc0:pc0 + n], ident.t[:]), reads=[P, ident], writes=[pPT])
                ncf = len(chunks)
                S.op("dve", lambda: nc.vector.tensor_copy(PT.t[:, 0:ncf, :], pPT.t[:, 0:ncf, :]), reads=[pPT], writes=[PT])
                for ci, (pc0, n, vt) in enumerate(chunks):
                    S.op("pe", lambda: nc.tensor.matmul(pO.t[:, :], lhsT=PT.t[:n, ci, :], rhs=vA.t[:n, vt, :], start=(ci == 0), stop=(ci == ncf - 1)), reads=[PT, vA], writes=[pO])
                S.op("act", lambda: nc.scalar.activation(out=oA.t[:, ti, 0:64], in_=pO.t[:, :], func=AF.Copy, scale=mx.t[:, 1:2]), reads=[pO, mx], writes=[oA])
        S.barrier(pSs + pPTs + pOs + Ts + Ps + PTs + mxs)
    out_v = out_d.rearrange("(t p) n -> p t n", p=128)
    nchunk = 4
    per = (NT + nchunk - 1) // nchunk
    for k in range(nchunk):
        a, bnd = k * per, min(NT, (k + 1) * per)
        if a < bnd:
            S.dma("sp" if k % 2 == 0 else "act", out_v[:, a:bnd, :], oA.t[:, a:bnd, :], reads=[oA], writes=[], out=True)
    S.finish()
    return nc


def _fm(v):
    return np.ascontiguousarray(np.asarray(v, np.float32).reshape(-1, 128).T)


def na_bias_tables(rpb_h, n_lat):
    rows = 2 * n_lat
    out = np.full((5, 128, 576), -1e30, np.float32)
    reps = [0, 1, 2, n_lat - 2, n_lat - 1]
    qi = np.arange(128)
    kj = np.arange(576)
    for cls, tl in enumerate(reps):
        tw = min(max(tl - 2, 0), n_lat - 4)
        r = 2 * tl + qi // 64
        qc = qi % 64
        r0 = np.clip(r - 4, 0, rows - 8)
        c0 = np.clip(qc - 8, 0, 64 - 16)
        kr = 2 * tw + kj // 64
        kc = kj % 64
        valid = ((kr[None, :] >= r0[:, None]) & (kr[None, :] < r0[:, None] + 8)
                 & (kc[None, :] >= c0[:, None]) & (kc[None, :] < c0[:, None] + 16))
        dr_ = np.clip(kr[None, :] - r[:, None] + 7, 0, 14)
        dc_ = np.clip(kc[None, :] - qc[:, None], -15, 15) + 15
        out[cls] = np.where(valid, rpb_h[dr_, dc_], np.float32(-1e30))
    return out


def prep_ME(h, x, xc, w_in, rpb, w_s, b_s, n0, sc1, sh1):
    n_b = x.shape[0]
    n_lat = x.shape[1] // 128
    x_all = np.concatenate([np.concatenate([x[b], xc[b]], 0) for b in range(n_b)], 0)
    hs = slice(h * 64, (h + 1) * 64)
    g_cols = w_in[:, 2048:2560]
    order = list(range(h * 64, (h + 1) * 64)) + [c for c in range(512) if not (h * 64 <= c < (h + 1) * 64)]
    ins = {
        "x": np.ascontiguousarray(x_all, np.float32),
        "wqk": np.ascontiguousarray(np.concatenate([w_in[:, 0:512][:, hs], w_in[:, 512:1024][:, hs]], 1)),
        "wgu": np.ascontiguousarray(np.concatenate([g_cols[:, order], w_in[:, 1536:2048][:, hs]], 1)),
        "wv": np.ascontiguousarray(w_in[:, 1024:1536][:, hs]),
        "wsT": np.ascontiguousarray(w_s[h].T),
        "bs": np.ascontiguousarray(b_s[h][:, None]),
        "bias": na_bias_tables(rpb[h], n_lat),
        "cols": np.ascontiguousarray(np.stack([_fm(n0)] + [_fm(v) for v in sc1] + [_fm(v) for v in sh1], 1)),
        "ident": np.eye(128, dtype=np.float32),
    }
    return ins, None


def build_MO(n_lat=64, n_ctx=2, n_b=2, dbg=None, lvl=99):
    nc = bass.Bass("TRN2", target_bir_lowering=False)
    TPB = n_lat + n_ctx
    NT = n_b * TPB
    NR = n_b + 1
    dr = lambda name, shape, dt=F32, kind="ExternalInput": nc.dram_tensor(name, list(shape), dt, kind=kind).ap()
    x_d = dr("x", [NT * 128, D])
    w_d = dr("w", [D, 768])
    cols_d = dr("cols", [128, 1 + 2 * NR, 8])
    rope_d = dr("rope", [n_lat, 128, 256])
    lg_d = dr("lg", [128, 2])
    cE_d = dr("cE", [2, 128, 128])
    cM_d = dr("cM", [2, 128, 128])
    cQ_d = dr("cQ", [2, 128, 128])
    cK_d = dr("cK", [128, 2])
    gn_d = dr("gn", [128, 256])
    ident_d = dr("ident", [128, 128])
    out_d = dr("out", [NT * 128, 256], BF16, kind="ExternalOutput")
    gS_d = dr("gS", [NT * 128, 256], kind="Internal")
    oF_d = dr("oF", [NT * 128, 256], kind="Internal")
    S = Sched(nc)
    ident_f = S.sb("ident_f", [128, 128], F32)
    ident = S.sb("ident_b", [128, 128], BF16)
    S.dma("sp", ident_f.t[:], ident_d, writes=[ident_f])
    S.op("dve", lambda: nc.vector.tensor_copy(ident.t[:], ident_f.t[:]), reads=[ident_f], writes=[ident])
    epsc = S.sb("epsc", [128, 1], F32)
    S.op("dve", lambda: nc.vector.memset(epsc.t[:], EPS), writes=[epsc])
    cols = S.sb("cols", [128, 1 + 2 * NR, 8], F32)
    S.dma("sp", cols.t[:], cols_d, writes=[cols])
    gain = S.sb("gain", [128, NR, 8], F32)
    shift = S.sb("shift", [128, NR, 8], F32)
    for r in range(NR):
        S.op("dve", lambda: nc.vector.scalar_tensor_tensor(out=gain.t[:, r, :], in0=cols.t[:, 1 + r, :], scalar=1.0, in1=cols.t[:, 0, :], op0=ALU.add, op1=ALU.mult),
             reads=[cols], writes=[gain])
        S.op("dve", lambda: nc.vector.tensor_copy(shift.t[:, r, :], cols.t[:, 1 + NR + r, :]), reads=[cols], writes=[shift])
    wb = S.sb("wb", [128, 8, 768], BF16)
    with S.scope():
        stage = S.sb("stage", [128, 8, 768], F32)
        S.dma("sp", stage.t[:], w_d.rearrange("(c p) n -> p c n", p=128), writes=[stage])
        S.op("dve", lambda: nc.vector.tensor_copy(wb.t[:], stage.t[:]), reads=[stage], writes=[wb])
        S.barrier([stage])
    lg = S.sb("lg", [128, 2], F32)
    S.dma("sp", lg.t[:], lg_d, writes=[lg])
    DT = S.sb("DT", [128, 2, 128], F32)
    DQ = S.sb("DQ", [128, 2, 128], F32)
    DK = S.sb("DK", [128, 2], F32)
    GC = S.sb("GC", [128, 2], F32)
    gn = S.sb("gn", [128, 256], F32)
    S.dma("sp", gn.t[:], gn_d, writes=[gn])
    with S.scope():
        cE = S.sb("cE", [128, 2, 128], F32)
        cM = S.sb("cM", [128, 2, 128], F32)
        cQ = S.sb("cQ", [128, 2, 128], F32)
        cK = S.sb("cK", [128, 2], F32)
        c128 = S.sb("c128", [128, 1], F32)
        S.op("dve", lambda: nc.vector.memset(c128.t[:], 128.0), writes=[c128])
        S.dma("sp", cK.t[:], cK_d, writes=[cK])
        for d in range(2):
            S.dma("sp", cE.t[:, d, :], cE_d[d], writes=[cE])
            S.dma("act", cM.t[:, d, :], cM_d[d], writes=[cM])
            S.dma("sp", cQ.t[:, d, :], cQ_d[d], writes=[cQ])
        for d in range(2):
            S.op("act", lambda: nc.scalar.activation(out=DT.t[:, d, :], in_=cE.t[:, d, :], func=AF.Exp, scale=lg.t[:, d:d + 1]), reads=[cE, lg], writes=[DT])
            S.op("dve", lambda: nc.vector.tensor_tensor(out=DT.t[:, d, :], in0=DT.t[:, d, :], in1=cM.t[:, d, :], op=ALU.mult), reads=[DT, cM], writes=[DT])
            S.op("act", lambda: nc.scalar.activation(out=DQ.t[:, d, :], in_=cQ.t[:, d, :], func=AF.Exp, scale=lg.t[:, d:d + 1]), reads=[cQ, lg], writes=[DQ])
            S.op("act", lambda: nc.scalar.activation(out=DK.t[:, d:d + 1], in_=cK.t[:, d:d + 1], func=AF.Exp, scale=lg.t[:, d:d + 1]), reads=[cK, lg], writes=[DK])
            S.op("act", lambda: nc.scalar.activation(out=GC.t[:, d:d + 1], in_=c128.t[:], func=AF.Exp, scale=lg.t[:, d:d + 1]), reads=[c128, lg], writes=[GC])
        S.barrier([cE, cM, cQ, cK, c128])
    if dbg == "P":
        S.barrier([DT, DQ, DK, GC, gn, wb])
        S.finish()
        return nc
    gS_tok = S.shared_toks("gS", NT, 4)
    oF_tok = S.shared_toks("oF", NT, 4)
    for b in range(n_b):
        base = b * TPB
        with S.scope():
            qT = S.sb(f"qT{b}", [128, TPB * 128], BF16)
            kT = S.sb(f"kT{b}", [128, TPB * 128], BF16)
            kK = S.sb(f"kK{b}", [128, TPB, 128], BF16)
            vA = S.sb(f"vA{b}", [128, TPB, 256], BF16)
            with S.scope():
                N = NormCtx(S, ident, epsc)
                xts = [S.sb(f"xt{b}_{i}", [128, D], F32) for i in range(2)]
                hTs = [S.sb(f"hT{b}_{i}", [128, 8, 128], BF16) for i in range(2)]
                pIn = S.ps(f"pIn{b}", [128, 1024])
                pT2 = S.ps(f"pT2{b}", [128, 2, 128], BF16)
                qk = S.sb(f"qk{b}", [128, 256], F32)
                rp = [S.sb(f"rp{b}_{i}", [128, 256], F32) for i in range(2)]
                ta = S.sb(f"ta{b}", [128, 2, 64], F32)
                tb = S.sb(f"tb{b}", [128, 2, 64], F32)
                rq = S.sb(f"rq{b}", [128, 256], BF16)
                gs = [S.sb(f"gs{b}_{i}", [128, 256], F32) for i in range(2)]
                for tl in range(TPB):
                    ti = base + tl
                    is_lat = tl < n_lat
                    r = b if is_lat else n_b
                    xt = xts[tl % 2]
                    hT = hTs[tl % 2]
                    S.dma("sp" if tl % 2 == 0 else "act", xt.t[:], x_d[ti * 128:(ti + 1) * 128, :], writes=[xt])
                    if lvl < 1:
                        continue
                    emit_norm_T(S, nc, N, xt, 128, gain, shift, r, hT, 0)
                    if lvl < 2:
                        continue
                    for (o0, n) in ((0, 512), (512, 256)):
                        for c in range(8):
                            S.op("pe", lambda: nc.tensor.matmul(pIn.t[:, o0:o0 + n], lhsT=hT.t[:, c, :], rhs=wb.t[:, c, o0:o0 + n], start=(c == 0), stop=(c == 7)), reads=[wb, hT], writes=[pIn])
                    S.op("act", lambda: nc.scalar.copy(vA.t[:, tl, :], pIn.t[:, 256:512]), reads=[pIn], writes=[vA])
                    if lvl < 3:
                        continue
                    g_ = gs[tl % 2]
                    S.op("act", lambda: nc.scalar.activation(out=g_.t[:], in_=pIn.t[:, 512:768], func=AF.Silu), reads=[pIn], writes=[g_])
                    if dbg != "A2":
                        S.dma("act", gS_d[ti * 128:(ti + 1) * 128, :], g_.t[:], reads=[g_], writes=[gS_tok[ti]])
                    if is_lat and dbg != "A1":
                        S.op("act", lambda: nc.scalar.copy(qk.t[:, 0:128], pIn.t[:, 0:128]), reads=[pIn], writes=[qk])
                        S.op("act", lambda: nc.scalar.activation(out=qk.t[:, 128:256], in_=pIn.t[:, 128:256], func=AF.Copy, scale=128.0 ** -0.5), reads=[pIn], writes=[qk])
                        rpt = rp[tl % 2]
                        S.dma("sp", rpt.t[:], rope_d[tl], writes=[rpt])
                        q4 = qk.t[:].rearrange("p (a h c) -> p a h c", a=2, h=2)
                        o4 = rq.t[:].rearrange("p (a h c) -> p a h c", a=2, h=2)
                        cs = rpt.t[:, 0:128].rearrange("p (a c) -> p a c", a=2)
                        sn = rpt.t[:, 128:256].rearrange("p (a c) -> p a c", a=2)
                        S.op("dve", lambda: nc.vector.tensor_tensor(out=ta.t[:], in0=q4[:, :, 0, :], in1=cs, op=ALU.mult), reads=[qk, rpt], writes=[ta])
                        S.op("pool", lambda: nc.gpsimd.tensor_tensor(out=tb.t[:], in0=q4[:, :, 1, :], in1=sn, op=ALU.mult), reads=[qk, rpt], writes=[tb])
                        S.op("dve", lambda: nc.vector.tensor_tensor(out=o4[:, :, 0, :], in0=ta.t[:], in1=tb.t[:], op=ALU.subtract), reads=[ta, tb], writes=[rq])
                        S.op("dve", lambda: nc.vector.tensor_tensor(out=ta.t[:], in0=q4[:, :, 0, :], in1=sn, op=ALU.mult), reads=[qk, rpt, rq], writes=[ta])
                        S.op("pool", lambda: nc.gpsimd.tensor_tensor(out=tb.t[:], in0=q4[:, :, 1, :], in1=cs, op=ALU.mult), reads=[qk, rpt, rq], writes=[tb])
                        S.op("dve", lambda: nc.vector.tensor_tensor(out=o4[:, :, 1, :], in0=ta.t[:], in1=tb.t[:], op=ALU.add), reads=[ta, tb], writes=[rq])
                    else:
                        S.op("act", lambda: nc.scalar.copy(rq.t[:, 0:128], pIn.t[:, 0:128]), reads=[pIn], writes=[rq])
                        S.op("act", lambda: nc.scalar.activation(out=rq.t[:, 128:256], in_=pIn.t[:, 128:256], func=AF.Copy, scale=128.0 ** -0.5), reads=[pIn], writes=[rq])
                    if lvl < 4:
                        continue
                    S.op("pool", lambda: nc.gpsimd.tensor_copy(kK.t[:, tl, :], rq.t[:, 128:256]), reads=[rq], writes=[kK])
                    if lvl < 5:
                        continue
                    for w in range(2):
                        S.op("pe", lambda: nc.tensor.transpose(pT2.t[:, w, :], rq.t[:, w * 128:(w + 1) * 128], ident.t[:]), reads=[rq, ident], writes=[pT2])
                    if lvl < 6:
                        continue
                    S.op("dve", lambda: nc.vector.tensor_copy(qT.t[:, tl * 128:(tl + 1) * 128], pT2.t[:, 0, :]), reads=[pT2], writes=[qT])
                    if lvl < 7:
                        continue
                    S.op("dve", lambda: nc.vector.tensor_copy(kT.t[:, tl * 128:(tl + 1) * 128], pT2.t[:, 1, :]), reads=[pT2], writes=[kT])
                S.barrier(xts + hTs + [pIn, pT2, qk, ta, tb, rq] + rp + gs + N.sq.bufs + N.ss.bufs + N.xn.bufs + N.pT.bufs)
            if dbg in ("A", "A1", "A2"):
                S.barrier(gS_tok + [qT, kT, kK, vA])
                continue
            with S.scope():
                Sf = [S.sb(f"Sf{b}_{d}", [128, 256], F32) for d in range(2)]
                Sb = [S.sb(f"Sb{b}_{d}", [128, 256], BF16) for d in range(2)]
                for d in range(2):
                    S.op("dve", lambda: nc.vector.memset(Sf[d].t[:], 0.0), writes=[Sf[d]])
                    S.op("dve", lambda: nc.vector.memset(Sb[d].t[:], 0.0), writes=[Sb[d]])
                pST = [S.ps(f"pST{b}_{i}", [128, 128]) for i in range(2)]
                pOo = [S.ps(f"pOo{b}_{i}", [128, 256]) for i in range(2)]
                pDS = [S.ps(f"pDS{b}_{i}", [128, 256]) for i in range(2)]
                sTm = [S.sb(f"sTm{b}_{i}", [128, 128], BF16) for i in range(2)]
                qd = [S.sb(f"qd{b}_{i}", [128, 128], BF16) for i in range(2)]
                kd = [S.sb(f"kd{b}_{i}", [128, 128], BF16) for i in range(2)]
                of_ = [S.sb(f"of{b}_{i}", [128, 256], F32) for i in range(2)]
                ofl = [S.sb(f"ofl{b}_{i}", [128, 256], F32) for i in range(2)]
                gl_ = [S.sb(f"gl{b}_{i}", [128, 256], F32) for i in range(2)]
                junk = S.sb(f"junk{b}", [128, 256], F32)
                st = [S.sb(f"st{b}_{i}", [128, 4], F32) for i in range(2)]
                ot = [S.sb(f"ot{b}_{i}", [128, 256], BF16) for i in range(2)]
                order = [list(range(n_lat, TPB)) + list(range(n_lat)),
                         list(range(TPB - 1, n_lat - 1, -1)) + list(range(n_lat - 1, -1, -1))]
                for i in range(TPB):
                    for d in range(2):
                        tl = order[d][i]
                        ti = base + tl
                        k2 = d
                        cs_ = slice(tl * 128, (tl + 1) * 128)
                        S.op("pe", lambda: nc.tensor.matmul(pST[d].t[:], lhsT=kT.t[:, cs_], rhs=qT.t[:, cs_], start=True, stop=True), reads=[kT, qT], writes=[pST[d]])
                        S.op("dve", lambda: nc.vector.tensor_tensor(out=sTm[d].t[:], in0=pST[d].t[:], in1=DT.t[:, d, :], op=ALU.mult), reads=[pST[d], DT], writes=[sTm[d]])
                        S.op("pool", lambda: nc.gpsimd.tensor_tensor(out=qd[d].t[:], in0=qT.t[:, cs_], in1=DQ.t[:, d, :], op=ALU.mult), reads=[qT, DQ], writes=[qd[d]])
                        S.op("pool", lambda: nc.gpsimd.tensor_scalar(kd[d].t[:], kK.t[:, tl, :], DK.t[:, d:d + 1], None, ALU.mult), reads=[kK, DK], writes=[kd[d]])
                        S.op("pe", lambda: nc.tensor.matmul(pOo[d].t[:], lhsT=sTm[d].t[:], rhs=vA.t[:, tl, :], start=True, stop=False), reads=[sTm[d], vA], writes=[pOo[d]])
                        S.op("pe", lambda: nc.tensor.matmul(pOo[d].t[:], lhsT=qd[d].t[:], rhs=Sb[d].t[:], start=False, stop=True), reads=[qd[d], Sb[d]], writes=[pOo[d]])
                        S.op("pe", lambda: nc.tensor.matmul(pDS[d].t[:], lhsT=kd[d].t[:], rhs=vA.t[:, tl, :], start=True, stop=True), reads=[kd[d], vA], writes=[pDS[d]])
                        S.op("dve", lambda: nc.vector.scalar_tensor_tensor(out=Sf[d].t[:], in0=Sf[d].t[:], scalar=GC.t[:, d:d + 1], in1=pDS[d].t[:], op0=ALU.mult, op1=ALU.add),
                             reads=[Sf[d], GC, pDS[d]], writes=[Sf[d]])
                        S.op("act", lambda: nc.scalar.copy(Sb[d].t[:], Sf[d].t[:]), reads=[Sf[d]], writes=[Sb[d]])
                        i_other = order[1 - d].index(tl)
                        if i < i_other or (i == i_other and d == 0):
                            o_ = of_[k2]
                            S.op("act", lambda: nc.scalar.copy(o_.t[:], pOo[d].t[:]), reads=[pOo[d]], writes=[o_])
                            S.dma("sp", oF_d[ti * 128:(ti + 1) * 128, :], o_.t[:], reads=[o_], writes=[oF_tok[ti]])
                        else:
                            o_ = ofl[k2]
                            g_ = gl_[k2]
                            s_ = st[k2]
                            S.dma("sp", o_.t[:], oF_d[ti * 128:(ti + 1) * 128, :], reads=[oF_tok[ti]], writes=[o_])
                            S.dma("act", g_.t[:], gS_d[ti * 128:(ti + 1) * 128, :], reads=[gS_tok[ti]], writes=[g_])
                            S.op("dve", lambda: nc.vector.tensor_tensor(out=o_.t[:], in0=pOo[d].t[:], in1=o_.t[:], op=ALU.add), reads=[pOo[d], o_], writes=[o_])
                            S.op("act", lambda: nc.scalar.activation(out=junk.t[:], in_=o_.t[:], func=AF.Identity, accum_out=s_.t[:, 0:1]), reads=[o_], writes=[junk, s_])
                            S.op("act", lambda: nc.scalar.activation(out=junk.t[:], in_=o_.t[:], func=AF.Square, accum_out=s_.t[:, 1:2]), reads=[o_], writes=[junk, s_])
                            S.op("dve", lambda: nc.vector.tensor_scalar(s_.t[:, 0:2], s_.t[:, 0:2], 1.0 / 256, None, ALU.mult), reads=[s_], writes=[s_])
                            S.op("dve", lambda: nc.vector.tensor_tensor(out=s_.t[:, 2:3], in0=s_.t[:, 0:1], in1=s_.t[:, 0:1], op=ALU.mult), reads=[s_], writes=[s_])
                            S.op("dve", lambda: nc.vector.tensor_tensor(out=s_.t[:, 2:3], in0=s_.t[:, 1:2], in1=s_.t[:, 2:3], op=ALU.subtract), reads=[s_], writes=[s_])
                            S.op("act", lambda: nc.scalar.activation(out=s_.t[:, 2:3], in_=s_.t[:, 2:3], func=AF.Sqrt, bias=epsc.t[:], scale=1.0), reads=[s_, epsc], writes=[s_])
                            S.op("dve", lambda: nc.vector.reciprocal(s_.t[:, 2:3], s_.t[:, 2:3]), reads=[s_], writes=[s_])
                            S.op("dve", lambda: nc.vector.tensor_scalar(o_.t[:], o_.t[:], s_.t[:, 0:1], s_.t[:, 2:3], ALU.subtract, ALU.mult), reads=[o_, s_], writes=[o_])
                            S.op("pool", lambda: nc.gpsimd.tensor_tensor(out=g_.t[:], in0=g_.t[:], in1=gn.t[:], op=ALU.mult), reads=[g_, gn], writes=[g_])
                            ob = ot[k2]
                            S.op("pool", lambda: nc.gpsimd.tensor_tensor(out=ob.t[:], in0=o_.t[:], in1=g_.t[:], op=ALU.mult), reads=[o_, g_], writes=[ob])
                            S.dma("act", out_d[ti * 128:(ti + 1) * 128, :], ob.t[:], reads=[ob], writes=[], out=True)
                S.barrier(Sf + Sb + pST + pOo + pDS + sTm + qd + kd + of_ + ofl + gl_ + [junk] + st + ot)
            S.barrier([qT, kT, kK, vA])
    S.finish()
    return nc


def prep_MO(h, x, xc, w_in, log_decay, gn_g, n0, sc1, sh1):
    n_b = x.shape[0]
    n_lat = x.shape[1] // 128
    x_all = np.concatenate([np.concatenate([x[b], xc[b]], 0) for b in range(n_b)], 0)
    w = np.concatenate([w_in[:, h * 128:(h + 1) * 128], w_in[:, 1024 + h * 128:1024 + (h + 1) * 128],
                        w_in[:, 2048 + h * 256:2048 + (h + 1) * 256], w_in[:, 4096 + h * 256:4096 + (h + 1) * 256]], 1)
    t = np.arange(n_lat * 128)
    row = (t // 64).astype(np.float32)
    col = (t % 64).astype(np.float32)
    inv = (10000.0 ** (-np.arange(32, dtype=np.float32) / 32)).astype(np.float32)
    ang = np.concatenate([row[:, None] * inv, col[:, None] * inv], -1).astype(np.float32)
    cos, sin = np.cos(ang).astype(np.float32), np.sin(ang).astype(np.float32)
    rope = np.concatenate([cos, cos, sin, sin], -1).reshape(n_lat, 128, 256)
    pos = np.arange(128, dtype=np.float32)
    kq = pos[None, :] - pos[:, None]
    cE = np.stack([np.where(kq >= 0, kq, 0), np.where(kq <= 0, -kq, 0)]).astype(np.float32)
    cM = np.stack([(kq >= 0), (kq <= 0)]).astype(np.float32)
    cQ = np.stack([np.broadcast_to(pos + 1, (128, 128)), np.broadcast_to(128 - pos, (128, 128))]).astype(np.float32)
    cK = np.stack([127 - pos, pos], 1).astype(np.float32)
    ins = {
        "x": np.ascontiguousarray(x_all, np.float32), "w": np.ascontiguousarray(w),
        "cols": np.ascontiguousarray(np.stack([_fm(n0)] + [_fm(v) for v in sc1] + [_fm(v) for v in sh1], 1)),
        "rope": np.ascontiguousarray(rope), "lg": np.ascontiguousarray(np.broadcast_to(log_decay[:, h].astype(np.float32), (128, 2))),
        "cE": np.ascontiguousarray(cE), "cM": np.ascontiguousarray(cM), "cQ": np.ascontiguousarray(cQ), "cK": np.ascontiguousarray(cK),
        "gn": np.ascontiguousarray(np.broadcast_to(gn_g[h * 256:(h + 1) * 256].astype(np.float32), (128, 256))),
        "ident": np.eye(128, dtype=np.float32),
    }
    return ins, None


def build_ADA():
    nc = bass.Bass("TRN2", target_bir_lowering=False)
    dr = lambda name, shape, dt=F32, kind="ExternalInput": nc.dram_tensor(name, list(shape), dt, kind=kind).ap()
    c_d = dr("cT", [128, 8, 3])
    w_d = dr("w", [4, D, 768])
    b_d = dr("b", [128, 4, 6])
    o_d = dr("out", [128, 4, 6, 3], kind="ExternalOutput")
    S = Sched(nc)
    cT = S.sb("cT", [128, 8, 3], F32)
    sc = S.sb("sc", [128, 8, 3], BF16)
    bb = S.sb("bb", [128, 4, 6], F32)
    ot = S.sb("ot", [128, 4, 6, 3], F32)
    S.dma("sp", cT.t[:], c_d, writes=[cT])
    S.dma("sp", bb.t[:], b_d, writes=[bb])
    S.op("act", lambda: nc.scalar.activation(out=sc.t[:], in_=cT.t[:], func=AF.Silu), reads=[cT], writes=[sc])
    stg = [S.sb(f"stg{i}", [128, 8, 768], F32) for i in range(2)]
    wl = [S.sb(f"wl{i}", [128, 8, 768], BF16) for i in range(2)]
    ps = [S.ps(f"ps{i}", [128, 4]) for i in range(2)]
    k = 0
    for l in range(4):
        st, w = stg[l % 2], wl[l % 2]
        S.dma("sp" if l % 2 == 0 else "act", st.t[:], w_d[l].rearrange("(c p) n -> p c n", p=128), writes=[st])
        S.op("dve" if l % 2 == 0 else "pool", lambda: (nc.vector if l % 2 == 0 else nc.gpsimd).tensor_copy(w.t[:], st.t[:]), reads=[st], writes=[w])
        for j in range(6):
            p = ps[k % 2]
            k += 1
            for c in range(8):
                S.op("pe", lambda: nc.tensor.matmul(p.t[:, 0:3], lhsT=w.t[:, c, j * 128:(j + 1) * 128], rhs=sc.t[:, c, :], start=(c == 0), stop=(c == 7)), reads=[w, sc], writes=[p])
            S.op("dve", lambda: nc.vector.tensor_scalar(ot.t[:, l, j, :], p.t[:, 0:3], bb.t[:, l, j:j + 1], None, ALU.add), reads=[p, bb], writes=[ot])
    S.dma("sp", o_d, ot.t[:], reads=[ot], writes=[], out=True)
    S.finish()
    return nc


_PROGS = {}


def _prog(key, fn):
    if key not in _PROGS:
        _PROGS[key] = fn()
    return _PROGS[key]


def _run(nc, in_maps):
    res = run_bass_kernel_spmd(nc, in_maps, core_ids=list(range(len(in_maps))))
    return res.results


def kernel(x, c, ctx, c_ctx, ada_w, ada_b, norm_g, hyb_w_in, na_rpb, sgu_w, sgu_b, hyb_w_out,
           ret_w_in, ret_log_decay, ret_gn_g, ret_w_out, ffn_w_in, ffn_conv_w, ffn_conv_b, ffn_w_out):
    import ml_dtypes
    f32 = lambda a: np.ascontiguousarray(np.asarray(a, dtype=np.float32))
    x, c, ctx, c_ctx, ada_w, ada_b, norm_g = map(f32, (x, c, ctx, c_ctx, ada_w, ada_b, norm_g))
    hyb_w_in, na_rpb, sgu_w, sgu_b, hyb_w_out = map(f32, (hyb_w_in, na_rpb, sgu_w, sgu_b, hyb_w_out))
    ret_w_in, ret_log_decay, ret_gn_g, ret_w_out = map(f32, (ret_w_in, ret_log_decay, ret_gn_g, ret_w_out))
    ffn_w_in, ffn_conv_w, ffn_conv_b, ffn_w_out = map(f32, (ffn_w_in, ffn_conv_w, ffn_conv_b, ffn_w_out))
    B, T, _ = x.shape
    L = ctx.shape[1]
    depth = ada_w.shape[0]
    ident = np.eye(128, dtype=np.float32)
    cvec = np.stack([c[0], c[1], c_ctx], 0)
    cT = np.ascontiguousarray(cvec.T.reshape(8, 128, 3).transpose(1, 0, 2))
    ada_maps = []
    for j in range(8):
        cs = slice(j * 768, (j + 1) * 768)
        ada_maps.append({"cT": cT, "w": np.ascontiguousarray(ada_w[:, :, cs]),
                         "b": np.ascontiguousarray(ada_b[:, cs].reshape(4, 6, 128).transpose(2, 0, 1))})
    res = _run(_prog("ada", build_ADA), ada_maps)
    mod = np.zeros((depth, 3, 6 * D), np.float32)
    for j in range(8):
        o = res[j]["out"]
        mod[:, :, j * 768:(j + 1) * 768] = o.transpose(1, 3, 2, 0).reshape(4, 3, 768)
    xcur = x.copy()
    ccur = ctx.copy()
    n_lat = T // 128
    n_ctx = L // 128
    TPB = n_lat + n_ctx
    for i in range(depth):
        j2 = i // 2
        sh1, sc1, g1, sh2, sc2, g2 = [mod[i][:, k * D:(k + 1) * D] for k in range(6)]
        if i % 2 == 0:
            maps = [prep_ME(h, xcur, ccur, hyb_w_in[j2], na_rpb[j2], sgu_w[j2], sgu_b[j2], norm_g[i, 0], sc1, sh1)[0] for h in range(8)]
            res = _run(_prog("me", lambda: build_ME(n_lat, n_ctx, B)), maps)
            KO = D
            o_full = np.zeros((B * TPB * 128, KO), ml_dtypes.bfloat16)
            for h in range(8):
                o = res[h]["out"]
                o_full[:, h * 64:(h + 1) * 64] = o[:, 0:64]
                o_full[:, 512 + h * 64:512 + (h + 1) * 64] = o[:, 64:128]
            w_o = hyb_w_out[j2]
        else:
            maps = [prep_MO(h, xcur, ccur, ret_w_in[j2], ret_log_decay[j2], ret_gn_g[j2], norm_g[i, 0], sc1, sh1)[0] for h in range(8)]
            res = _run(_prog("mo", lambda: build_MO(n_lat, n_ctx, B)), maps)
            KO = 2 * D
            o_full = np.zeros((B * TPB * 128, KO), ml_dtypes.bfloat16)
            for h in range(8):
                o_full[:, h * 256:(h + 1) * 256] = res[h]["out"]
            w_o = ret_w_out[j2]
        del maps, res
        o_full = o_full.reshape(B, TPB * 128, KO)
        passes = [(8, n_ctx), (8, 0)]
        bc = lambda v: np.ascontiguousarray(np.broadcast_to(v, (128, D)))
        convw = np.ascontiguousarray(ffn_conv_w[i].reshape(3, 2 * NFC, 128).transpose(2, 1, 0))
        convb = np.ascontiguousarray(ffn_conv_b[i].reshape(2 * NFC, 128).T)
        maps = []
        QT = T // 4
        for core in range(8):
            b, q = divmod(core, 4)
            rows = np.stack([bc(norm_g[i, 1]), bc(norm_g[i, 3]), bc(g1[b]), bc(g1[2]), bc(g2[b]), bc(g2[2])])
            cols = np.ascontiguousarray(np.stack([_fm(norm_g[i, 2]), _fm(sc2[b]), _fm(sc2[2]), _fm(sh2[b]), _fm(sh2[2])], 1))
            m = {"w_o": w_o, "w_in": ffn_w_in[i], "w_out": ffn_w_out[i], "convw": convw, "convb": convb,
                 "rows": rows, "cols": cols, "ident": ident}
            hm = np.zeros((128, 4), np.float32)
            for p in range(2):
                m0 = q * QT + p * (QT // 2)
                idx_main = np.arange(m0, m0 + QT // 2)
                hl, hr = m0 - 1, m0 + QT // 2
                hm[:, 2 * p] = 1.0 if hl >= 0 else 0.0
                hm[:, 2 * p + 1] = 1.0 if hr < T else 0.0
                idx = np.concatenate([idx_main, [max(hl, 0)], [min(hr, T - 1)]])
                o_rows = o_full[b, idx]
                x_rows = xcur[b, idx]
                if p == 0:
                    o_rows = np.concatenate([o_rows, o_full[b, T:T + L]], 0)
                    x_rows = np.concatenate([x_rows, ccur[b]], 0)
                m[f"oT{p}"] = np.ascontiguousarray(o_rows.T)
                m[f"x{p}"] = np.ascontiguousarray(x_rows)
            m["hmask"] = hm
            maps.append(m)
        res = _run(_prog(("f", KO), lambda: build_F(KO, passes)), maps)
        xn = np.empty_like(xcur)
        cn = np.empty_like(ccur)
        for core in range(8):
            b, q = divmod(core, 4)
            o0, o1 = res[core]["xo0"], res[core]["xo1"]
            xn[b, q * QT:q * QT + QT // 2] = o0[:QT // 2]
            xn[b, q * QT + QT // 2:(q + 1) * QT] = o1
            if q == 0:
                cn[b] = o0[QT // 2:]
        xcur, ccur = xn, cn
        del maps, res
    return xcur
```

```python
import contextlib
import numpy as np
import concourse.bass as bass
import concourse.mybir as mybir
from concourse.bass_utils import run_bass_kernel_spmd

F32 = mybir.dt.float32
BF16 = mybir.dt.bfloat16
AF = mybir.ActivationFunctionType
ALU = mybir.AluOpType
AX = mybir.AxisListType


class Buf:
    def __init__(self, t=None, name=""):
        self.t = t
        self.name = name
        self.w = None
        self.r = []
        self.ld = None
        self.st = None


class SemCounter:
    def __init__(self, sem, key, shared=False):
        self.sem = sem
        self.key = key
        self.cnt = 0
        self.shared = shared


class Sched:
    def __init__(self, nc, strict_same=True):
        self.nc = nc
        self.es = contextlib.ExitStack()
        self.stack = [self.es]
        self.eng = {"pe": nc.tensor, "act": nc.scalar, "dve": nc.vector, "pool": nc.gpsimd, "sp": nc.sync}
        self.sem = {}
        self.cnt = {}
        for e in self.eng:
            self.sem[e] = self.es.enter_context(nc.semaphore("s_" + e))
            self.cnt[e] = 0
        self.waited = {e: {} for e in self.eng}
        self.ekey = {e: e for e in self.eng}
        self.gen = 0
        self.strict_same = strict_same
        self.out_events = []
        self.nsem = 0
        self.n_ops = 0
        self.uid = 0
        self.coll_inc = 1
        self.scope_bufs = [[]]
        self.free_counters = []

    def sb(self, name, shape, dtype):
        self.uid += 1
        b = Buf(self.stack[-1].enter_context(self.nc.sbuf_tensor(f"sb_{name}_{self.uid}", list(shape), dtype)), name)
        self.scope_bufs[-1].append(b)
        return b

    def ps(self, name, shape, dtype=F32):
        self.uid += 1
        b = Buf(self.stack[-1].enter_context(self.nc.psum_tensor(f"ps_{name}_{self.uid}", list(shape), dtype)), name)
        self.scope_bufs[-1].append(b)
        return b

    @contextlib.contextmanager
    def scope(self):
        es = contextlib.ExitStack()
        self.stack.append(es)
        self.scope_bufs.append([])
        try:
            yield es
        finally:
            bufs = self.scope_bufs.pop()
            self.barrier(bufs)
            for b in bufs:
                for sc_ in (b.ld, b.st):
                    if sc_ is not None and not sc_.shared:
                        self.free_counters.append(sc_)
            self.stack.pop()
            es.close()

    def tok(self, name=""):
        return Buf(None, name)

    def new_sem(self, name):
        self.nsem += 1
        return self.es.enter_context(self.nc.semaphore(f"{name}_{self.nsem}"))

    def new_counter(self, name, shared=False):
        if not shared and self.free_counters:
            return self.free_counters.pop()
        sem = self.new_sem(name)
        return SemCounter(sem, f"D{self.nsem}", shared)

    def shared_toks(self, name, n, k=8):
        cs = [self.new_counter(f"{name}{i}", shared=True) for i in range(k)]
        toks = []
        for i in range(n):
            t = Buf(None, f"{name}{i}")
            t.ld = cs[i % k]
            toks.append(t)
        return toks

    def _deps(self, reads, writes):
        ev = []
        for b in reads:
            if b.w is not None:
                ev.append(b.w)
        for b in writes:
            if b.w is not None:
                ev.append(b.w)
            ev.extend(b.r)
        return ev

    def _emit_waits(self, e, events):
        need = {}
        for (k, sem, v) in events:
            if k == self.ekey[e] and not self.strict_same:
                continue
            if self.waited[e].get(k, 0) >= v:
                continue
            if k not in need or need[k][1] < v:
                need[k] = (sem, v)
        for k, (sem, v) in need.items():
            self.eng[e].wait_ge(sem, v)
            self.waited[e][k] = v

    def op(self, e, fn, reads=(), writes=()):
        self._emit_waits(e, self._deps(reads, writes))
        ins = fn()
        self.cnt[e] += 1
        ins.then_inc(self.sem[e], 1)
        evt = (self.ekey[e], self.sem[e], self.cnt[e])
        for b in writes:
            b.w = evt
            b.r = []
        for b in reads:
            if b not in writes:
                b.r.append(evt)
        self.n_ops += 1
        return ins

    def dma(self, q, out_ap, in_ap, reads=(), writes=(), out=False, indep=False, **kw):
        if writes:
            b = writes[0]
            if b.ld is None:
                b.ld = self.new_counter("ld_" + b.name)
            sc = b.ld
        else:
            b = reads[0]
            if b.st is None:
                b.st = self.new_counter("st_" + b.name)
            sc = b.st
        ev = self._deps(reads, ())
        for wb_ in writes:
            ev.extend(wb_.r)
            if wb_.w is not None and not (indep and wb_.w[0] == sc.key):
                ev.append(wb_.w)
        if sc.shared and sc.cnt > 0:
            ev.append((sc.key, sc.sem, sc.cnt))
        self._emit_waits(q, ev)
        ins = self.eng[q].dma_start(out=out_ap, in_=in_ap, **kw)
        sc.cnt += 16
        ins.then_inc(sc.sem, 16)
        evt = (sc.key, sc.sem, sc.cnt)
        for b in writes:
            if indep:
                if b.w is not None and b.w[0] != sc.key:
                    b.r.append(b.w)
                b.w = evt
            else:
                b.w = evt
                b.r = []
        for b in reads:
            if b not in writes:
                b.r.append(evt)
        if out:
            self.out_events.append(evt)
        return ins

    def coll(self, kind, ins, outs, reads=(), writes=(), groups=None, op=None):
        b = writes[0]
        if b.ld is None:
            b.ld = self.new_counter("cc_" + b.name)
        sc = b.ld
        ev = self._deps(reads, writes)
        if sc.shared and sc.cnt > 0:
            ev.append((sc.key, sc.sem, sc.cnt))
        self._emit_waits("pool", ev)
        ins_ = self.nc.gpsimd.collective_compute(kind, op or ALU.bypass, replica_groups=groups or [list(range(8))], ins=list(ins), outs=list(outs))
        sc.cnt += self.coll_inc
        ins_.then_inc(sc.sem, self.coll_inc)
        evt = (sc.key, sc.sem, sc.cnt)
        for w in writes:
            w.w = evt
            w.r = []
        for r in reads:
            if r not in writes:
                r.r.append(evt)
        return ins_

    def barrier(self, bufs=()):
        ev = [(self.ekey[e], self.sem[e], self.cnt[e]) for e in self.eng if self.cnt[e] > 0]
        for b in bufs:
            if b.w is not None:
                ev.append(b.w)
            ev.extend(b.r)
        for e in self.eng:
            self._emit_waits(e, [x for x in ev if x[0] != self.ekey[e]])

    def renew(self):
        self.barrier()
        self.gen += 1
        for e in list(self.eng):
            if self.cnt[e] == 0:
                continue
            self.sem[e] = self.new_sem(f"s_{e}_g{self.gen}")
            self.cnt[e] = 0
            self.ekey[e] = f"{e}#{self.gen}"

    def finish(self):
        self._emit_waits("sp", self.out_events)
        self.es.close()


D = 1024
EPS = 1e-6
D_FF = 2816
NFC = 22


class Rot:
    def __init__(self, bufs):
        self.bufs = bufs
        self.i = 0

    def next(self):
        b = self.bufs[self.i % len(self.bufs)]
        self.i += 1
        return b


def rot_sb(S, name, shape, dtype, n):
    return Rot([S.sb(f"{name}{i}", shape, dtype) for i in range(n)])


def rot_ps(S, name, shape, dtype, n):
    return Rot([S.ps(f"{name}{i}", shape, dtype) for i in range(n)])


class NormCtx:
    def __init__(self, S, ident, epsc, nrot=2):
        self.S = S
        self.ident = ident
        self.epsc = epsc
        self.sq = rot_sb(S, "nsq", [128, D], F32, 1)
        self.ss = rot_sb(S, "nss", [128, 1], F32, nrot)
        self.xn = rot_sb(S, "nxn", [128, D], BF16, nrot)
        self.pT = rot_ps(S, "npT", [128, 8, 128], BF16, 2)


def emit_rstd(S, nc, src_ap, src_buf, P, ss, sq, epsc, width=D):
    S.op("act", lambda: nc.scalar.activation(out=sq.t[:P, :width], in_=src_ap, func=AF.Square, accum_out=ss.t[:P, :]),
         reads=[src_buf], writes=[sq, ss])
    S.op("act", lambda: nc.scalar.activation(out=ss.t[:P, :], in_=ss.t[:P, :], func=AF.Sqrt, bias=epsc.t[:P, :], scale=1.0 / width),
         reads=[ss, epsc], writes=[ss])
    S.op("dve", lambda: nc.vector.reciprocal(ss.t[:P, :], ss.t[:P, :]), reads=[ss], writes=[ss])


def emit_norm_T(S, nc, N, xbuf, P, gain, shift, gi, hT, col0):
    ss = N.ss.next()
    sq = N.sq.next()
    xn = N.xn.next()
    pT = N.pT.next()
    emit_rstd(S, nc, xbuf.t[:P, :], xbuf, P, ss, sq, N.epsc)
    S.op("act", lambda: nc.scalar.activation(out=xn.t[:P, :], in_=xbuf.t[:P, :], func=AF.Copy, scale=ss.t[:P, :]),
         reads=[xbuf, ss], writes=[xn])
    for c in range(8):
        S.op("pe", lambda: nc.tensor.transpose(pT.t[:, c, :P], xn.t[:P, c * 128:(c + 1) * 128], N.ident.t[:P, :P]),
             reads=[xn, N.ident], writes=[pT])
    for c in range(8):
        e = "dve" if c % 2 == 0 else "pool_no"
        S.op("dve", lambda: nc.vector.tensor_scalar(hT.t[:, c, col0:col0 + P], pT.t[:, c, :P], gain.t[:, gi, c:c + 1], shift.t[:, gi, c:c + 1], ALU.mult, ALU.add),
             reads=[pT, gain, shift], writes=[hT])


def load_cast(S, nc, q, dst_ap, dst_buf, src_ap, stage, caste="pool"):
    S.dma(q, stage.t[:], src_ap, writes=[stage])
    eng = {"pool": nc.gpsimd, "dve": nc.vector, "act": nc.scalar}[caste]
    if caste == "act":
        S.op("act", lambda: nc.scalar.copy(dst_ap, stage.t[:]), reads=[stage], writes=[dst_buf])
    else:
        S.op(caste, lambda: eng.tensor_copy(dst_ap, stage.t[:]), reads=[stage], writes=[dst_buf])


def f_ntok(n_main, n_ctx):
    return n_main * 128 + 2 + n_ctx * 128


def build_F(KO, passes):
    nc = bass.Bass("TRN2", target_bir_lowering=False)
    KC = KO // 128
    dr = lambda name, shape, dt=F32, kind="ExternalInput": nc.dram_tensor(name, list(shape), dt, kind=kind).ap()
    w_o_d = dr("w_o", [KO, D])
    w_in_d = dr("w_in", [D, 2 * D_FF])
    w_out_d = dr("w_out", [D_FF, D])
    convw_d = dr("convw", [128, 2 * NFC, 3])
    convb_d = dr("convb", [128, 2 * NFC])
    rows_d = dr("rows", [6, 128, D])
    cols_d = dr("cols", [128, 5, 8])
    hmask_d = dr("hmask", [128, 2 * len(passes)])
    ident_d = dr("ident", [128, 128])
    S = Sched(nc)
    ident_f = S.sb("ident_f", [128, 128], F32)
    ident = S.sb("ident_b", [128, 128], BF16)
    S.dma("sp", ident_f.t[:], ident_d, writes=[ident_f])
    S.op("dve", lambda: nc.vector.tensor_copy(ident.t[:], ident_f.t[:]), reads=[ident_f], writes=[ident])
    epsc = S.sb("epsc", [128, 1], F32)
    S.op("dve", lambda: nc.vector.memset(epsc.t[:], EPS), writes=[epsc])
    cols = S.sb("cols", [128, 5, 8], F32)
    S.dma("sp", cols.t[:], cols_d, writes=[cols])
    gain = S.sb("gain", [128, 2, 8], F32)
    shift = S.sb("shift", [128, 2, 8], F32)
    for r in range(2):
        S.op("dve", lambda: nc.vector.scalar_tensor_tensor(out=gain.t[:, r, :], in0=cols.t[:, 1 + r, :], scalar=1.0, in1=cols.t[:, 0, :], op0=ALU.add, op1=ALU.mult),
             reads=[cols], writes=[gain])
        S.op("dve", lambda: nc.vector.tensor_copy(shift.t[:, r, :], cols.t[:, 3 + r, :]), reads=[cols], writes=[shift])
    convw = S.sb("convw", [128, 2 * NFC, 3], F32)
    convb = S.sb("convb", [128, 2 * NFC], F32)
    hmask = S.sb("hmask", [128, 2 * len(passes)], F32)
    S.dma("sp", convw.t[:], convw_d, writes=[convw])
    S.dma("sp", convb.t[:], convb_d, writes=[convb])
    S.dma("sp", hmask.t[:], hmask_d, writes=[hmask])
    GB = S.sb("GB", [128, 4, D], F32)
    with contextlib.ExitStack() as es0:
        tn = [Buf(es0.enter_context(nc.sbuf_tensor(f"sb_tn{i}", [128, D], F32)), f"tn{i}") for i in range(2)]
        for i in range(2):
            S.dma("sp", tn[i].t[:], rows_d[i], writes=[tn[i]])
        for k in range(4):
            S.dma("act", GB.t[:, k, :], rows_d[2 + k], writes=[GB])
        for k in range(4):
            S.op("dve", lambda: nc.vector.tensor_tensor(out=GB.t[:, k, :], in0=GB.t[:, k, :], in1=tn[k // 2].t[:], op=ALU.mult),
                 reads=[GB, tn[k // 2]], writes=[GB])
        S.barrier(tn + [GB])
    N = NormCtx(S, ident, epsc)
    stage = rot_sb(S, "stage", [128, 8, 256], F32, 2)
    pY = rot_ps(S, "pY", [128, D], F32, 2)
    pG = rot_ps(S, "pG", [128, 512], F32, 2)

    for pi, (n_main, n_ctx) in enumerate(passes):
        NTOK = f_ntok(n_main, n_ctx)
        NOUT = (n_main + n_ctx) * 128
        oT_d = dr(f"oT{pi}", [KO, NTOK], BF16)
        x_d = dr(f"x{pi}", [NTOK, D])
        xo_d = dr(f"xo{pi}", [NOUT, D], kind="ExternalOutput")
        xmid_d = dr(f"xmid{pi}", [NTOK, D], kind="Internal")
        xmid_tok = S.tok(f"xmid{pi}")
        hr = 1 + n_main * 128
        NA = n_main * 128 + 4 + n_ctx * 128
        tiles = [(t * 128, 128, 0, 1 + t * 128, t * 128) for t in range(n_main)]
        tiles.append((n_main * 128, 2, 0, None, None))
        tiles += [(n_main * 128 + 2 + j * 128, 128, 1, hr + 2 + j * 128, n_main * 128 + j * 128) for j in range(n_ctx)]
        with contextlib.ExitStack() as esP:
            sbp = lambda name, shape, dt: Buf(esP.enter_context(nc.sbuf_tensor(f"sb_{name}_{pi}", list(shape), dt)), f"{name}_{pi}")
            hid = sbp("hid", [128, NFC, NA], BF16)
            with contextlib.ExitStack() as es2:
                sb2 = lambda name, shape, dt: Buf(es2.enter_context(nc.sbuf_tensor(f"sb_{name}_{pi}", list(shape), dt)), f"{name}_{pi}")
                hT = sb2("hT", [128, 8, NTOK], BF16)
                with contextlib.ExitStack() as es1:
                    sb1 = lambda name, shape, dt: Buf(es1.enter_context(nc.sbuf_tensor(f"sb_{name}_{pi}", list(shape), dt)), f"{name}_{pi}")
                    oTs = sb1("oTs", [128, KC, NTOK], BF16)
                    oT_v = oT_d.rearrange("(c p) n -> p c n", p=128)
                    for c in range(KC):
                        S.dma("sp" if c % 2 == 0 else "act", oTs.t[:, c, :], oT_v[:, c, :], writes=[oTs])
                    wo = sb1("wo", [128, KC, D], BF16)
                    wo_v = w_o_d.rearrange("(c p) n -> p c n", p=128)
                    for c in range(KC):
                        for h4 in range(4):
                            st = stage.next()
                            S.dma("sp", st.t[:, 0, :], wo_v[:, c, h4 * 256:(h4 + 1) * 256], writes=[st])
                            S.op("pool", lambda: nc.gpsimd.tensor_copy(wo.t[:, c, h4 * 256:(h4 + 1) * 256], st.t[:, 0, :]), reads=[st], writes=[wo])
                    xts = [sb1(f"xt{i}", [128, D], F32) for i in range(2)]
                    xms = [sb1(f"xm{i}", [128, D], F32) for i in range(2)]
                    for ti, (col0, P, r, acol, orow) in enumerate(tiles):
                        xt = xts[ti % 2]
                        xm = xms[ti % 2]
                        S.dma("sp", xt.t[:P, :], x_d[col0:col0 + P, :], writes=[xt])
                        y = pY.next()
                        for half in range(2):
                            for c in range(KC):
                                S.op("pe", lambda: nc.tensor.matmul(y.t[:P, half * 512:(half + 1) * 512], lhsT=oTs.t[:, c, col0:col0 + P], rhs=wo.t[:, c, half * 512:(half + 1) * 512],
                                                                    start=(c == 0), stop=(c == KC - 1)), reads=[oTs, wo], writes=[y])
                        ss = N.ss.next()
                        sq = N.sq.next()
                        emit_rstd(S, nc, y.t[:P, :], y, P, ss, sq, epsc)
                        S.op("dve", lambda: nc.vector.scalar_tensor_tensor(out=xm.t[:P, :], in0=y.t[:P, :], scalar=ss.t[:P, 0:1], in1=GB.t[:P, r, :], op0=ALU.mult, op1=ALU.mult),
                             reads=[y, ss, GB], writes=[xm])
                        S.op("pool", lambda: nc.gpsimd.tensor_tensor(out=xm.t[:P, :], in0=xm.t[:P, :], in1=xt.t[:P, :], op=ALU.add), reads=[xm, xt], writes=[xm])
                        if acol is not None:
                            S.dma("act", xmid_d[col0:col0 + P, :], xm.t[:P, :], reads=[xm], writes=[xmid_tok])
                        emit_norm_T(S, nc, N, xm, P, gain, shift, r, hT, col0)
                    S.barrier([oTs, wo] + xts + xms)
                with contextlib.ExitStack() as es1:
                    sb1 = lambda name, shape, dt: Buf(es1.enter_context(nc.sbuf_tensor(f"sb_{name}_{pi}", list(shape), dt)), f"{name}_{pi}")
                    wblk = [sb1(f"wblk{i}", [128, 8, 256], BF16) for i in range(2)]
                    ab = [sb1("abg", [128, NA], F32), sb1("abu", [128, NA], F32)]
                    cg = sb1("cg", [128, NA], F32)
                    cu = sb1("cu", [128, NA], F32)
                    tp = sb1("tp", [128, NA], F32)
                    for a in ab:
                        S.op("pool", lambda: nc.gpsimd.memset(a.t[:], 0.0), writes=[a])
                    win_v = w_in_d.rearrange("(c p) n -> p c n", p=128)
                    groups = [(g * 512, 512, 1 + g * 512) for g in range(n_main // 4)]
                    if n_ctx:
                        groups.append((n_main * 128 + 2, n_ctx * 128, hr + 2))
                    wi = 0
                    for jb in range(NFC // 2):
                        blk = []
                        for which in range(2):
                            st = stage.next()
                            wb = wblk[which]
                            c0 = which * D_FF + jb * 256
                            S.dma("sp" if which == 0 else "act", st.t[:], win_v[:, :, c0:c0 + 256], writes=[st])
                            S.op("pool", lambda: nc.gpsimd.tensor_copy(wb.t[:], st.t[:]), reads=[st], writes=[wb])
                            blk.append(wb)
                        for jl in range(2):
                            j = jb * 2 + jl
                            for which in range(2):
                                wb = blk[which]
                                a = ab[which]
                                fc = which * NFC + j
                                for (tc0, n, ac0) in groups:
                                    pg = pG.next()
                                    for c in range(8):
                                        S.op("pe", lambda: nc.tensor.matmul(pg.t[:, :n], lhsT=wb.t[:, c, jl * 128:(jl + 1) * 128], rhs=hT.t[:, c, tc0:tc0 + n], start=(c == 0), stop=(c == 7)),
                                             reads=[wb, hT], writes=[pg])
                                    if wi % 2 == 0:
                                        S.op("act", lambda: nc.scalar.copy(a.t[:, ac0:ac0 + n], pg.t[:, :n]), reads=[pg], writes=[a])
                                    else:
                                        S.op("dve", lambda: nc.vector.tensor_copy(a.t[:, ac0:ac0 + n], pg.t[:, :n]), reads=[pg], writes=[a])
                                    wi += 1
                                pg = pG.next()
                                hc = n_main * 128
                                for c in range(8):
                                    S.op("pe", lambda: nc.tensor.matmul(pg.t[:, :2], lhsT=wb.t[:, c, jl * 128:(jl + 1) * 128], rhs=hT.t[:, c, hc:hc + 2], start=(c == 0), stop=(c == 7)),
                                         reads=[wb, hT], writes=[pg])
                                S.op("dve", lambda: nc.vector.tensor_tensor(out=a.t[:, 0:1], in0=pg.t[:, 0:1], in1=hmask.t[:, 2 * pi:2 * pi + 1], op=ALU.mult), reads=[pg, hmask], writes=[a])
                                S.op("dve", lambda: nc.vector.tensor_tensor(out=a.t[:, hr:hr + 1], in0=pg.t[:, 1:2], in1=hmask.t[:, 2 * pi + 1:2 * pi + 2], op=ALU.mult), reads=[pg, hmask], writes=[a])
                                cc = cg if which == 0 else cu
                                S.op("act", lambda: nc.scalar.activation(out=cc.t[:, 1:NA - 1], in_=a.t[:, 1:NA - 1], func=AF.Identity, bias=convb.t[:, fc:fc + 1], scale=convw.t[:, fc, 1:2]),
                                     reads=[a, convw, convb], writes=[cc])
                                S.op("dve", lambda: nc.vector.scalar_tensor_tensor(out=cc.t[:, 1:NA - 1], in0=a.t[:, 0:NA - 2], scalar=convw.t[:, fc, 0:1], in1=cc.t[:, 1:NA - 1], op0=ALU.mult, op1=ALU.add),
                                     reads=[a, convw, cc], writes=[cc])
                                if which == 0:
                                    S.op("dve", lambda: nc.vector.scalar_tensor_tensor(out=cc.t[:, 1:NA - 1], in0=a.t[:, 2:NA], scalar=convw.t[:, fc, 2:3], in1=cc.t[:, 1:NA - 1], op0=ALU.mult, op1=ALU.add),
                                         reads=[a, convw, cc], writes=[cc])
                                else:
                                    S.op("pool", lambda: nc.gpsimd.tensor_scalar(tp.t[:, 1:NA - 1], a.t[:, 2:NA], convw.t[:, fc, 2:3], None, ALU.mult), reads=[a, convw], writes=[tp])
                                    S.op("pool", lambda: nc.gpsimd.tensor_tensor(out=cc.t[:, 1:NA - 1], in0=cc.t[:, 1:NA - 1], in1=tp.t[:, 1:NA - 1], op=ALU.add), reads=[cc, tp], writes=[cc])
                            S.op("act", lambda: nc.scalar.activation(out=cg.t[:, 1:NA - 1], in_=cg.t[:, 1:NA - 1], func=AF.Silu), reads=[cg], writes=[cg])
                            S.op("pool", lambda: nc.gpsimd.tensor_tensor(out=hid.t[:, j, 1:NA - 1], in0=cg.t[:, 1:NA - 1], in1=cu.t[:, 1:NA - 1], op=ALU.mult), reads=[cg, cu], writes=[hid])
                    S.barrier([cg, cu, tp, hT] + ab + wblk)
            with contextlib.ExitStack() as es1:
                sb1 = lambda name, shape, dt: Buf(es1.enter_context(nc.sbuf_tensor(f"sb_{name}_{pi}", list(shape), dt)), f"{name}_{pi}")
                wout = sb1("wout", [128, NFC, D], BF16)
                wout_v = w_out_d.rearrange("(c p) n -> p c n", p=128)
                for j in range(NFC):
                    for h4 in range(4):
                        st = stage.next()
                        S.dma("sp" if h4 % 2 == 0 else "act", st.t[:, 0, :], wout_v[:, j, h4 * 256:(h4 + 1) * 256], writes=[st])
                        S.op("pool", lambda: nc.gpsimd.tensor_copy(wout.t[:, j, h4 * 256:(h4 + 1) * 256], st.t[:, 0, :]), reads=[st], writes=[wout])
                xms = [sb1(f"xm3{i}", [128, D], F32) for i in range(2)]
                xos = [sb1(f"xo3{i}", [128, D], F32) for i in range(2)]
                k3 = 0
                for (col0, P, r, acol, orow) in tiles:
                    if acol is None:
                        continue
                    xm = xms[k3 % 2]
                    xo = xos[k3 % 2]
                    k3 += 1
                    S.dma("sp", xm.t[:], xmid_d[col0:col0 + 128, :], reads=[xmid_tok], writes=[xm])
                    y = pY.next()
                    for half in range(2):
                        for j in range(NFC):
                            S.op("pe", lambda: nc.tensor.matmul(y.t[:, half * 512:(half + 1) * 512], lhsT=hid.t[:, j, acol:acol + 128], rhs=wout.t[:, j, half * 512:(half + 1) * 512],
                                                                start=(j == 0), stop=(j == NFC - 1)), reads=[hid, wout], writes=[y])
                    ss = N.ss.next()
                    sq = N.sq.next()
                    emit_rstd(S, nc, y.t[:, :], y, 128, ss, sq, epsc)
                    S.op("dve", lambda: nc.vector.scalar_tensor_tensor(out=xo.t[:], in0=y.t[:], scalar=ss.t[:, 0:1], in1=GB.t[:, 2 + r, :], op0=ALU.mult, op1=ALU.mult),
                         reads=[y, ss, GB], writes=[xo])
                    S.op("pool", lambda: nc.gpsimd.tensor_tensor(out=xo.t[:], in0=xo.t[:], in1=xm.t[:], op=ALU.add), reads=[xo, xm], writes=[xo])
                    S.dma("act", xo_d[orow:orow + 128, :], xo.t[:], reads=[xo], writes=[], out=True)
                S.barrier([wout, hid] + xms + xos)
    S.finish()
    return nc


GELU_C = 1.5957691216057308


def build_ME(n_lat=64, n_ctx=2, n_b=2):
    nc = bass.Bass("TRN2", target_bir_lowering=False)
    TPB = n_lat + n_ctx
    NT = n_b * TPB
    dr = lambda name, shape, dt=F32, kind="ExternalInput": nc.dram_tensor(name, list(shape), dt, kind=kind).ap()
    x_d = dr("x", [NT * 128, D])
    wqk_d = dr("wqk", [D, 128])
    wgu_d = dr("wgu", [D, 576])
    wv_d = dr("wv", [D, 64])
    wsT_d = dr("wsT", [128, 128])
    bs_d = dr("bs", [128, 1])
    bias_d = dr("bias", [5, 128, 576])
    cols_d = dr("cols", [128, 1 + 2 * (n_b + 1), 8])
    ident_d = dr("ident", [128, 128])
    out_d = dr("out", [NT * 128, 128], BF16, kind="ExternalOutput")
    S = Sched(nc)
    ident_f = S.sb("ident_f", [128, 128], F32)
    ident = S.sb("ident_b", [128, 128], BF16)
    S.dma("sp", ident_f.t[:], ident_d, writes=[ident_f])
    S.op("dve", lambda: nc.vector.tensor_copy(ident.t[:], ident_f.t[:]), reads=[ident_f], writes=[ident])
    epsc = S.sb("epsc", [128, 1], F32)
    S.op("dve", lambda: nc.vector.memset(epsc.t[:], EPS), writes=[epsc])
    NR = n_b + 1
    cols = S.sb("cols", [128, 1 + 2 * NR, 8], F32)
    S.dma("sp", cols.t[:], cols_d, writes=[cols])
    gain = S.sb("gain", [128, NR, 8], F32)
    shift = S.sb("shift", [128, NR, 8], F32)
    for r in range(NR):
        S.op("dve", lambda: nc.vector.scalar_tensor_tensor(out=gain.t[:, r, :], in0=cols.t[:, 1 + r, :], scalar=1.0, in1=cols.t[:, 0, :], op0=ALU.add, op1=ALU.mult),
             reads=[cols], writes=[gain])
        S.op("dve", lambda: nc.vector.tensor_copy(shift.t[:, r, :], cols.t[:, 1 + NR + r, :]), reads=[cols], writes=[shift])
    stage = S.sb("stage", [128, 8, 576], F32)
    wqk = S.sb("wqk", [128, 8, 128], BF16)
    wgu = S.sb("wgu", [128, 8, 576], BF16)
    wv = S.sb("wv", [128, 8, 64], BF16)
    for (wb, wd, n) in ((wqk, wqk_d, 128), (wgu, wgu_d, 576), (wv, wv_d, 64)):
        S.dma("sp", stage.t[:, :, :n], wd.rearrange("(c p) n -> p c n", p=128), writes=[stage])
        S.op("dve", lambda: nc.vector.tensor_copy(wb.t[:], stage.t[:, :, :n]), reads=[stage], writes=[wb])
    wsT = S.sb("wsT", [128, 128], BF16)
    S.dma("sp", stage.t[:, 0, :128], wsT_d, writes=[stage])
    S.op("dve", lambda: nc.vector.tensor_copy(wsT.t[:], stage.t[:, 0, :128]), reads=[stage], writes=[wsT])
    bs = S.sb("bs", [128, 1], F32)
    S.dma("sp", bs.t[:], bs_d, writes=[bs])
    biasT = S.sb("biasT", [128, 5, 576], F32)
    for k in range(5):
        S.dma("act", biasT.t[:, k, :], bias_d[k], writes=[biasT])
    qT = S.sb("qT", [64, NT * 128], BF16)
    kT = S.sb("kT", [64, NT * 128], BF16)
    vA = S.sb("vA", [128, NT, 64], BF16)
    oA = S.sb("oA", [128, NT, 128], BF16)
    with S.scope() as esA:
        def sbA(name, shape, dt):
            return Buf(esA.enter_context(nc.sbuf_tensor("sbA_" + name, list(shape), dt)), name)
        def psA(name, shape, dt=F32):
            return Buf(esA.enter_context(nc.psum_tensor("psA_" + name, list(shape), dt)), name)
        N = NormCtx(S, ident, epsc)
        xts = [sbA(f"xt{i}", [128, D], F32) for i in range(2)]
        hTs = [sbA(f"hT{i}", [128, 8, 128], BF16) for i in range(2)]
        pQK = psA("pQK", [64, 2, 128])
        pGU = psA("pGU", [128, 1024])
        pSG = psA("pSG", [128, 128])
        xg = sbA("xg", [128, 576], F32)
        t1 = sbA("t1", [128, 576], F32)
        gl = sbA("gl", [128, 576], F32)
        junk = sbA("junk", [128, 512], F32)
        st = sbA("st", [128, 4], F32)
        vn = sbA("vn", [128, 64], BF16)
        for ti in range(NT):
            b, tl = divmod(ti, TPB)
            r = b if tl < n_lat else n_b
            xt = xts[ti % 2]
            hT = hTs[ti % 2]
            S.dma("sp" if ti % 2 == 0 else "act", xt.t[:], x_d[ti * 128:(ti + 1) * 128, :], writes=[xt])
            emit_norm_T(S, nc, N, xt, 128, gain, shift, r, hT, 0)
            for w in range(2):
                for c in range(8):
                    S.op("pe", lambda: nc.tensor.matmul(pQK.t[:, w, :], lhsT=wqk.t[:, c, w * 64:(w + 1) * 64], rhs=hT.t[:, c, :], start=(c == 0), stop=(c == 7)), reads=[wqk, hT], writes=[pQK])
            S.op("act", lambda: nc.scalar.activation(out=qT.t[:, ti * 128:(ti + 1) * 128], in_=pQK.t[:, 0, :], func=AF.Copy, scale=0.125), reads=[pQK], writes=[qT])
            S.op("act", lambda: nc.scalar.copy(kT.t[:, ti * 128:(ti + 1) * 128], pQK.t[:, 1, :]), reads=[pQK], writes=[kT])
            for (o0, n, wb, wo0) in ((0, 512, wgu, 0), (512, 64, wgu, 512), (576, 64, wv, 0)):
                for c in range(8):
                    S.op("pe", lambda: nc.tensor.matmul(pGU.t[:, o0:o0 + n], lhsT=hT.t[:, c, :], rhs=wb.t[:, c, wo0:wo0 + n], start=(c == 0), stop=(c == 7)), reads=[wb, hT], writes=[pGU])
            S.op("act", lambda: nc.scalar.copy(vA.t[:, ti, :], pGU.t[:, 576:640]), reads=[pGU], writes=[vA])
            S.op("act", lambda: nc.scalar.copy(xg.t[:], pGU.t[:, 0:576]), reads=[pGU], writes=[xg])
            S.op("dve", lambda: nc.vector.tensor_tensor(out=t1.t[:], in0=xg.t[:], in1=xg.t[:], op=ALU.mult), reads=[xg], writes=[t1])
            S.op("dve", lambda: nc.vector.tensor_scalar(t1.t[:], t1.t[:], 0.044715, 1.0, ALU.mult, ALU.add), reads=[t1], writes=[t1])
            S.op("pool", lambda: nc.gpsimd.tensor_tensor(out=t1.t[:], in0=t1.t[:], in1=xg.t[:], op=ALU.mult), reads=[t1, xg], writes=[t1])
            S.op("act", lambda: nc.scalar.activation(out=t1.t[:], in_=t1.t[:], func=AF.Sigmoid, scale=GELU_C), reads=[t1], writes=[t1])
            S.op("pool", lambda: nc.gpsimd.tensor_tensor(out=gl.t[:], in0=t1.t[:], in1=xg.t[:], op=ALU.mult), reads=[t1, xg], writes=[gl])
            S.op("act", lambda: nc.scalar.activation(out=junk.t[:], in_=gl.t[:, 0:512], func=AF.Identity, accum_out=st.t[:, 0:1]), reads=[gl], writes=[junk, st])
            S.op("act", lambda: nc.scalar.activation(out=junk.t[:], in_=gl.t[:, 0:512], func=AF.Square, accum_out=st.t[:, 1:2]), reads=[gl], writes=[junk, st])
            S.op("dve", lambda: nc.vector.tensor_scalar(st.t[:, 0:2], st.t[:, 0:2], 1.0 / 512, None, ALU.mult), reads=[st], writes=[st])
            S.op("dve", lambda: nc.vector.tensor_tensor(out=st.t[:, 2:3], in0=st.t[:, 0:1], in1=st.t[:, 0:1], op=ALU.mult), reads=[st], writes=[st])
            S.op("dve", lambda: nc.vector.tensor_tensor(out=st.t[:, 2:3], in0=st.t[:, 1:2], in1=st.t[:, 2:3], op=ALU.subtract), reads=[st], writes=[st])
            S.op("act", lambda: nc.scalar.activation(out=st.t[:, 2:3], in_=st.t[:, 2:3], func=AF.Sqrt, bias=epsc.t[:], scale=1.0), reads=[st, epsc], writes=[st])
            S.op("dve", lambda: nc.vector.reciprocal(st.t[:, 2:3], st.t[:, 2:3]), reads=[st], writes=[st])
            S.op("dve", lambda: nc.vector.tensor_scalar(vn.t[:], gl.t[:, 0:64], st.t[:, 0:1], st.t[:, 2:3], ALU.subtract, ALU.mult), reads=[gl, st], writes=[vn])
            S.op("pe", lambda: nc.tensor.matmul(pSG.t[:, 0:64], lhsT=wsT.t[:], rhs=vn.t[:], start=True, stop=True), reads=[wsT, vn], writes=[pSG])
            S.op("dve", lambda: nc.vector.scalar_tensor_tensor(out=oA.t[:, ti, 64:128], in0=pSG.t[:, 0:64], scalar=bs.t[:, 0:1], in1=gl.t[:, 512:576], op0=ALU.add, op1=ALU.mult),
                 reads=[pSG, bs, gl], writes=[oA])
        S.barrier(xts + hTs + [xg, t1, gl, junk, st, vn, pQK, pGU, pSG] + N.sq.bufs + N.ss.bufs + N.xn.bufs + N.pT.bufs)
    with contextlib.ExitStack() as esB:
        def sbB(name, shape, dt):
            return Buf(esB.enter_context(nc.sbuf_tensor("sbB_" + name, list(shape), dt)), name)
        def psB(name, shape, dt=F32):
            return Buf(esB.enter_context(nc.psum_tensor("psB_" + name, list(shape), dt)), name)
        pSs = [psB(f"pS{i}", [128, 1024]) for i in range(2)]
        pPTs = [psB(f"pPT{i}", [128, 7, 128], BF16) for i in range(2)]
        pOs = [psB(f"pO{i}", [128, 64]) for i in range(2)]
        Ts = [sbB(f"T{i}", [128, 832], F32) for i in range(2)]
        Ps = [sbB(f"P{i}", [128, 832], BF16) for i in range(2)]
        PTs = [sbB(f"PT{i}", [128, 7, 128], BF16) for i in range(2)]
        mxs = [sbB(f"mx{i}", [128, 2], F32) for i in range(2)]
        step = 0
        for b in range(n_b):
            base = b * TPB
            ctx_tiles = [base + n_lat + j for j in range(n_ctx)]
            for tl in range(TPB):
                ti = base + tl
                pS, pPT, pO, T, P, PT, mx = [x[step % 2] for x in (pSs, pPTs, pOs, Ts, Ps, PTs, mxs)]
                step += 1
                if tl < n_lat:
                    tw = min(max(tl - 2, 0), n_lat - 4)
                    full = [base + tw + k for k in range(4)]
                    extra = (base + tw + 4) if (2 <= tl <= n_lat - 3) else None
                    cls = 0 if tl == 0 else 1 if tl == 1 else 3 if tl == n_lat - 2 else 4 if tl == n_lat - 1 else 2
                    nnb = 512 + (64 if extra is not None else 0)
                else:
                    full, extra, cls, nnb = [], None, None, 0
                ncx = n_ctx * 128
                W = nnb + ncx
                qs = qT.t[:, ti * 128:(ti + 1) * 128]
                if full:
                    S.op("pe", lambda: nc.tensor.matmul(pS.t[:, 0:512], lhsT=qs, rhs=kT.t[:, full[0] * 128:(full[0] + 4) * 128], start=True, stop=True), reads=[qT, kT], writes=[pS])
                    if extra is not None:
                        S.op("pe", lambda: nc.tensor.matmul(pS.t[:, 512:576], lhsT=qs, rhs=kT.t[:, extra * 128:extra * 128 + 64], start=True, stop=True), reads=[qT, kT], writes=[pS])
                c0 = 512 + (64 if extra is not None else 0) if full else 0
                S.op("pe", lambda: nc.tensor.matmul(pS.t[:, c0:c0 + ncx], lhsT=qs, rhs=kT.t[:, ctx_tiles[0] * 128:ctx_tiles[0] * 128 + ncx], start=True, stop=True), reads=[qT, kT], writes=[pS])
                if full:
                    S.op("dve", lambda: nc.vector.tensor_tensor(out=T.t[:, 0:nnb], in0=pS.t[:, 0:nnb], in1=biasT.t[:, cls, 0:nnb], op=ALU.add), reads=[pS, biasT], writes=[T])
                S.op("act", lambda: nc.scalar.copy(T.t[:, nnb:W], pS.t[:, c0:c0 + ncx]), reads=[pS], writes=[T])
                S.op("dve", lambda: nc.vector.tensor_reduce(out=mx.t[:, 0:1], in_=T.t[:, 0:W], axis=AX.X, op=ALU.max), reads=[T], writes=[mx])
                S.op("dve", lambda: nc.vector.tensor_scalar(mx.t[:, 0:1], mx.t[:, 0:1], -1.0, None, ALU.mult), reads=[mx], writes=[mx])
                S.op("act", lambda: nc.scalar.activation(out=P.t[:, 0:W], in_=T.t[:, 0:W], func=AF.Exp, bias=mx.t[:, 0:1], scale=1.0, accum_out=mx.t[:, 1:2]), reads=[T, mx], writes=[P, mx])
                S.op("dve", lambda: nc.vector.reciprocal(mx.t[:, 1:2], mx.t[:, 1:2]), reads=[mx], writes=[mx])
                chunks = [(k * 128, 128, full[k]) for k in range(len(full))]
                if extra is not None:
                    chunks.append((512, 64, extra))
                chunks += [(nnb + j * 128, 128, ctx_tiles[j]) for j in range(n_ctx)]
                for ci, (pc0, n, vt) in enumerate(chunks):
                    S.op("pe", lambda: nc.tensor.transpose(pPT.t[:n, ci, :], P.t[:, pc0:pc0 + n], ident.t[:]), reads=[P, ident], writes=[pPT])
                ncf = len(chunks)
                S.op("dve", lambda: nc.vector.tensor_copy(PT.t[:, 0:ncf, :], pPT.t[:, 0:ncf, :]), reads=[pPT], writes=[PT])
                for ci, (pc0, n, vt) in enumerate(chunks):
                    S.op("pe", lambda: nc.tensor.matmul(pO.t[:, :], lhsT=PT.t[:n, ci, :], rhs=vA.t[:n, vt, :], start=(ci == 0), stop=(ci == ncf - 1)), reads=[PT, vA], writes=[pO])
                S.op("act", lambda: nc.scalar.activation(out=oA.t[:, ti, 0:64], in_=pO.t[:, :], func=AF.Copy, scale=mx.t[:, 1:2]), reads=[pO, mx], writes=[oA])
        S.barrier(pSs + pPTs + pOs + Ts + Ps + PTs + mxs)
    out_v = out_d.rearrange("(t p) n -> p t n", p=128)
    nchunk = 4
    per = (NT + nchunk - 1) // nchunk
    for k in range(nchunk):
        a, bnd = k * per, min(NT, (k + 1) * per)
        if a < bnd:
            S.dma("sp" if k % 2 == 0 else "act", out_v[:, a:bnd, :], oA.t[:, a:bnd, :], reads=[oA], writes=[], out=True)
    S.finish()
    return nc


def _fm(v):
    return np.ascontiguousarray(np.asarray(v, np.float32).reshape(-1, 128).T)


def na_bias_tables(rpb_h, n_lat):
    rows = 2 * n_lat
    out = np.full((5, 128, 576), -1e30, np.float32)
    reps = [0, 1, 2, n_lat - 2, n_lat - 1]
    qi = np.arange(128)
    kj = np.arange(576)
    for cls, tl in enumerate(reps):
        tw = min(max(tl - 2, 0), n_lat - 4)
        r = 2 * tl + qi // 64
        qc = qi % 64
        r0 = np.clip(r - 4, 0, rows - 8)
        c0 = np.clip(qc - 8, 0, 64 - 16)
        kr = 2 * tw + kj // 64
        kc = kj % 64
        valid = ((kr[None, :] >= r0[:, None]) & (kr[None, :] < r0[:, None] + 8)
                 & (kc[None, :] >= c0[:, None]) & (kc[None, :] < c0[:, None] + 16))
        dr_ = np.clip(kr[None, :] - r[:, None] + 7, 0, 14)
        dc_ = np.clip(kc[None, :] - qc[:, None], -15, 15) + 15
        out[cls] = np.where(valid, rpb_h[dr_, dc_], np.float32(-1e30))
    return out


def prep_ME(h, x, xc, w_in, rpb, w_s, b_s, n0, sc1, sh1):
    n_b = x.shape[0]
    n_lat = x.shape[1] // 128
    x_all = np.concatenate([np.concatenate([x[b], xc[b]], 0) for b in range(n_b)], 0)
    hs = slice(h * 64, (h + 1) * 64)
    g_cols = w_in[:, 2048:2560]
    order = list(range(h * 64, (h + 1) * 64)) + [c for c in range(512) if not (h * 64 <= c < (h + 1) * 64)]
    ins = {
        "x": np.ascontiguousarray(x_all, np.float32),
        "wqk": np.ascontiguousarray(np.concatenate([w_in[:, 0:512][:, hs], w_in[:, 512:1024][:, hs]], 1)),
        "wgu": np.ascontiguousarray(np.concatenate([g_cols[:, order], w_in[:, 1536:2048][:, hs]], 1)),
        "wv": np.ascontiguousarray(w_in[:, 1024:1536][:, hs]),
        "wsT": np.ascontiguousarray(w_s[h].T),
        "bs": np.ascontiguousarray(b_s[h][:, None]),
        "bias": na_bias_tables(rpb[h], n_lat),
        "cols": np.ascontiguousarray(np.stack([_fm(n0)] + [_fm(v) for v in sc1] + [_fm(v) for v in sh1], 1)),
        "ident": np.eye(128, dtype=np.float32),
    }
    return ins, None


def build_MO(n_lat=64, n_ctx=2, n_b=2, dbg=None, lvl=99):
    nc = bass.Bass("TRN2", target_bir_lowering=False)
    TPB = n_lat + n_ctx
    NT = n_b * TPB
    NR = n_b + 1
    dr = lambda name, shape, dt=F32, kind="ExternalInput": nc.dram_tensor(name, list(shape), dt, kind=kind).ap()
    x_d = dr("x", [NT * 128, D])
    w_d = dr("w", [D, 768])
    cols_d = dr("cols", [128, 1 + 2 * NR, 8])
    rope_d = dr("rope", [n_lat, 128, 256])
    lg_d = dr("lg", [128, 2])
    cE_d = dr("cE", [2, 128, 128])
    cM_d = dr("cM", [2, 128, 128])
    cQ_d = dr("cQ", [2, 128, 128])
    cK_d = dr("cK", [128, 2])
    gn_d = dr("gn", [128, 256])
    ident_d = dr("ident", [128, 128])
    out_d = dr("out", [NT * 128, 256], BF16, kind="ExternalOutput")
    gS_d = dr("gS", [NT * 128, 256], kind="Internal")
    oF_d = dr("oF", [NT * 128, 256], kind="Internal")
    S = Sched(nc)
    ident_f = S.sb("ident_f", [128, 128], F32)
    ident = S.sb("ident_b", [128, 128], BF16)
    S.dma("sp", ident_f.t[:], ident_d, writes=[ident_f])
    S.op("dve", lambda: nc.vector.tensor_copy(ident.t[:], ident_f.t[:]), reads=[ident_f], writes=[ident])
    epsc = S.sb("epsc", [128, 1], F32)
    S.op("dve", lambda: nc.vector.memset(epsc.t[:], EPS), writes=[epsc])
    cols = S.sb("cols", [128, 1 + 2 * NR, 8], F32)
    S.dma("sp", cols.t[:], cols_d, writes=[cols])
    gain = S.sb("gain", [128, NR, 8], F32)
    shift = S.sb("shift", [128, NR, 8], F32)
    for r in range(NR):
        S.op("dve", lambda: nc.vector.scalar_tensor_tensor(out=gain.t[:, r, :], in0=cols.t[:, 1 + r, :], scalar=1.0, in1=cols.t[:, 0, :], op0=ALU.add, op1=ALU.mult),
             reads=[cols], writes=[gain])
        S.op("dve", lambda: nc.vector.tensor_copy(shift.t[:, r, :], cols.t[:, 1 + NR + r, :]), reads=[cols], writes=[shift])
    wb = S.sb("wb", [128, 8, 768], BF16)
    with S.scope():
        stage = S.sb("stage", [128, 8, 768], F32)
        S.dma("sp", stage.t[:], w_d.rearrange("(c p) n -> p c n", p=128), writes=[stage])
        S.op("dve", lambda: nc.vector.tensor_copy(wb.t[:], stage.t[:]), reads=[stage], writes=[wb])
        S.barrier([stage])
    lg = S.sb("lg", [128, 2], F32)
    S.dma("sp", lg.t[:], lg_d, writes=[lg])
    DT = S.sb("DT", [128, 2, 128], F32)
    DQ = S.sb("DQ", [128, 2, 128], F32)
    DK = S.sb("DK", [128, 2], F32)
    GC = S.sb("GC", [128, 2], F32)
    gn = S.sb("gn", [128, 256], F32)
    S.dma("sp", gn.t[:], gn_d, writes=[gn])
    with S.scope():
        cE = S.sb("cE", [128, 2, 128], F32)
        cM = S.sb("cM", [128, 2, 128], F32)
        cQ = S.sb("cQ", [128, 2, 128], F32)
        cK = S.sb("cK", [128, 2], F32)
        c128 = S.sb("c128", [128, 1], F32)
        S.op("dve", lambda: nc.vector.memset(c128.t[:], 128.0), writes=[c128])
        S.dma("sp", cK.t[:], cK_d, writes=[cK])
        for d in range(2):
            S.dma("sp", cE.t[:, d, :], cE_d[d], writes=[cE])
            S.dma("act", cM.t[:, d, :], cM_d[d], writes=[cM])
            S.dma("sp", cQ.t[:, d, :], cQ_d[d], writes=[cQ])
        for d in range(2):
            S.op("act", lambda: nc.scalar.activation(out=DT.t[:, d, :], in_=cE.t[:, d, :], func=AF.Exp, scale=lg.t[:, d:d + 1]), reads=[cE, lg], writes=[DT])
            S.op("dve", lambda: nc.vector.tensor_tensor(out=DT.t[:, d, :], in0=DT.t[:, d, :], in1=cM.t[:, d, :], op=ALU.mult), reads=[DT, cM], writes=[DT])
            S.op("act", lambda: nc.scalar.activation(out=DQ.t[:, d, :], in_=cQ.t[:, d, :], func=AF.Exp, scale=lg.t[:, d:d + 1]), reads=[cQ, lg], writes=[DQ])
            S.op("act", lambda: nc.scalar.activation(out=DK.t[:, d:d + 1], in_=cK.t[:, d:d + 1], func=AF.Exp, scale=lg.t[:, d:d + 1]), reads=[cK, lg], writes=[DK])
            S.op("act", lambda: nc.scalar.activation(out=GC.t[:, d:d + 1], in_=c128.t[:], func=AF.Exp, scale=lg.t[:, d:d + 1]), reads=[c128, lg], writes=[GC])
        S.barrier([cE, cM, cQ, cK, c128])
    if dbg == "P":
        S.barrier([DT, DQ, DK, GC, gn, wb])
        S.finish()
        return nc
    gS_tok = S.shared_toks("gS", NT, 4)
    oF_tok = S.shared_toks("oF", NT, 4)
    for b in range(n_b):
        base = b * TPB
        with S.scope():
            qT = S.sb(f"qT{b}", [128, TPB * 128], BF16)
            kT = S.sb(f"kT{b}", [128, TPB * 128], BF16)
            kK = S.sb(f"kK{b}", [128, TPB, 128], BF16)
            vA = S.sb(f"vA{b}", [128, TPB, 256], BF16)
            with S.scope():
                N = NormCtx(S, ident, epsc)
                xts = [S.sb(f"xt{b}_{i}", [128, D], F32) for i in range(2)]
                hTs = [S.sb(f"hT{b}_{i}", [128, 8, 128], BF16) for i in range(2)]
                pIn = S.ps(f"pIn{b}", [128, 1024])
                pT2 = S.ps(f"pT2{b}", [128, 2, 128], BF16)
                qk = S.sb(f"qk{b}", [128, 256], F32)
                rp = [S.sb(f"rp{b}_{i}", [128, 256], F32) for i in range(2)]
                ta = S.sb(f"ta{b}", [128, 2, 64], F32)
                tb = S.sb(f"tb{b}", [128, 2, 64], F32)
                rq = S.sb(f"rq{b}", [128, 256], BF16)
                gs = [S.sb(f"gs{b}_{i}", [128, 256], F32) for i in range(2)]
                for tl in range(TPB):
                    ti = base + tl
                    is_lat = tl < n_lat
                    r = b if is_lat else n_b
                    xt = xts[tl % 2]
                    hT = hTs[tl % 2]
                    S.dma("sp" if tl % 2 == 0 else "act", xt.t[:], x_d[ti * 128:(ti + 1) * 128, :], writes=[xt])
                    if lvl < 1:
                        continue
                    emit_norm_T(S, nc, N, xt, 128, gain, shift, r, hT, 0)
                    if lvl < 2:
                        continue
                    for (o0, n) in ((0, 512), (512, 256)):
                        for c in range(8):
                            S.op("pe", lambda: nc.tensor.matmul(pIn.t[:, o0:o0 + n], lhsT=hT.t[:, c, :], rhs=wb.t[:, c, o0:o0 + n], start=(c == 0), stop=(c == 7)), reads=[wb, hT], writes=[pIn])
                    S.op("act", lambda: nc.scalar.copy(vA.t[:, tl, :], pIn.t[:, 256:512]), reads=[pIn], writes=[vA])
                    if lvl < 3:
                        continue
                    g_ = gs[tl % 2]
                    S.op("act", lambda: nc.scalar.activation(out=g_.t[:], in_=pIn.t[:, 512:768], func=AF.Silu), reads=[pIn], writes=[g_])
                    if dbg != "A2":
                        S.dma("act", gS_d[ti * 128:(ti + 1) * 128, :], g_.t[:], reads=[g_], writes=[gS_tok[ti]])
                    if is_lat and dbg != "A1":
                        S.op("act", lambda: nc.scalar.copy(qk.t[:, 0:128], pIn.t[:, 0:128]), reads=[pIn], writes=[qk])
                        S.op("act", lambda: nc.scalar.activation(out=qk.t[:, 128:256], in_=pIn.t[:, 128:256], func=AF.Copy, scale=128.0 ** -0.5), reads=[pIn], writes=[qk])
                        rpt = rp[tl % 2]
                        S.dma("sp", rpt.t[:], rope_d[tl], writes=[rpt])
                        q4 = qk.t[:].rearrange("p (a h c) -> p a h c", a=2, h=2)
                        o4 = rq.t[:].rearrange("p (a h c) -> p a h c", a=2, h=2)
                        cs = rpt.t[:, 0:128].rearrange("p (a c) -> p a c", a=2)
                        sn = rpt.t[:, 128:256].rearrange("p (a c) -> p a c", a=2)
                        S.op("dve", lambda: nc.vector.tensor_tensor(out=ta.t[:], in0=q4[:, :, 0, :], in1=cs, op=ALU.mult), reads=[qk, rpt], writes=[ta])
                        S.op("pool", lambda: nc.gpsimd.tensor_tensor(out=tb.t[:], in0=q4[:, :, 1, :], in1=sn, op=ALU.mult), reads=[qk, rpt], writes=[tb])
                        S.op("dve", lambda: nc.vector.tensor_tensor(out=o4[:, :, 0, :], in0=ta.t[:], in1=tb.t[:], op=ALU.subtract), reads=[ta, tb], writes=[rq])
                        S.op("dve", lambda: nc.vector.tensor_tensor(out=ta.t[:], in0=q4[:, :, 0, :], in1=sn, op=ALU.mult), reads=[qk, rpt, rq], writes=[ta])
                        S.op("pool", lambda: nc.gpsimd.tensor_tensor(out=tb.t[:], in0=q4[:, :, 1, :], in1=cs, op=ALU.mult), reads=[qk, rpt, rq], writes=[tb])
                        S.op("dve", lambda: nc.vector.tensor_tensor(out=o4[:, :, 1, :], in0=ta.t[:], in1=tb.t[:], op=ALU.add), reads=[ta, tb], writes=[rq])
                    else:
                        S.op("act", lambda: nc.scalar.copy(rq.t[:, 0:128], pIn.t[:, 0:128]), reads=[pIn], writes=[rq])
                        S.op("act", lambda: nc.scalar.activation(out=rq.t[:, 128:256], in_=pIn.t[:, 128:256], func=AF.Copy, scale=128.0 ** -0.5), reads=[pIn], writes=[rq])
                    if lvl < 4:
                        continue
                    S.op("pool", lambda: nc.gpsimd.tensor_copy(kK.t[:, tl, :], rq.t[:, 128:256]), reads=[rq], writes=[kK])
                    if lvl < 5:
                        continue
                    for w in range(2):
                        S.op("pe", lambda: nc.tensor.transpose(pT2.t[:, w, :], rq.t[:, w * 128:(w + 1) * 128], ident.t[:]), reads=[rq, ident], writes=[pT2])
                    if lvl < 6:
                        continue
                    S.op("dve", lambda: nc.vector.tensor_copy(qT.t[:, tl * 128:(tl + 1) * 128], pT2.t[:, 0, :]), reads=[pT2], writes=[qT])
                    if lvl < 7:
                        continue
                    S.op("dve", lambda: nc.vector.tensor_copy(kT.t[:, tl * 128:(tl + 1) * 128], pT2.t[:, 1, :]), reads=[pT2], writes=[kT])
                S.barrier(xts + hTs + [pIn, pT2, qk, ta, tb, rq] + rp + gs + N.sq.bufs + N.ss.bufs + N.xn.bufs + N.pT.bufs)
            if dbg in ("A", "A1", "A2"):
                S.barrier(gS_tok + [qT, kT, kK, vA])
                continue
            with S.scope():
                Sf = [S.sb(f"Sf{b}_{d}", [128, 256], F32) for d in range(2)]
                Sb = [S.sb(f"Sb{b}_{d}", [128, 256], BF16) for d in range(2)]
                for d in range(2):
                    S.op("dve", lambda: nc.vector.memset(Sf[d].t[:], 0.0), writes=[Sf[d]])
                    S.op("dve", lambda: nc.vector.memset(Sb[d].t[:], 0.0), writes=[Sb[d]])
                pST = [S.ps(f"pST{b}_{i}", [128, 128]) for i in range(2)]
                pOo = [S.ps(f"pOo{b}_{i}", [128, 256]) for i in range(2)]
                pDS = [S.ps(f"pDS{b}_{i}", [128, 256]) for i in range(2)]
                sTm = [S.sb(f"sTm{b}_{i}", [128, 128], BF16) for i in range(2)]
                qd = [S.sb(f"qd{b}_{i}", [128, 128], BF16) for i in range(2)]
                kd = [S.sb(f"kd{b}_{i}", [128, 128], BF16) for i in range(2)]
                of_ = [S.sb(f"of{b}_{i}", [128, 256], F32) for i in range(2)]
                ofl = [S.sb(f"ofl{b}_{i}", [128, 256], F32) for i in range(2)]
                gl_ = [S.sb(f"gl{b}_{i}", [128, 256], F32) for i in range(2)]
                junk = S.sb(f"junk{b}", [128, 256], F32)
                st = [S.sb(f"st{b}_{i}", [128, 4], F32) for i in range(2)]
                ot = [S.sb(f"ot{b}_{i}", [128, 256], BF16) for i in range(2)]
                order = [list(range(n_lat, TPB)) + list(range(n_lat)),
                         list(range(TPB - 1, n_lat - 1, -1)) + list(range(n_lat - 1, -1, -1))]
                for i in range(TPB):
                    for d in range(2):
                        tl = order[d][i]
                        ti = base + tl
                        k2 = d
                        cs_ = slice(tl * 128, (tl + 1) * 128)
                        S.op("pe", lambda: nc.tensor.matmul(pST[d].t[:], lhsT=kT.t[:, cs_], rhs=qT.t[:, cs_], start=True, stop=True), reads=[kT, qT], writes=[pST[d]])
                        S.op("dve", lambda: nc.vector.tensor_tensor(out=sTm[d].t[:], in0=pST[d].t[:], in1=DT.t[:, d, :], op=ALU.mult), reads=[pST[d], DT], writes=[sTm[d]])
                        S.op("pool", lambda: nc.gpsimd.tensor_tensor(out=qd[d].t[:], in0=qT.t[:, cs_], in1=DQ.t[:, d, :], op=ALU.mult), reads=[qT, DQ], writes=[qd[d]])
                        S.op("pool", lambda: nc.gpsimd.tensor_scalar(kd[d].t[:], kK.t[:, tl, :], DK.t[:, d:d + 1], None, ALU.mult), reads=[kK, DK], writes=[kd[d]])
                        S.op("pe", lambda: nc.tensor.matmul(pOo[d].t[:], lhsT=sTm[d].t[:], rhs=vA.t[:, tl, :], start=True, stop=False), reads=[sTm[d], vA], writes=[pOo[d]])
                        S.op("pe", lambda: nc.tensor.matmul(pOo[d].t[:], lhsT=qd[d].t[:], rhs=Sb[d].t[:], start=False, stop=True), reads=[qd[d], Sb[d]], writes=[pOo[d]])
                        S.op("pe", lambda: nc.tensor.matmul(pDS[d].t[:], lhsT=kd[d].t[:], rhs=vA.t[:, tl, :], start=True, stop=True), reads=[kd[d], vA], writes=[pDS[d]])
                        S.op("dve", lambda: nc.vector.scalar_tensor_tensor(out=Sf[d].t[:], in0=Sf[d].t[:], scalar=GC.t[:, d:d + 1], in1=pDS[d].t[:], op0=ALU.mult, op1=ALU.add),
                             reads=[Sf[d], GC, pDS[d]], writes=[Sf[d]])
                        S.op("act", lambda: nc.scalar.copy(Sb[d].t[:], Sf[d].t[:]), reads=[Sf[d]], writes=[Sb[d]])
                        i_other = order[1 - d].index(tl)
                        if i < i_other or (i == i_other and d == 0):
                            o_ = of_[k2]
                            S.op("act", lambda: nc.scalar.copy(o_.t[:], pOo[d].t[:]), reads=[pOo[d]], writes=[o_])
                            S.dma("sp", oF_d[ti * 128:(ti + 1) * 128, :], o_.t[:], reads=[o_], writes=[oF_tok[ti]])
                        else:
                            o_ = ofl[k2]
                            g_ = gl_[k2]
                            s_ = st[k2]
                            S.dma("sp", o_.t[:], oF_d[ti * 128:(ti + 1) * 128, :], reads=[oF_tok[ti]], writes=[o_])
                            S.dma("act", g_.t[:], gS_d[ti * 128:(ti + 1) * 128, :], reads=[gS_tok[ti]], writes=[g_])
                            S.op("dve", lambda: nc.vector.tensor_tensor(out=o_.t[:], in0=pOo[d].t[:], in1=o_.t[:], op=ALU.add), reads=[pOo[d], o_], writes=[o_])
                            S.op("act", lambda: nc.scalar.activation(out=junk.t[:], in_=o_.t[:], func=AF.Identity, accum_out=s_.t[:, 0:1]), reads=[o_], writes=[junk, s_])
                            S.op("act", lambda: nc.scalar.activation(out=junk.t[:], in_=o_.t[:], func=AF.Square, accum_out=s_.t[:, 1:2]), reads=[o_], writes=[junk, s_])
                            S.op("dve", lambda: nc.vector.tensor_scalar(s_.t[:, 0:2], s_.t[:, 0:2], 1.0 / 256, None, ALU.mult), reads=[s_], writes=[s_])
                            S.op("dve", lambda: nc.vector.tensor_tensor(out=s_.t[:, 2:3], in0=s_.t[:, 0:1], in1=s_.t[:, 0:1], op=ALU.mult), reads=[s_], writes=[s_])
                            S.op("dve", lambda: nc.vector.tensor_tensor(out=s_.t[:, 2:3], in0=s_.t[:, 1:2], in1=s_.t[:, 2:3], op=ALU.subtract), reads=[s_], writes=[s_])
                            S.op("act", lambda: nc.scalar.activation(out=s_.t[:, 2:3], in_=s_.t[:, 2:3], func=AF.Sqrt, bias=epsc.t[:], scale=1.0), reads=[s_, epsc], writes=[s_])
                            S.op("dve", lambda: nc.vector.reciprocal(s_.t[:, 2:3], s_.t[:, 2:3]), reads=[s_], writes=[s_])
                            S.op("dve", lambda: nc.vector.tensor_scalar(o_.t[:], o_.t[:], s_.t[:, 0:1], s_.t[:, 2:3], ALU.subtract, ALU.mult), reads=[o_, s_], writes=[o_])
                            S.op("pool", lambda: nc.gpsimd.tensor_tensor(out=g_.t[:], in0=g_.t[:], in1=gn.t[:], op=ALU.mult), reads=[g_, gn], writes=[g_])
                            ob = ot[k2]
                            S.op("pool", lambda: nc.gpsimd.tensor_tensor(out=ob.t[:], in0=o_.t[:], in1=g_.t[:], op=ALU.mult), reads=[o_, g_], writes=[ob])
                            S.dma("act", out_d[ti * 128:(ti + 1) * 128, :], ob.t[:], reads=[ob], writes=[], out=True)
                S.barrier(Sf + Sb + pST + pOo + pDS + sTm + qd + kd + of_ + ofl + gl_ + [junk] + st + ot)
            S.barrier([qT, kT, kK, vA])
    S.finish()
    return nc


def prep_MO(h, x, xc, w_in, log_decay, gn_g, n0, sc1, sh1):
    n_b = x.shape[0]
    n_lat = x.shape[1] // 128
    x_all = np.concatenate([np.concatenate([x[b], xc[b]], 0) for b in range(n_b)], 0)
    w = np.concatenate([w_in[:, h * 128:(h + 1) * 128], w_in[:, 1024 + h * 128:1024 + (h + 1) * 128],
                        w_in[:, 2048 + h * 256:2048 + (h + 1) * 256], w_in[:, 4096 + h * 256:4096 + (h + 1) * 256]], 1)
    t = np.arange(n_lat * 128)
    row = (t // 64).astype(np.float32)
    col = (t % 64).astype(np.float32)
    inv = (10000.0 ** (-np.arange(32, dtype=np.float32) / 32)).astype(np.float32)
    ang = np.concatenate([row[:, None] * inv, col[:, None] * inv], -1).astype(np.float32)
    cos, sin = np.cos(ang).astype(np.float32), np.sin(ang).astype(np.float32)
    rope = np.concatenate([cos, cos, sin, sin], -1).reshape(n_lat, 128, 256)
    pos = np.arange(128, dtype=np.float32)
    kq = pos[None, :] - pos[:, None]
    cE = np.stack([np.where(kq >= 0, kq, 0), np.where(kq <= 0, -kq, 0)]).astype(np.float32)
    cM = np.stack([(kq >= 0), (kq <= 0)]).astype(np.float32)
    cQ = np.stack([np.broadcast_to(pos + 1, (128, 128)), np.broadcast_to(128 - pos, (128, 128))]).astype(np.float32)
    cK = np.stack([127 - pos, pos], 1).astype(np.float32)
    ins = {
        "x": np.ascontiguousarray(x_all, np.float32), "w": np.ascontiguousarray(w),
        "cols": np.ascontiguousarray(np.stack([_fm(n0)] + [_fm(v) for v in sc1] + [_fm(v) for v in sh1], 1)),
        "rope": np.ascontiguousarray(rope), "lg": np.ascontiguousarray(np.broadcast_to(log_decay[:, h].astype(np.float32), (128, 2))),
        "cE": np.ascontiguousarray(cE), "cM": np.ascontiguousarray(cM), "cQ": np.ascontiguousarray(cQ), "cK": np.ascontiguousarray(cK),
        "gn": np.ascontiguousarray(np.broadcast_to(gn_g[h * 256:(h + 1) * 256].astype(np.float32), (128, 256))),
        "ident": np.eye(128, dtype=np.float32),
    }
    return ins, None


def build_ADA():
    nc = bass.Bass("TRN2", target_bir_lowering=False)
    dr = lambda name, shape, dt=F32, kind="ExternalInput": nc.dram_tensor(name, list(shape), dt, kind=kind).ap()
    c_d = dr("cT", [128, 8, 3])
    w_d = dr("w", [4, D, 768])
    b_d = dr("b", [128, 4, 6])
    o_d = dr("out", [128, 4, 6, 3], kind="ExternalOutput")
    S = Sched(nc)
    cT = S.sb("cT", [128, 8, 3], F32)
    sc = S.sb("sc", [128, 8, 3], BF16)
    bb = S.sb("bb", [128, 4, 6], F32)
    ot = S.sb("ot", [128, 4, 6, 3], F32)
    S.dma("sp", cT.t[:], c_d, writes=[cT])
    S.dma("sp", bb.t[:], b_d, writes=[bb])
    S.op("act", lambda: nc.scalar.activation(out=sc.t[:], in_=cT.t[:], func=AF.Silu), reads=[cT], writes=[sc])
    stg = [S.sb(f"stg{i}", [128, 8, 768], F32) for i in range(2)]
    wl = [S.sb(f"wl{i}", [128, 8, 768], BF16) for i in range(2)]
    ps = [S.ps(f"ps{i}", [128, 4]) for i in range(2)]
    k = 0
    for l in range(4):
        st, w = stg[l % 2], wl[l % 2]
        S.dma("sp" if l % 2 == 0 else "act", st.t[:], w_d[l].rearrange("(c p) n -> p c n", p=128), writes=[st])
        S.op("dve" if l % 2 == 0 else "pool", lambda: (nc.vector if l % 2 == 0 else nc.gpsimd).tensor_copy(w.t[:], st.t[:]), reads=[st], writes=[w])
        for j in range(6):
            p = ps[k % 2]
            k += 1
            for c in range(8):
                S.op("pe", lambda: nc.tensor.matmul(p.t[:, 0:3], lhsT=w.t[:, c, j * 128:(j + 1) * 128], rhs=sc.t[:, c, :], start=(c == 0), stop=(c == 7)), reads=[w, sc], writes=[p])
            S.op("dve", lambda: nc.vector.tensor_scalar(ot.t[:, l, j, :], p.t[:, 0:3], bb.t[:, l, j:j + 1], None, ALU.add), reads=[p, bb], writes=[ot])
    S.dma("sp", o_d, ot.t[:], reads=[ot], writes=[], out=True)
    S.finish()
    return nc


_PROGS = {}
_DBG = {}


def _prog(key, fn):
    if key not in _PROGS:
        _PROGS[key] = fn()
    return _PROGS[key]


def _run(nc, in_maps):
    res = run_bass_kernel_spmd(nc, in_maps, core_ids=list(range(len(in_maps))))
    return res.results


def kernel_unfused(x, c, ctx, c_ctx, ada_w, ada_b, norm_g, hyb_w_in, na_rpb, sgu_w, sgu_b, hyb_w_out,
           ret_w_in, ret_log_decay, ret_gn_g, ret_w_out, ffn_w_in, ffn_conv_w, ffn_conv_b, ffn_w_out):
    import ml_dtypes
    f32 = lambda a: np.ascontiguousarray(np.asarray(a, dtype=np.float32))
    x, c, ctx, c_ctx, ada_w, ada_b, norm_g = map(f32, (x, c, ctx, c_ctx, ada_w, ada_b, norm_g))
    hyb_w_in, na_rpb, sgu_w, sgu_b, hyb_w_out = map(f32, (hyb_w_in, na_rpb, sgu_w, sgu_b, hyb_w_out))
    ret_w_in, ret_log_decay, ret_gn_g, ret_w_out = map(f32, (ret_w_in, ret_log_decay, ret_gn_g, ret_w_out))
    ffn_w_in, ffn_conv_w, ffn_conv_b, ffn_w_out = map(f32, (ffn_w_in, ffn_conv_w, ffn_conv_b, ffn_w_out))
    B, T, _ = x.shape
    L = ctx.shape[1]
    depth = ada_w.shape[0]
    ident = np.eye(128, dtype=np.float32)
    cvec = np.stack([c[0], c[1], c_ctx], 0)
    cT = np.ascontiguousarray(cvec.T.reshape(8, 128, 3).transpose(1, 0, 2))
    ada_maps = []
    for j in range(8):
        cs = slice(j * 768, (j + 1) * 768)
        ada_maps.append({"cT": cT, "w": np.ascontiguousarray(ada_w[:, :, cs]),
                         "b": np.ascontiguousarray(ada_b[:, cs].reshape(4, 6, 128).transpose(2, 0, 1))})
    res = _run(_prog("ada", build_ADA), ada_maps)
    mod = np.zeros((depth, 3, 6 * D), np.float32)
    for j in range(8):
        o = res[j]["out"]
        mod[:, :, j * 768:(j + 1) * 768] = o.transpose(1, 3, 2, 0).reshape(4, 3, 768)
    xcur = x.copy()
    ccur = ctx.copy()
    n_lat = T // 128
    n_ctx = L // 128
    TPB = n_lat + n_ctx
    for i in range(depth):
        j2 = i // 2
        sh1, sc1, g1, sh2, sc2, g2 = [mod[i][:, k * D:(k + 1) * D] for k in range(6)]
        if i % 2 == 0:
            maps = [prep_ME(h, xcur, ccur, hyb_w_in[j2], na_rpb[j2], sgu_w[j2], sgu_b[j2], norm_g[i, 0], sc1, sh1)[0] for h in range(8)]
            res = _run(_prog("me", lambda: build_ME(n_lat, n_ctx, B)), maps)
            KO = D
            o_full = np.zeros((B * TPB * 128, KO), ml_dtypes.bfloat16)
            for h in range(8):
                o = res[h]["out"]
                o_full[:, h * 64:(h + 1) * 64] = o[:, 0:64]
                o_full[:, 512 + h * 64:512 + (h + 1) * 64] = o[:, 64:128]
            w_o = hyb_w_out[j2]
        else:
            maps = [prep_MO(h, xcur, ccur, ret_w_in[j2], ret_log_decay[j2], ret_gn_g[j2], norm_g[i, 0], sc1, sh1)[0] for h in range(8)]
            res = _run(_prog("mo", lambda: build_MO(n_lat, n_ctx, B)), maps)
            KO = 2 * D
            o_full = np.zeros((B * TPB * 128, KO), ml_dtypes.bfloat16)
            for h in range(8):
                o_full[:, h * 256:(h + 1) * 256] = res[h]["out"]
            w_o = ret_w_out[j2]
        del maps, res
        o_full = o_full.reshape(B, TPB * 128, KO)
        passes = [(8, n_ctx), (8, 0)]
        bc = lambda v: np.ascontiguousarray(np.broadcast_to(v, (128, D)))
        convw = np.ascontiguousarray(ffn_conv_w[i].reshape(3, 2 * NFC, 128).transpose(2, 1, 0))
        convb = np.ascontiguousarray(ffn_conv_b[i].reshape(2 * NFC, 128).T)
        maps = []
        QT = T // 4
        for core in range(8):
            b, q = divmod(core, 4)
            rows = np.stack([bc(norm_g[i, 1]), bc(norm_g[i, 3]), bc(g1[b]), bc(g1[2]), bc(g2[b]), bc(g2[2])])
            cols = np.ascontiguousarray(np.stack([_fm(norm_g[i, 2]), _fm(sc2[b]), _fm(sc2[2]), _fm(sh2[b]), _fm(sh2[2])], 1))
            m = {"w_o": w_o, "w_in": ffn_w_in[i], "w_out": ffn_w_out[i], "convw": convw, "convb": convb,
                 "rows": rows, "cols": cols, "ident": ident}
            hm = np.zeros((128, 4), np.float32)
            for p in range(2):
                m0 = q * QT + p * (QT // 2)
                idx_main = np.arange(m0, m0 + QT // 2)
                hl, hr = m0 - 1, m0 + QT // 2
                hm[:, 2 * p] = 1.0 if hl >= 0 else 0.0
                hm[:, 2 * p + 1] = 1.0 if hr < T else 0.0
                idx = np.concatenate([idx_main, [max(hl, 0)], [min(hr, T - 1)]])
                o_rows = o_full[b, idx]
                x_rows = xcur[b, idx]
                if p == 0:
                    o_rows = np.concatenate([o_rows, o_full[b, T:T + L]], 0)
                    x_rows = np.concatenate([x_rows, ccur[b]], 0)
                m[f"oT{p}"] = np.ascontiguousarray(o_rows.T)
                m[f"x{p}"] = np.ascontiguousarray(x_rows)
            m["hmask"] = hm
            maps.append(m)
        res = _run(_prog(("f", KO), lambda: build_F(KO, passes)), maps)
        xn = np.empty_like(xcur)
        cn = np.empty_like(ccur)
        for core in range(8):
            b, q = divmod(core, 4)
            o0, o1 = res[core]["xo0"], res[core]["xo1"]
            xn[b, q * QT:q * QT + QT // 2] = o0[:QT // 2]
            xn[b, q * QT + QT // 2:(q + 1) * QT] = o1
            if q == 0:
                cn[b] = o0[QT // 2:]
        xcur, ccur = xn, cn
        if _DBG.get("stash") is not None:
            _DBG["stash"].append((xcur.copy(), ccur.copy()))
        del maps, res
    return xcur


NLAT, NCTX, NB = 64, 2, 2
TPB_ = NLAT + NCTX
RB = 2048 + 256
NSH = 1282 + 1026
PASS_BASE = (0, 1282)
HALO_BASE = (1024, 2306)
CTX_BASE = 1026


AGC = 256
NAG = RB // AGC


def xall_rn(r, n):
    return ((n // AGC) * 8 + r) * AGC + n % AGC


def xall_row(b, tl):
    if tl < NLAT:
        return xall_rn(b * 4 + tl // 16, (tl % 16) * 128)
    return xall_rn(b * 4, 2048 + (tl - NLAT) * 128)


class Fz:
    pass


def mod_col_loads(S, nc, Z, dst, dst_idx, l, k, r):
    for c in range(8):
        gc = k * 8 + c
        rank, loc0 = gc // 6, (gc % 6) * 128
        src = Z.modall_d[rank * 12 + l * 3 + r, loc0:loc0 + 128].rearrange("(p o) -> p o", o=1)
        S.dma("sp" if c % 2 == 0 else "act", dst.t[:, dst_idx, c:c + 1], src, reads=[Z.modall_tok], writes=[dst], indep=True)


def mod_row_bcast(S, nc, Z, dst, dst_idx, l, k, r):
    c0 = k * 1024
    qi = 0
    while c0 < (k + 1) * 1024:
        rank = c0 // 768
        c1 = min((rank + 1) * 768, (k + 1) * 1024)
        src = Z.modall_d[rank * 12 + l * 3 + r:rank * 12 + l * 3 + r + 1, c0 - rank * 768:c1 - rank * 768].partition_broadcast(128)
        S.dma("sp" if qi % 2 == 0 else "act", dst.t[:, dst_idx, c0 - k * 1024:c1 - k * 1024], src, reads=[Z.modall_tok], writes=[dst], indep=True)
        qi += 1
        c0 = c1


def emit_ADA(S, nc, Z):
    dr = Z.dr
    c_d = dr("cT", [128, 8, 3])
    w_d = dr("ada_w", [4, D, 768])
    b_d = dr("ada_b", [128, 4, 6])
    with S.scope():
        cT = S.sb("cT", [128, 8, 3], F32)
        sc = S.sb("sc", [128, 8, 3], BF16)
        bb = S.sb("bb", [128, 4, 6], F32)
        ot = S.sb("ot", [128, 4, 6, 3], F32)
        S.dma("sp", cT.t[:], c_d, writes=[cT])
        S.dma("sp", bb.t[:], b_d, writes=[bb])
        S.op("act", lambda: nc.scalar.activation(out=sc.t[:], in_=cT.t[:], func=AF.Silu), reads=[cT], writes=[sc])
        stg = [S.sb(f"stg{i}", [128, 8, 768], F32) for i in range(2)]
        wl = [S.sb(f"wl{i}", [128, 8, 768], BF16) for i in range(2)]
        ps = [S.ps(f"ps{i}", [128, 4]) for i in range(2)]
        k = 0
        for l in range(4):
            st, w = stg[l % 2], wl[l % 2]
            wlv = w_d[l].rearrange("(c p) n -> p c n", p=128)
            for c8 in range(8):
                S.dma("sp" if c8 % 2 == 0 else "act", st.t[:, c8, :], wlv[:, c8, :], writes=[st], indep=True)
            S.op("dve" if l % 2 == 0 else "pool", lambda: (nc.vector if l % 2 == 0 else nc.gpsimd).tensor_copy(w.t[:], st.t[:]), reads=[st], writes=[w])
            for j in range(6):
                p = ps[k % 2]
                k += 1
                for c in range(8):
                    S.op("pe", lambda: nc.tensor.matmul(p.t[:, 0:3], lhsT=w.t[:, c, j * 128:(j + 1) * 128], rhs=sc.t[:, c, :], start=(c == 0), stop=(c == 7)), reads=[w, sc], writes=[p])
                S.op("dve", lambda: nc.vector.tensor_scalar(ot.t[:, l, j, :], p.t[:, 0:3], bb.t[:, l, j:j + 1], None, ALU.add), reads=[p, bb], writes=[ot])
        q = 0
        for l in range(4):
            for j in range(6):
                for r in range(3):
                    dst = Z.modloc_d[l * 3 + r, j * 128:(j + 1) * 128].rearrange("(p o) -> p o", o=1)
                    S.dma("sp" if q % 2 == 0 else "act", dst, ot.t[:, l, j, r:r + 1], reads=[ot], writes=[Z.modloc_tok], indep=True)
                    q += 1
        S.coll("AllGather", [Z.modloc_d], [Z.modall_d], reads=[Z.modloc_tok], writes=[Z.modall_tok])
        S.barrier([cT, sc, bb, ot, Z.modloc_tok, Z.modall_tok] + stg + wl + ps)


def emit_gain_shift(S, nc, Z, i, which, k_sc, k_sh, rows):
    R = len(rows)
    cols = S.sb("gcols", [128, 1 + R, 8], F32)
    gain = S.sb("gain", [128, R, 8], F32)
    shift = S.sb("shift", [128, R, 8], F32)
    S.dma("sp", cols.t[:, 0, :], Z.ngfm_d[:, i, which, :], writes=[cols], indep=True)
    for ri, r in enumerate(rows):
        mod_col_loads(S, nc, Z, cols, 1 + ri, i, k_sc, r)
        mod_col_loads(S, nc, Z, shift, ri, i, k_sh, r)
    for ri in range(R):
        S.op("dve", lambda: nc.vector.scalar_tensor_tensor(out=gain.t[:, ri, :], in0=cols.t[:, 1 + ri, :], scalar=1.0, in1=cols.t[:, 0, :], op0=ALU.add, op1=ALU.mult),
             reads=[cols], writes=[gain])
    return gain, shift, cols


def emit_stageC(S, nc, Z, oA, tiles, C, woh):
    KC = C // 128
    with S.scope():
        pOT = [S.ps(f"pOT{i}", [128, KC, 128], BF16) for i in range(2)]
        pY = [S.ps(f"pYc{i}", [128, D]) for i in range(2)]
        oTt = [S.sb(f"oTt{i}", [128, KC, 128], BF16) for i in range(2)]
        ysb = [S.sb(f"ysb{i}", [128, D], F32) for i in range(3)]
        yin3 = Z.yin_d.rearrange("(j n) d -> j n d", j=8)
        for n, (b, tl, oidx) in enumerate(tiles):
            pt, py, ot_, ys = pOT[n % 2], pY[n % 2], oTt[n % 2], ysb[n % 3]
            for kc in range(KC):
                S.op("pe", lambda: nc.tensor.transpose(pt.t[:, kc, :], oA.t[:, oidx, kc * 128:(kc + 1) * 128], Z.ident.t[:]), reads=[oA, Z.ident], writes=[pt])
            S.op("dve", lambda: nc.vector.tensor_copy(ot_.t[:], pt.t[:]), reads=[pt], writes=[ot_])
            for half in range(2):
                for kc in range(KC):
                    S.op("pe", lambda: nc.tensor.matmul(py.t[:, half * 512:(half + 1) * 512], lhsT=ot_.t[:, kc, :], rhs=woh.t[:, kc, half * 512:(half + 1) * 512],
                                                        start=(kc == 0), stop=(kc == KC - 1)), reads=[ot_, woh], writes=[py])
            if n % 2 == 0:
                S.op("act", lambda: nc.scalar.copy(ys.t[:], py.t[:]), reads=[py], writes=[ys])
            else:
                S.op("dve", lambda: nc.vector.tensor_copy(ys.t[:], py.t[:]), reads=[py], writes=[ys])
            wr = lambda q, dst, src: S.dma(q, dst, src, reads=[ys], writes=[Z.yin_tok], indep=True)
            if tl < NLAT:
                blk, m = divmod(tl * 128, 1024)
                j, p = b * 4 + blk // 2, blk % 2
                wr("sp" if n % 2 == 0 else "act", yin3[j, PASS_BASE[p] + m:PASS_BASE[p] + m + 128, :], ys.t[:])
                if m == 0 and blk > 0:
                    jp, pp = b * 4 + (blk - 1) // 2, (blk - 1) % 2
                    wr("act", yin3[jp, HALO_BASE[pp] + 1:HALO_BASE[pp] + 2, :], ys.t[0:1, :])
                if m == 1024 - 128 and blk < 7:
                    jn, pn = b * 4 + (blk + 1) // 2, (blk + 1) % 2
                    wr("sp", yin3[jn, HALO_BASE[pn]:HALO_BASE[pn] + 1, :], ys.t[127:128, :])
            else:
                c = tl - NLAT
                for q4 in range(4):
                    wr("sp" if q4 % 2 == 0 else "act", yin3[b * 4 + q4, CTX_BASE + c * 128:CTX_BASE + (c + 1) * 128, :], ys.t[:])
        S.barrier(pOT + pY + oTt + ysb)


def emit_ME_fused(S, nc, Z, i):
    j2 = i // 2
    n_lat, n_ctx, n_b = NLAT, NCTX, NB
    TPB = TPB_
    NT = n_b * TPB
    x_src = Z.xin_d if i == 0 else Z.xall_d
    with S.scope():
        gain, shift, _ = emit_gain_shift(S, nc, Z, i, 0, 1, 0, [0, 1, 2])
        wqk = S.sb("wqk", [128, 8, 128], BF16)
        wgu = S.sb("wgu", [128, 8, 576], BF16)
        wv = S.sb("wv", [128, 8, 64], BF16)
        wsT = S.sb("wsT", [128, 128], BF16)
        woh = S.sb("woh", [128, 1, D], BF16)
        with S.scope():
            stage = S.sb("stage", [128, 8, 576], F32)
            for (wb, wd, n) in ((wqk, Z.me_wqk_d[j2], 128), (wgu, Z.me_wgu_d[j2], 576), (wv, Z.me_wv_d[j2], 64)):
                wdv = wd.rearrange("(c p) n -> p c n", p=128)
                stg_ = S.sb(f"stg{n}", [128, 8, n], F32)
                for c8 in range(8):
                    S.dma("sp" if c8 % 2 == 0 else "act", stg_.t[:, c8, :], wdv[:, c8, :], writes=[stg_], indep=True)
                S.op("dve", lambda: nc.vector.tensor_copy(wb.t[:], stg_.t[:]), reads=[stg_], writes=[wb])
            S.dma("sp", stage.t[:, 0, :128], Z.me_wsT_d[j2], writes=[stage])
            S.op("dve", lambda: nc.vector.tensor_copy(wsT.t[:], stage.t[:, 0, :128]), reads=[stage], writes=[wsT])
            for h2 in range(2):
                S.dma("sp", stage.t[:, h2, :512], Z.me_woh_d[j2][:, h2 * 512:(h2 + 1) * 512], writes=[stage])
            S.op("dve", lambda: nc.vector.tensor_copy(woh.t[:, 0, :].rearrange("p (a n) -> p a n", a=2), stage.t[:, 0:2, :512]), reads=[stage], writes=[woh])
        bs = S.sb("bs", [128, 1], F32)
        S.dma("sp", bs.t[:], Z.me_bs_d[j2], writes=[bs])
        biasT = S.sb("biasT", [128, 5, 576], F32)
        for k in range(5):
            S.dma("act", biasT.t[:, k, :], Z.me_bias_d[j2, k], writes=[biasT], indep=True)
        qT = S.sb("qT", [64, NT * 128], BF16)
        kT = S.sb("kT", [64, NT * 128], BF16)
        vA = S.sb("vA", [128, NT, 64], BF16)
        oA = S.sb("oA", [128, NT, 128], BF16)
        ident, epsc = Z.ident, Z.epsc
        with S.scope():
            N = NormCtx(S, ident, epsc)
            xts = [S.sb(f"xt{k}", [128, D], F32) for k in range(2)]
            hTs = [S.sb(f"hT{k}", [128, 8, 128], BF16) for k in range(2)]
            pQK = S.ps("pQK", [64, 2, 128])
            pGU = S.ps("pGU", [128, 1024])
            pSG = S.ps("pSG", [128, 128])
            xg = S.sb("xg", [128, 576], F32)
            t1 = S.sb("t1", [128, 576], F32)
            gl = S.sb("gl", [128, 576], F32)
            junk = S.sb("junk", [128, 512], F32)
            st = S.sb("st", [128, 4], F32)
            vn = S.sb("vn", [128, 64], BF16)
            for ti in range(NT):
                b, tl = divmod(ti, TPB)
                r = b if tl < n_lat else 2
                xt = xts[ti % 2]
                hT = hTs[ti % 2]
                r0 = xall_row(b, tl)
                S.dma("sp" if ti % 2 == 0 else "act", xt.t[:], x_src[r0:r0 + 128, :], reads=[Z.xall_tok], writes=[xt])
                emit_norm_T(S, nc, N, xt, 128, gain, shift, r, hT, 0)
                for w in range(2):
                    for c in range(8):
                        S.op("pe", lambda: nc.tensor.matmul(pQK.t[:, w, :], lhsT=wqk.t[:, c, w * 64:(w + 1) * 64], rhs=hT.t[:, c, :], start=(c == 0), stop=(c == 7)), reads=[wqk, hT], writes=[pQK])
                S.op("act", lambda: nc.scalar.activation(out=qT.t[:, ti * 128:(ti + 1) * 128], in_=pQK.t[:, 0, :], func=AF.Copy, scale=0.125), reads=[pQK], writes=[qT])
                S.op("act", lambda: nc.scalar.copy(kT.t[:, ti * 128:(ti + 1) * 128], pQK.t[:, 1, :]), reads=[pQK], writes=[kT])
                for (o0, n, wb, wo0) in ((0, 512, wgu, 0), (512, 64, wgu, 512), (576, 64, wv, 0)):
                    for c in range(8):
                        S.op("pe", lambda: nc.tensor.matmul(pGU.t[:, o0:o0 + n], lhsT=hT.t[:, c, :], rhs=wb.t[:, c, wo0:wo0 + n], start=(c == 0), stop=(c == 7)), reads=[wb, hT], writes=[pGU])
                S.op("act", lambda: nc.scalar.copy(vA.t[:, ti, :], pGU.t[:, 576:640]), reads=[pGU], writes=[vA])
                S.op("act", lambda: nc.scalar.copy(xg.t[:], pGU.t[:, 0:576]), reads=[pGU], writes=[xg])
                S.op("dve", lambda: nc.vector.tensor_tensor(out=t1.t[:], in0=xg.t[:], in1=xg.t[:], op=ALU.mult), reads=[xg], writes=[t1])
                S.op("dve", lambda: nc.vector.tensor_scalar(t1.t[:], t1.t[:], 0.044715, 1.0, ALU.mult, ALU.add), reads=[t1], writes=[t1])
                S.op("pool", lambda: nc.gpsimd.tensor_tensor(out=t1.t[:], in0=t1.t[:], in1=xg.t[:], op=ALU.mult), reads=[t1, xg], writes=[t1])
                S.op("act", lambda: nc.scalar.activation(out=t1.t[:], in_=t1.t[:], func=AF.Sigmoid, scale=GELU_C), reads=[t1], writes=[t1])
                S.op("pool", lambda: nc.gpsimd.tensor_tensor(out=gl.t[:], in0=t1.t[:], in1=xg.t[:], op=ALU.mult), reads=[t1, xg], writes=[gl])
                S.op("act", lambda: nc.scalar.activation(out=junk.t[:], in_=gl.t[:, 0:512], func=AF.Identity, accum_out=st.t[:, 0:1]), reads=[gl], writes=[junk, st])
                S.op("act", lambda: nc.scalar.activation(out=junk.t[:], in_=gl.t[:, 0:512], func=AF.Square, accum_out=st.t[:, 1:2]), reads=[gl], writes=[junk, st])
                S.op("dve", lambda: nc.vector.tensor_scalar(st.t[:, 0:2], st.t[:, 0:2], 1.0 / 512, None, ALU.mult), reads=[st], writes=[st])
                S.op("dve", lambda: nc.vector.tensor_tensor(out=st.t[:, 2:3], in0=st.t[:, 0:1], in1=st.t[:, 0:1], op=ALU.mult), reads=[st], writes=[st])
                S.op("dve", lambda: nc.vector.tensor_tensor(out=st.t[:, 2:3], in0=st.t[:, 1:2], in1=st.t[:, 2:3], op=ALU.subtract), reads=[st], writes=[st])
                S.op("act", lambda: nc.scalar.activation(out=st.t[:, 2:3], in_=st.t[:, 2:3], func=AF.Sqrt, bias=epsc.t[:], scale=1.0), reads=[st, epsc], writes=[st])
                S.op("dve", lambda: nc.vector.reciprocal(st.t[:, 2:3], st.t[:, 2:3]), reads=[st], writes=[st])
                S.op("dve", lambda: nc.vector.tensor_scalar(vn.t[:], gl.t[:, 0:64], st.t[:, 0:1], st.t[:, 2:3], ALU.subtract, ALU.mult), reads=[gl, st], writes=[vn])
                S.op("pe", lambda: nc.tensor.matmul(pSG.t[:, 0:64], lhsT=wsT.t[:], rhs=vn.t[:], start=True, stop=True), reads=[wsT, vn], writes=[pSG])
                S.op("dve", lambda: nc.vector.scalar_tensor_tensor(out=oA.t[:, ti, 64:128], in0=pSG.t[:, 0:64], scalar=bs.t[:, 0:1], in1=gl.t[:, 512:576], op0=ALU.add, op1=ALU.mult),
                     reads=[pSG, bs, gl], writes=[oA])
        with S.scope():
            pSs = [S.ps(f"pS{k}", [128, 1024]) for k in range(2)]
            pPTs = [S.ps(f"pPT{k}", [128, 7, 128], BF16) for k in range(2)]
            pOs = [S.ps(f"pO{k}", [128, 64]) for k in range(2)]
            Ts = [S.sb(f"T{k}", [128, 832], F32) for k in range(2)]
            Ps = [S.sb(f"P{k}", [128, 832], BF16) for k in range(2)]
            PTs = [S.sb(f"PT{k}", [128, 7, 128], BF16) for k in range(2)]
            mxs = [S.sb(f"mx{k}", [128, 2], F32) for k in range(2)]
            step = 0
            for b in range(n_b):
                base = b * TPB
                ctx_tiles = [base + n_lat + j for j in range(n_ctx)]
                for tl in range(TPB):
                    ti = base + tl
                    pS, pPT, pO, T, P, PT, mx = [x[step % 2] for x in (pSs, pPTs, pOs, Ts, Ps, PTs, mxs)]
                    step += 1
                    if tl < n_lat:
                        tw = min(max(tl - 2, 0), n_lat - 4)
                        full = [base + tw + k for k in range(4)]
                        extra = (base + tw + 4) if (2 <= tl <= n_lat - 3) else None
                        cls = 0 if tl == 0 else 1 if tl == 1 else 3 if tl == n_lat - 2 else 4 if tl == n_lat - 1 else 2
                        nnb = 512 + (64 if extra is not None else 0)
                    else:
                        full, extra, cls, nnb = [], None, None, 0
                    ncx = n_ctx * 128
                    W = nnb + ncx
                    qs = qT.t[:, ti * 128:(ti + 1) * 128]
                    if full:
                        S.op("pe", lambda: nc.tensor.matmul(pS.t[:, 0:512], lhsT=qs, rhs=kT.t[:, full[0] * 128:(full[0] + 4) * 128], start=True, stop=True), reads=[qT, kT], writes=[pS])
                        if extra is not None:
                            S.op("pe", lambda: nc.tensor.matmul(pS.t[:, 512:576], lhsT=qs, rhs=kT.t[:, extra * 128:extra * 128 + 64], start=True, stop=True), reads=[qT, kT], writes=[pS])
                    c0 = 512 + (64 if extra is not None else 0) if full else 0
                    S.op("pe", lambda: nc.tensor.matmul(pS.t[:, c0:c0 + ncx], lhsT=qs, rhs=kT.t[:, ctx_tiles[0] * 128:ctx_tiles[0] * 128 + ncx], start=True, stop=True), reads=[qT, kT], writes=[pS])
                    if full:
                        S.op("dve", lambda: nc.vector.tensor_tensor(out=T.t[:, 0:nnb], in0=pS.t[:, 0:nnb], in1=biasT.t[:, cls, 0:nnb], op=ALU.add), reads=[pS, biasT], writes=[T])
                    S.op("act", lambda: nc.scalar.copy(T.t[:, nnb:W], pS.t[:, c0:c0 + ncx]), reads=[pS], writes=[T])
                    S.op("dve", lambda: nc.vector.tensor_reduce(out=mx.t[:, 0:1], in_=T.t[:, 0:W], axis=AX.X, op=ALU.max), reads=[T], writes=[mx])
                    S.op("dve", lambda: nc.vector.tensor_scalar(mx.t[:, 0:1], mx.t[:, 0:1], -1.0, None, ALU.mult), reads=[mx], writes=[mx])
                    S.op("act", lambda: nc.scalar.activation(out=P.t[:, 0:W], in_=T.t[:, 0:W], func=AF.Exp, bias=mx.t[:, 0:1], scale=1.0, accum_out=mx.t[:, 1:2]), reads=[T, mx], writes=[P, mx])
                    S.op("dve", lambda: nc.vector.reciprocal(mx.t[:, 1:2], mx.t[:, 1:2]), reads=[mx], writes=[mx])
                    chunks = [(k * 128, 128, full[k]) for k in range(len(full))]
                    if extra is not None:
                        chunks.append((512, 64, extra))
                    chunks += [(nnb + j * 128, 128, ctx_tiles[j]) for j in range(n_ctx)]
                    for ci, (pc0, n, vt) in enumerate(chunks):
                        S.op("pe", lambda: nc.tensor.transpose(pPT.t[:n, ci, :], P.t[:, pc0:pc0 + n], ident.t[:]), reads=[P, ident], writes=[pPT])
                    ncf = len(chunks)
                    S.op("dve", lambda: nc.vector.tensor_copy(PT.t[:, 0:ncf, :], pPT.t[:, 0:ncf, :]), reads=[pPT], writes=[PT])
                    for ci, (pc0, n, vt) in enumerate(chunks):
                        S.op("pe", lambda: nc.tensor.matmul(pO.t[:, :], lhsT=PT.t[:n, ci, :], rhs=vA.t[:n, vt, :], start=(ci == 0), stop=(ci == ncf - 1)), reads=[PT, vA], writes=[pO])
                    S.op("act", lambda: nc.scalar.activation(out=oA.t[:, ti, 0:64], in_=pO.t[:, :], func=AF.Copy, scale=mx.t[:, 1:2]), reads=[pO, mx], writes=[oA])
        emit_stageC(S, nc, Z, oA, [(ti // TPB, ti % TPB, ti) for ti in range(NT)], 128, woh)


def emit_MO_fused(S, nc, Z, i):
    j2 = i // 2
    n_lat, n_ctx, n_b = NLAT, NCTX, NB
    TPB = TPB_
    x_src = Z.xin_d if i == 0 else Z.xall_d
    ident, epsc = Z.ident, Z.epsc
    gS_d, oF_d, gS_tok, oF_tok = Z.gS_d, Z.oF_d, Z.gS_tok, Z.oF_tok
    with S.scope():
        gain, shift, _ = emit_gain_shift(S, nc, Z, i, 0, 1, 0, [0, 1, 2])
        wb = S.sb("wb", [128, 8, 768], BF16)
        woh = S.sb("woh", [128, 2, D], BF16)
        with S.scope():
            stage = S.sb("stage", [128, 8, 768], F32)
            wv_ = Z.mo_w_d[j2].rearrange("(c p) n -> p c n", p=128)
            for c8 in range(8):
                S.dma("sp" if c8 % 2 == 0 else "act", stage.t[:, c8, :], wv_[:, c8, :], writes=[stage], indep=True)
            S.op("dve", lambda: nc.vector.tensor_copy(wb.t[:], stage.t[:]), reads=[stage], writes=[wb])
            stg2 = S.sb("stg2", [128, 4, 512], F32)
            for k4 in range(4):
                S.dma("sp" if k4 % 2 == 0 else "act", stg2.t[:, k4, :], Z.mo_woh_d[j2][(k4 // 2) * 128:(k4 // 2 + 1) * 128, (k4 % 2) * 512:(k4 % 2 + 1) * 512], writes=[stg2], indep=True)
            S.op("dve", lambda: nc.vector.tensor_copy(woh.t[:].rearrange("p k (a n) -> p (k a) n", a=2), stg2.t[:]), reads=[stg2], writes=[woh])
        if getattr(Z, "dump_wb", None) is not None:
            for c8 in range(8):
                S.dma("sp", Z.dump_wb[:, c8, :], wb.t[:, c8, :], reads=[wb], writes=[], out=True)
            S.dma("act", Z.dump_gain, gain.t[:], reads=[gain], writes=[], out=True)
            S.dma("act", Z.dump_shift, shift.t[:], reads=[shift], writes=[], out=True)
        lg = S.sb("lg", [128, 2], F32)
        S.dma("sp", lg.t[:], Z.mo_lg_d[j2], writes=[lg])
        DT = S.sb("DT", [128, 2, 128], F32)
        DQ = S.sb("DQ", [128, 2, 128], F32)
        DK = S.sb("DK", [128, 2], F32)
        GC = S.sb("GC", [128, 2], F32)
        gn = S.sb("gn", [128, 256], F32)
        S.dma("sp", gn.t[:], Z.mo_gn_d[j2], writes=[gn])
        with S.scope():
            cE = S.sb("cE", [128, 2, 128], F32)
            cM = S.sb("cM", [128, 2, 128], F32)
            cQ = S.sb("cQ", [128, 2, 128], F32)
            cK = S.sb("cK", [128, 2], F32)
            c128 = S.sb("c128", [128, 1], F32)
            S.op("dve", lambda: nc.vector.memset(c128.t[:], 128.0), writes=[c128])
            S.dma("sp", cK.t[:], Z.cK_d, writes=[cK])
            for d in range(2):
                S.dma("sp", cE.t[:, d, :], Z.cE_d[d], writes=[cE], indep=True)
                S.dma("act", cM.t[:, d, :], Z.cM_d[d], writes=[cM], indep=True)
                S.dma("sp", cQ.t[:, d, :], Z.cQ_d[d], writes=[cQ], indep=True)
            for d in range(2):
                S.op("act", lambda: nc.scalar.activation(out=DT.t[:, d, :], in_=cE.t[:, d, :], func=AF.Exp, scale=lg.t[:, d:d + 1]), reads=[cE, lg], writes=[DT])
                S.op("dve", lambda: nc.vector.tensor_tensor(out=DT.t[:, d, :], in0=DT.t[:, d, :], in1=cM.t[:, d, :], op=ALU.mult), reads=[DT, cM], writes=[DT])
                S.op("act", lambda: nc.scalar.activation(out=DQ.t[:, d, :], in_=cQ.t[:, d, :], func=AF.Exp, scale=lg.t[:, d:d + 1]), reads=[cQ, lg], writes=[DQ])
                S.op("act", lambda: nc.scalar.activation(out=DK.t[:, d:d + 1], in_=cK.t[:, d:d + 1], func=AF.Exp, scale=lg.t[:, d:d + 1]), reads=[cK, lg], writes=[DK])
                S.op("act", lambda: nc.scalar.activation(out=GC.t[:, d:d + 1], in_=c128.t[:], func=AF.Exp, scale=lg.t[:, d:d + 1]), reads=[c128, lg], writes=[GC])
        for b in range(n_b):
            base = b * TPB
            with S.scope():
                qT = S.sb(f"qT{b}", [128, TPB * 128], BF16)
                kT = S.sb(f"kT{b}", [128, TPB * 128], BF16)
                kK = S.sb(f"kK{b}", [128, TPB, 128], BF16)
                vA = S.sb(f"vA{b}", [128, TPB, 256], BF16)
                with S.scope():
                    N = NormCtx(S, ident, epsc)
                    xts = [S.sb(f"xt{b}_{k}", [128, D], F32) for k in range(2)]
                    hTs = [S.sb(f"hT{b}_{k}", [128, 8, 128], BF16) for k in range(2)]
                    pIn = S.ps(f"pIn{b}", [128, 1024])
                    pT2 = S.ps(f"pT2{b}", [128, 2, 128], BF16)
                    qk = S.sb(f"qk{b}", [128, 256], F32)
                    rp = [S.sb(f"rp{b}_{k}", [128, 256], F32) for k in range(2)]
                    ta = S.sb(f"ta{b}", [128, 2, 64], F32)
                    tb = S.sb(f"tb{b}", [128, 2, 64], F32)
                    rq = S.sb(f"rq{b}", [128, 256], BF16)
                    gs = [S.sb(f"gs{b}_{k}", [128, 256], F32) for k in range(2)]
                    for tl in range(TPB):
                        ti = base + tl
                        is_lat = tl < n_lat
                        r = b if is_lat else 2
                        xt = xts[tl % 2]
                        hT = hTs[tl % 2]
                        r0 = xall_row(b, tl)
                        S.dma("sp" if tl % 2 == 0 else "act", xt.t[:], x_src[r0:r0 + 128, :], reads=[Z.xall_tok], writes=[xt])
                        emit_norm_T(S, nc, N, xt, 128, gain, shift, r, hT, 0)
                        for (o0, n) in ((0, 512), (512, 256)):
                            for c in range(8):
                                S.op("pe", lambda: nc.tensor.matmul(pIn.t[:, o0:o0 + n], lhsT=hT.t[:, c, :], rhs=wb.t[:, c, o0:o0 + n], start=(c == 0), stop=(c == 7)), reads=[wb, hT], writes=[pIn])
                        S.op("act", lambda: nc.scalar.copy(vA.t[:, tl, :], pIn.t[:, 256:512]), reads=[pIn], writes=[vA])
                        g_ = gs[tl % 2]
                        S.op("act", lambda: nc.scalar.activation(out=g_.t[:], in_=pIn.t[:, 512:768], func=AF.Silu), reads=[pIn], writes=[g_])
                        S.dma("act", gS_d[ti * 128:(ti + 1) * 128, :], g_.t[:], reads=[g_], writes=[gS_tok[ti]])
                        if is_lat:
                            S.op("act", lambda: nc.scalar.copy(qk.t[:, 0:128], pIn.t[:, 0:128]), reads=[pIn], writes=[qk])
                            S.op("act", lambda: nc.scalar.activation(out=qk.t[:, 128:256], in_=pIn.t[:, 128:256], func=AF.Copy, scale=128.0 ** -0.5), reads=[pIn], writes=[qk])
                            rpt = rp[tl % 2]
                            S.dma("sp", rpt.t[:], Z.rope_d[tl], writes=[rpt])
                            q4 = qk.t[:].rearrange("p (a h c) -> p a h c", a=2, h=2)
                            o4 = rq.t[:].rearrange("p (a h c) -> p a h c", a=2, h=2)
                            cs = rpt.t[:, 0:128].rearrange("p (a c) -> p a c", a=2)
                            sn = rpt.t[:, 128:256].rearrange("p (a c) -> p a c", a=2)
                            S.op("dve", lambda: nc.vector.tensor_tensor(out=ta.t[:], in0=q4[:, :, 0, :], in1=cs, op=ALU.mult), reads=[qk, rpt], writes=[ta])
                            S.op("pool", lambda: nc.gpsimd.tensor_tensor(out=tb.t[:], in0=q4[:, :, 1, :], in1=sn, op=ALU.mult), reads=[qk, rpt], writes=[tb])
                            S.op("dve", lambda: nc.vector.tensor_tensor(out=o4[:, :, 0, :], in0=ta.t[:], in1=tb.t[:], op=ALU.subtract), reads=[ta, tb], writes=[rq])
                            S.op("dve", lambda: nc.vector.tensor_tensor(out=ta.t[:], in0=q4[:, :, 0, :], in1=sn, op=ALU.mult), reads=[qk, rpt, rq], writes=[ta])
                            S.op("pool", lambda: nc.gpsimd.tensor_tensor(out=tb.t[:], in0=q4[:, :, 1, :], in1=cs, op=ALU.mult), reads=[qk, rpt, rq], writes=[tb])
                            S.op("dve", lambda: nc.vector.tensor_tensor(out=o4[:, :, 1, :], in0=ta.t[:], in1=tb.t[:], op=ALU.add), reads=[ta, tb], writes=[rq])
                        else:
                            S.op("act", lambda: nc.scalar.copy(rq.t[:, 0:128], pIn.t[:, 0:128]), reads=[pIn], writes=[rq])
                            S.op("act", lambda: nc.scalar.activation(out=rq.t[:, 128:256], in_=pIn.t[:, 128:256], func=AF.Copy, scale=128.0 ** -0.5), reads=[pIn], writes=[rq])
                        S.op("pool", lambda: nc.gpsimd.tensor_copy(kK.t[:, tl, :], rq.t[:, 128:256]), reads=[rq], writes=[kK])
                        for w in range(2):
                            S.op("pe", lambda: nc.tensor.transpose(pT2.t[:, w, :], rq.t[:, w * 128:(w + 1) * 128], ident.t[:]), reads=[rq, ident], writes=[pT2])
                        S.op("dve", lambda: nc.vector.tensor_copy(qT.t[:, tl * 128:(tl + 1) * 128], pT2.t[:, 0, :]), reads=[pT2], writes=[qT])
                        S.op("dve", lambda: nc.vector.tensor_copy(kT.t[:, tl * 128:(tl + 1) * 128], pT2.t[:, 1, :]), reads=[pT2], writes=[kT])
                oA = S.sb(f"oA{b}", [128, TPB, 256], BF16)
                with S.scope():
                    Sf = [S.sb(f"Sf{b}_{d}", [128, 256], F32) for d in range(2)]
                    Sb = [S.sb(f"Sb{b}_{d}", [128, 256], BF16) for d in range(2)]
                    for d in range(2):
                        S.op("dve", lambda: nc.vector.memset(Sf[d].t[:], 0.0), writes=[Sf[d]])
                        S.op("dve", lambda: nc.vector.memset(Sb[d].t[:], 0.0), writes=[Sb[d]])
                    pST = [S.ps(f"pST{b}_{k}", [128, 128]) for k in range(2)]
                    pOo = [S.ps(f"pOo{b}_{k}", [128, 256]) for k in range(2)]
                    pDS = [S.ps(f"pDS{b}_{k}", [128, 256]) for k in range(2)]
                    sTm = [S.sb(f"sTm{b}_{k}", [128, 128], BF16) for k in range(2)]
                    qd = [S.sb(f"qd{b}_{k}", [128, 128], BF16) for k in range(2)]
                    kd = [S.sb(f"kd{b}_{k}", [128, 128], BF16) for k in range(2)]
                    of_ = [S.sb(f"of{b}_{k}", [128, 256], F32) for k in range(2)]
                    ofl = [S.sb(f"ofl{b}_{k}", [128, 256], F32) for k in range(2)]
                    gl_ = [S.sb(f"gl{b}_{k}", [128, 256], F32) for k in range(2)]
                    junk = S.sb(f"junk{b}", [128, 256], F32)
                    st = [S.sb(f"st{b}_{k}", [128, 4], F32) for k in range(2)]
                    order = [list(range(n_lat, TPB)) + list(range(n_lat)),
                             list(range(TPB - 1, n_lat - 1, -1)) + list(range(n_lat - 1, -1, -1))]
                    for ii in range(TPB):
                        for d in range(2):
                            tl = order[d][ii]
                            ti = base + tl
                            k2 = d
                            cs_ = slice(tl * 128, (tl + 1) * 128)
                            S.op("pe", lambda: nc.tensor.matmul(pST[d].t[:], lhsT=kT.t[:, cs_], rhs=qT.t[:, cs_], start=True, stop=True), reads=[kT, qT], writes=[pST[d]])
                            S.op("dve", lambda: nc.vector.tensor_tensor(out=sTm[d].t[:], in0=pST[d].t[:], in1=DT.t[:, d, :], op=ALU.mult), reads=[pST[d], DT], writes=[sTm[d]])
                            S.op("pool", lambda: nc.gpsimd.tensor_tensor(out=qd[d].t[:], in0=qT.t[:, cs_], in1=DQ.t[:, d, :], op=ALU.mult), reads=[qT, DQ], writes=[qd[d]])
                            S.op("pool", lambda: nc.gpsimd.tensor_scalar(kd[d].t[:], kK.t[:, tl, :], DK.t[:, d:d + 1], None, ALU.mult), reads=[kK, DK], writes=[kd[d]])
                            S.op("pe", lambda: nc.tensor.matmul(pOo[d].t[:], lhsT=sTm[d].t[:], rhs=vA.t[:, tl, :], start=True, stop=False), reads=[sTm[d], vA], writes=[pOo[d]])
                            S.op("pe", lambda: nc.tensor.matmul(pOo[d].t[:], lhsT=qd[d].t[:], rhs=Sb[d].t[:], start=False, stop=True), reads=[qd[d], Sb[d]], writes=[pOo[d]])
                            S.op("pe", lambda: nc.tensor.matmul(pDS[d].t[:], lhsT=kd[d].t[:], rhs=vA.t[:, tl, :], start=True, stop=True), reads=[kd[d], vA], writes=[pDS[d]])
                            S.op("dve", lambda: nc.vector.scalar_tensor_tensor(out=Sf[d].t[:], in0=Sf[d].t[:], scalar=GC.t[:, d:d + 1], in1=pDS[d].t[:], op0=ALU.mult, op1=ALU.add),
                                 reads=[Sf[d], GC, pDS[d]], writes=[Sf[d]])
                            S.op("act", lambda: nc.scalar.copy(Sb[d].t[:], Sf[d].t[:]), reads=[Sf[d]], writes=[Sb[d]])
                            i_other = order[1 - d].index(tl)
                            if ii < i_other or (ii == i_other and d == 0):
                                o_ = of_[k2]
                                S.op("act", lambda: nc.scalar.copy(o_.t[:], pOo[d].t[:]), reads=[pOo[d]], writes=[o_])
                                S.dma("sp", oF_d[ti * 128:(ti + 1) * 128, :], o_.t[:], reads=[o_], writes=[oF_tok[ti]])
                            else:
                                o_ = ofl[k2]
                                g_ = gl_[k2]
                                s_ = st[k2]
                                S.dma("sp", o_.t[:], oF_d[ti * 128:(ti + 1) * 128, :], reads=[oF_tok[ti]], writes=[o_])
                                S.dma("act", g_.t[:], gS_d[ti * 128:(ti + 1) * 128, :], reads=[gS_tok[ti]], writes=[g_])
                                S.op("dve", lambda: nc.vector.tensor_tensor(out=o_.t[:], in0=pOo[d].t[:], in1=o_.t[:], op=ALU.add), reads=[pOo[d], o_], writes=[o_])
                                S.op("act", lambda: nc.scalar.activation(out=junk.t[:], in_=o_.t[:], func=AF.Identity, accum_out=s_.t[:, 0:1]), reads=[o_], writes=[junk, s_])
                                S.op("act", lambda: nc.scalar.activation(out=junk.t[:], in_=o_.t[:], func=AF.Square, accum_out=s_.t[:, 1:2]), reads=[o_], writes=[junk, s_])
                                S.op("dve", lambda: nc.vector.tensor_scalar(s_.t[:, 0:2], s_.t[:, 0:2], 1.0 / 256, None, ALU.mult), reads=[s_], writes=[s_])
                                S.op("dve", lambda: nc.vector.tensor_tensor(out=s_.t[:, 2:3], in0=s_.t[:, 0:1], in1=s_.t[:, 0:1], op=ALU.mult), reads=[s_], writes=[s_])
                                S.op("dve", lambda: nc.vector.tensor_tensor(out=s_.t[:, 2:3], in0=s_.t[:, 1:2], in1=s_.t[:, 2:3], op=ALU.subtract), reads=[s_], writes=[s_])
                                S.op("act", lambda: nc.scalar.activation(out=s_.t[:, 2:3], in_=s_.t[:, 2:3], func=AF.Sqrt, bias=epsc.t[:], scale=1.0), reads=[s_, epsc], writes=[s_])
                                S.op("dve", lambda: nc.vector.reciprocal(s_.t[:, 2:3], s_.t[:, 2:3]), reads=[s_], writes=[s_])
                                S.op("dve", lambda: nc.vector.tensor_scalar(o_.t[:], o_.t[:], s_.t[:, 0:1], s_.t[:, 2:3], ALU.subtract, ALU.mult), reads=[o_, s_], writes=[o_])
                                S.op("pool", lambda: nc.gpsimd.tensor_tensor(out=g_.t[:], in0=g_.t[:], in1=gn.t[:], op=ALU.mult), reads=[g_, gn], writes=[g_])
                                if getattr(Z, "var", "") == "o_only":
                                    S.op("pool", lambda: nc.gpsimd.tensor_copy(oA.t[:, tl, :], o_.t[:]), reads=[o_, g_], writes=[oA])
                                elif getattr(Z, "var", "") == "prod_dve":
                                    S.op("dve", lambda: nc.vector.tensor_tensor(out=oA.t[:, tl, :], in0=o_.t[:], in1=g_.t[:], op=ALU.mult), reads=[o_, g_], writes=[oA])
                                elif getattr(Z, "var", "") == "prod_tmp":
                                    S.op("pool", lambda: nc.gpsimd.tensor_tensor(out=junk.t[:], in0=o_.t[:], in1=g_.t[:], op=ALU.mult), reads=[o_, g_], writes=[junk])
                                    S.op("dve", lambda: nc.vector.tensor_copy(oA.t[:, tl, :], junk.t[:]), reads=[junk], writes=[oA])
                                elif getattr(Z, "var", "") == "g_only":
                                    S.op("pool", lambda: nc.gpsimd.tensor_copy(oA.t[:, tl, :], g_.t[:]), reads=[o_, g_], writes=[oA])
                                else:
                                    S.op("pool", lambda: nc.gpsimd.tensor_tensor(out=oA.t[:, tl, :], in0=o_.t[:], in1=g_.t[:], op=ALU.mult), reads=[o_, g_], writes=[oA])
                if getattr(Z, "dump_oA", None) is not None and b == 0:
                    for tq in range(TPB):
                        S.dma("sp", Z.dump_oA[:, tq, :], oA.t[:, tq, :], reads=[oA], writes=[], out=True)
                        if getattr(Z, "dump_v", None) is not None:
                            S.dma("act", Z.dump_v[:, tq, :], vA.t[:, tq, :], reads=[vA], writes=[], out=True)
                            S.dma("act", Z.dump_k[:, tq, :], kK.t[:, tq, :], reads=[kK], writes=[], out=True)
                if getattr(Z, "var", "") != "nostagec":
                    emit_stageC(S, nc, Z, oA, [(b, tl, tl) for tl in range(TPB)], 256, woh)


def emit_F_fused(S, nc, Z, i, last):
    ident, epsc = Z.ident, Z.epsc
    xcur_d = Z.xloc_in_d if i == 0 else Z.xloc_d[(i - 1) % 2]
    xcur_tok = Z.xloc_in_tok if i == 0 else Z.xloc_tok[(i - 1) % 2]
    xnew_d, xnew_tok = Z.xloc_d[i % 2], Z.xloc_tok[i % 2]
    xall_src = Z.xin_d if i == 0 else Z.xall_d
    w_in_d, w_out_d = Z.ffn_w_in_d[i], Z.ffn_w_out_d[i]
    with S.scope():
        gain3, shift3, _ = emit_gain_shift(S, nc, Z, i, 2, 4, 3, [0, 1, 2])
        gain = S.sb("gainF", [128, 2, 8], F32)
        shift = S.sb("shiftF", [128, 2, 8], F32)
        wbt = Z.wb
        for (dst, src) in ((gain, gain3), (shift, shift3)):
            S.op("dve", lambda: nc.vector.tensor_scalar(dst.t[:, 0, :], src.t[:, 0, :], wbt.t[:, 0:1], None, ALU.mult), reads=[src, wbt], writes=[dst])
            S.op("dve", lambda: nc.vector.scalar_tensor_tensor(out=dst.t[:, 0, :], in0=src.t[:, 1, :], scalar=wbt.t[:, 1:2], in1=dst.t[:, 0, :], op0=ALU.mult, op1=ALU.add),
                 reads=[src, wbt, dst], writes=[dst])
            S.op("dve", lambda: nc.vector.tensor_copy(dst.t[:, 1, :], src.t[:, 2, :]), reads=[src], writes=[dst])
        convw = S.sb("convw", [128, 2 * NFC, 3], F32)
        convb = S.sb("convb", [128, 2 * NFC], F32)
        S.dma("sp", convw.t[:], Z.convw_d[i], writes=[convw])
        S.dma("sp", convb.t[:], Z.convb_d[i], writes=[convb])
        hmask = Z.hmask
        GB = S.sb("GB", [128, 4, D], F32)
        with S.scope():
            G6 = S.sb("G6", [128, 6, D], F32)
            tn = S.sb("tn", [128, 2, D], F32)
            for r in range(3):
                mod_row_bcast(S, nc, Z, G6, r, i, 2, r)
                mod_row_bcast(S, nc, Z, G6, 3 + r, i, 5, r)
            S.dma("sp", tn.t[:, 0, :], Z.ng_d[i, 1:2, :].partition_broadcast(128), writes=[tn], indep=True)
            S.dma("act", tn.t[:, 1, :], Z.ng_d[i, 3:4, :].partition_broadcast(128), writes=[tn], indep=True)
            for k in range(2):
                S.op("dve", lambda: nc.vector.tensor_scalar(GB.t[:, 2 * k, :], G6.t[:, 3 * k, :], wbt.t[:, 0:1], None, ALU.mult), reads=[G6, wbt], writes=[GB])
                S.op("dve", lambda: nc.vector.scalar_tensor_tensor(out=GB.t[:, 2 * k, :], in0=G6.t[:, 3 * k + 1, :], scalar=wbt.t[:, 1:2], in1=GB.t[:, 2 * k, :], op0=ALU.mult, op1=ALU.add),
                     reads=[G6, wbt, GB], writes=[GB])
                S.op("dve", lambda: nc.vector.tensor_copy(GB.t[:, 2 * k + 1, :], G6.t[:, 3 * k + 2, :]), reads=[G6], writes=[GB])
                for q in range(2):
                    S.op("dve", lambda: nc.vector.tensor_tensor(out=GB.t[:, 2 * k + q, :], in0=GB.t[:, 2 * k + q, :], in1=tn.t[:, k, :], op=ALU.mult), reads=[GB, tn], writes=[GB])
        xh = S.sb("xh", [2, D], F32)
        with S.scope():
            cand = S.sb("cand", [2, 8, D], F32)
            xa3 = xall_src.rearrange("(g n) d -> g n d", n=AGC)
            S.dma("sp", cand.t[0:1, :, :], xa3[56:64, AGC - 1:AGC, :].rearrange("r o d -> o r d"), reads=[Z.xall_tok], writes=[cand], indep=True)
            S.dma("act", cand.t[1:2, :, :], xa3[0:8, 0:1, :].rearrange("r o d -> o r d"), reads=[Z.xall_tok], writes=[cand], indep=True)
            S.op("dve", lambda: nc.vector.tensor_scalar(xh.t[:], cand.t[:, 0, :], Z.wnb.t[:, 0:1], None, ALU.mult), reads=[cand, Z.wnb], writes=[xh])
            for r in range(1, 8):
                S.op("dve", lambda: nc.vector.scalar_tensor_tensor(out=xh.t[:], in0=cand.t[:, r, :], scalar=Z.wnb.t[:, r:r + 1], in1=xh.t[:], op0=ALU.mult, op1=ALU.add),
                     reads=[cand, Z.wnb, xh], writes=[xh])
        N = NormCtx(S, ident, epsc)
        stage = rot_sb(S, "stage", [128, 8, 256], F32, 2)
        pY = rot_ps(S, "pY", [128, D], F32, 2)
        pG = rot_ps(S, "pG", [128, 512], F32, 2)
        for pi in range(2):
            n_main, n_ctx = 8, (NCTX if pi == 0 else 0)
            NTOK = f_ntok(n_main, n_ctx)
            hr = 1 + n_main * 128
            NA = n_main * 128 + 4 + n_ctx * 128
            tiles = [(t * 128, 128, 0, 1 + t * 128, PASS_BASE[pi] + t * 128, pi * 1024 + t * 128, pi * 1024 + t * 128) for t in range(n_main)]
            tiles.append((n_main * 128, 2, 0, None, HALO_BASE[pi], None, None))
            tiles += [(n_main * 128 + 2 + j * 128, 128, 1, hr + 2 + j * 128, CTX_BASE + j * 128, 2048 + j * 128, 2048 + j * 128) for j in range(n_ctx)]
            with S.scope():
                hid = S.sb("hid", [128, NFC, NA], BF16)
                xmid_tok = S.tok(f"xmid{i}_{pi}")
                xmid_d = Z.xmid_d
                with S.scope():
                    hT = S.sb("hT", [128, 8, NTOK], BF16)
                    with S.scope():
                        yts = [S.sb(f"yt{k}", [128, D], F32) for k in range(2)]
                        xts = [S.sb(f"xt{k}", [128, D], F32) for k in range(2)]
                        xms = [S.sb(f"xm{k}", [128, D], F32) for k in range(2)]
                        for ti, (col0, P, r, acol, yrow, xrow, orow) in enumerate(tiles):
                            yt, xt, xm = yts[ti % 2], xts[ti % 2], xms[ti % 2]
                            S.dma("sp", yt.t[:P, :], Z.yloc_d[yrow:yrow + P, :], reads=[Z.yloc_tok], writes=[yt])
                            if xrow is not None:
                                S.dma("act", xt.t[:P, :], xcur_d[xrow:xrow + P, :], reads=[xcur_tok], writes=[xt])
                            else:
                                S.op("dve", lambda: nc.vector.tensor_copy(xt.t[0:2, :], xh.t[:]), reads=[xh], writes=[xt])
                                if pi == 0:
                                    S.dma("act", xt.t[1:2, :], xcur_d[1024:1025, :], reads=[xcur_tok], writes=[xt])
                                else:
                                    S.dma("act", xt.t[0:1, :], xcur_d[1023:1024, :], reads=[xcur_tok], writes=[xt])
                            ss = N.ss.next()
                            sq = N.sq.next()
                            emit_rstd(S, nc, yt.t[:P, :], yt, P, ss, sq, epsc)
                            S.op("dve", lambda: nc.vector.scalar_tensor_tensor(out=xm.t[:P, :], in0=yt.t[:P, :], scalar=ss.t[:P, 0:1], in1=GB.t[:P, r, :], op0=ALU.mult, op1=ALU.mult),
                                 reads=[yt, ss, GB], writes=[xm])
                            S.op("pool", lambda: nc.gpsimd.tensor_tensor(out=xm.t[:P, :], in0=xm.t[:P, :], in1=xt.t[:P, :], op=ALU.add), reads=[xm, xt], writes=[xm])
                            if acol is not None:
                                S.dma("act", xmid_d[col0:col0 + P, :], xm.t[:P, :], reads=[xm], writes=[xmid_tok], indep=True)
                            emit_norm_T(S, nc, N, xm, P, gain, shift, r, hT, col0)
                    with S.scope():
                        wblk = [S.sb(f"wblk{k}", [128, 8, 256], BF16) for k in range(2)]
                        ab = [S.sb("abg", [128, NA], F32), S.sb("abu", [128, NA], F32)]
                        cg = S.sb("cg", [128, NA], F32)
                        cu = S.sb("cu", [128, NA], F32)
                        tp = S.sb("tp", [128, NA], F32)
                        for a in ab:
                            S.op("pool", lambda: nc.gpsimd.memset(a.t[:], 0.0), writes=[a])
                        win_v = w_in_d.rearrange("(c p) n -> p c n", p=128)
                        groups = [(g * 512, 512, 1 + g * 512) for g in range(n_main // 4)]
                        if n_ctx:
                            groups.append((n_main * 128 + 2, n_ctx * 128, hr + 2))
                        wi = 0
                        for jb in range(NFC // 2):
                            blk = []
                            for which in range(2):
                                st = stage.next()
                                wb = wblk[which]
                                c0 = which * D_FF + jb * 256
                                S.dma("sp" if which == 0 else "act", st.t[:], win_v[:, :, c0:c0 + 256], writes=[st])
                                S.op("pool", lambda: nc.gpsimd.tensor_copy(wb.t[:], st.t[:]), reads=[st], writes=[wb])
                                blk.append(wb)
                            for jl in range(2):
                                j = jb * 2 + jl
                                for which in range(2):
                                    wb = blk[which]
                                    a = ab[which]
                                    fc = which * NFC + j
                                    for (tc0, n, ac0) in groups:
                                        pg = pG.next()
                                        for c in range(8):
                                            S.op("pe", lambda: nc.tensor.matmul(pg.t[:, :n], lhsT=wb.t[:, c, jl * 128:(jl + 1) * 128], rhs=hT.t[:, c, tc0:tc0 + n], start=(c == 0), stop=(c == 7)),
                                                 reads=[wb, hT], writes=[pg])
                                        if wi % 2 == 0:
                                            S.op("act", lambda: nc.scalar.copy(a.t[:, ac0:ac0 + n], pg.t[:, :n]), reads=[pg], writes=[a])
                                        else:
                                            S.op("dve", lambda: nc.vector.tensor_copy(a.t[:, ac0:ac0 + n], pg.t[:, :n]), reads=[pg], writes=[a])
                                        wi += 1
                                    pg = pG.next()
                                    hc = n_main * 128
                                    for c in range(8):
                                        S.op("pe", lambda: nc.tensor.matmul(pg.t[:, :2], lhsT=wb.t[:, c, jl * 128:(jl + 1) * 128], rhs=hT.t[:, c, hc:hc + 2], start=(c == 0), stop=(c == 7)),
                                             reads=[wb, hT], writes=[pg])
                                    S.op("dve", lambda: nc.vector.tensor_tensor(out=a.t[:, 0:1], in0=pg.t[:, 0:1], in1=hmask.t[:, 2 * pi:2 * pi + 1], op=ALU.mult), reads=[pg, hmask], writes=[a])
                                    S.op("dve", lambda: nc.vector.tensor_tensor(out=a.t[:, hr:hr + 1], in0=pg.t[:, 1:2], in1=hmask.t[:, 2 * pi + 1:2 * pi + 2], op=ALU.mult), reads=[pg, hmask], writes=[a])
                                    cc = cg if which == 0 else cu
                                    S.op("act", lambda: nc.scalar.activation(out=cc.t[:, 1:NA - 1], in_=a.t[:, 1:NA - 1], func=AF.Identity, bias=convb.t[:, fc:fc + 1], scale=convw.t[:, fc, 1:2]),
                                         reads=[a, convw, convb], writes=[cc])
                                    S.op("dve", lambda: nc.vector.scalar_tensor_tensor(out=cc.t[:, 1:NA - 1], in0=a.t[:, 0:NA - 2], scalar=convw.t[:, fc, 0:1], in1=cc.t[:, 1:NA - 1], op0=ALU.mult, op1=ALU.add),
                                         reads=[a, convw, cc], writes=[cc])
                                    if which == 0:
                                        S.op("dve", lambda: nc.vector.scalar_tensor_tensor(out=cc.t[:, 1:NA - 1], in0=a.t[:, 2:NA], scalar=convw.t[:, fc, 2:3], in1=cc.t[:, 1:NA - 1], op0=ALU.mult, op1=ALU.add),
                                             reads=[a, convw, cc], writes=[cc])
                                    else:
                                        S.op("pool", lambda: nc.gpsimd.tensor_scalar(tp.t[:, 1:NA - 1], a.t[:, 2:NA], convw.t[:, fc, 2:3], None, ALU.mult), reads=[a, convw], writes=[tp])
                                        S.op("pool", lambda: nc.gpsimd.tensor_tensor(out=cc.t[:, 1:NA - 1], in0=cc.t[:, 1:NA - 1], in1=tp.t[:, 1:NA - 1], op=ALU.add), reads=[cc, tp], writes=[cc])
                                S.op("act", lambda: nc.scalar.activation(out=cg.t[:, 1:NA - 1], in_=cg.t[:, 1:NA - 1], func=AF.Silu), reads=[cg], writes=[cg])
                                S.op("pool", lambda: nc.gpsimd.tensor_tensor(out=hid.t[:, j, 1:NA - 1], in0=cg.t[:, 1:NA - 1], in1=cu.t[:, 1:NA - 1], op=ALU.mult), reads=[cg, cu], writes=[hid])
                with S.scope():
                    wout = S.sb("wout", [128, NFC, D], BF16)
                    wout_v = w_out_d.rearrange("(c p) n -> p c n", p=128)
                    for j in range(NFC):
                        for h4 in range(4):
                            st = stage.next()
                            S.dma("sp" if h4 % 2 == 0 else "act", st.t[:, 0, :], wout_v[:, j, h4 * 256:(h4 + 1) * 256], writes=[st])
                            S.op("pool", lambda: nc.gpsimd.tensor_copy(wout.t[:, j, h4 * 256:(h4 + 1) * 256], st.t[:, 0, :]), reads=[st], writes=[wout])
                    xms = [S.sb(f"xm3{k}", [128, D], F32) for k in range(2)]
                    xos = [S.sb(f"xo3{k}", [128, D], F32) for k in range(2)]
                    k3 = 0
                    for (col0, P, r, acol, yrow, xrow, orow) in tiles:
                        if acol is None:
                            continue
                        xm = xms[k3 % 2]
                        xo = xos[k3 % 2]
                        k3 += 1
                        S.dma("sp", xm.t[:], xmid_d[col0:col0 + 128, :], reads=[xmid_tok], writes=[xm])
                        y = pY.next()
                        for half in range(2):
                            for j in range(NFC):
                                S.op("pe", lambda: nc.tensor.matmul(y.t[:, half * 512:(half + 1) * 512], lhsT=hid.t[:, j, acol:acol + 128], rhs=wout.t[:, j, half * 512:(half + 1) * 512],
                                                                    start=(j == 0), stop=(j == NFC - 1)), reads=[hid, wout], writes=[y])
                        ss = N.ss.next()
                        sq = N.sq.next()
                        emit_rstd(S, nc, y.t[:, :], y, 128, ss, sq, epsc)
                        S.op("dve", lambda: nc.vector.scalar_tensor_tensor(out=xo.t[:], in0=y.t[:], scalar=ss.t[:, 0:1], in1=GB.t[:, 2 + r, :], op0=ALU.mult, op1=ALU.mult),
                             reads=[y, ss, GB], writes=[xo])
                        S.op("pool", lambda: nc.gpsimd.tensor_tensor(out=xo.t[:], in0=xo.t[:], in1=xm.t[:], op=ALU.add), reads=[xo, xm], writes=[xo])
                        if last:
                            if r == 0:
                                S.dma("act", Z.out_d[orow:orow + 128, :], xo.t[:], reads=[xo], writes=[], out=True)
                        else:
                            S.dma("act", xnew_d[orow:orow + 128, :], xo.t[:], reads=[xo], writes=[xnew_tok], indep=True)


def build_fused(dbg=False, n_layers=4, test_mo=False):
    nc = bass.Bass("TRN2", target_bir_lowering=False)
    Z = Fz()
    Z.dr = lambda name, shape, dt=F32, kind="ExternalInput": nc.dram_tensor(name, list(shape), dt, kind=kind).ap()
    dr = Z.dr
    S = Sched(nc)
    NTT = NB * TPB_
    Z.ngfm_d = dr("ngfm", [128, 4, 4, 8])
    Z.ng_d = dr("ng", [4, 4, D])
    Z.xin_d = dr("xin", [8 * RB, D])
    Z.xloc_in_d = dr("xloc_in", [RB, D])
    Z.me_wqk_d = dr("me_wqk", [2, D, 128]); Z.me_wgu_d = dr("me_wgu", [2, D, 576]); Z.me_wv_d = dr("me_wv", [2, D, 64])
    Z.me_wsT_d = dr("me_wsT", [2, 128, 128]); Z.me_bs_d = dr("me_bs", [2, 128, 1]); Z.me_bias_d = dr("me_bias", [2, 5, 128, 576])
    Z.me_woh_d = dr("me_woh", [2, 128, D])
    Z.mo_w_d = dr("mo_w", [2, D, 768]); Z.mo_woh_d = dr("mo_woh", [2, 256, D]); Z.mo_lg_d = dr("mo_lg", [2, 128, 2]); Z.mo_gn_d = dr("mo_gn", [2, 128, 256])
    Z.rope_d = dr("rope", [NLAT, 128, 256])
    Z.cE_d = dr("cE", [2, 128, 128]); Z.cM_d = dr("cM", [2, 128, 128]); Z.cQ_d = dr("cQ", [2, 128, 128]); Z.cK_d = dr("cK", [128, 2])
    Z.ffn_w_in_d = dr("ffn_w_in", [4, D, 2 * D_FF]); Z.ffn_w_out_d = dr("ffn_w_out", [4, D_FF, D])
    Z.convw_d = dr("convw", [4, 128, 2 * NFC, 3]); Z.convb_d = dr("convb", [4, 128, 2 * NFC])
    hmask_d = dr("hmask", [128, 4]); wb_d = dr("wb", [128, 2]); wnb_d = dr("wnb", [2, 8]); ident_d = dr("ident", [128, 128])
    Z.out_d = dr("out", [2048, D], kind="ExternalOutput")
    Z.modloc_d = dr("modloc", [12, 768], kind="Internal"); Z.modall_d = dr("modall", [96, 768], kind="Internal")
    Z.xall_d = dr("xall", [8 * RB, D], kind="Internal")
    Z.xloc_d = [dr(f"xloc{k}", [RB, D], kind="Internal") for k in range(2)]
    Z.yin_d = dr("yin", [8 * NSH, D], kind="Internal"); Z.yloc_d = dr("yloc", [NSH, D], kind="Internal")
    Z.xmid_d = dr("xmid", [1282, D], kind="Internal")
    Z.gS_d = dr("gS", [NTT * 128, 256], kind="Internal"); Z.oF_d = dr("oF", [NTT * 128, 256], kind="Internal")
    Z.modloc_tok = S.tok("modloc"); Z.modall_tok = S.tok("modall"); Z.xall_tok = S.tok("xall")
    Z.xloc_tok = [S.tok("xloc0"), S.tok("xloc1")]; Z.xloc_in_tok = S.tok("xlocin")
    Z.yin_tok = S.tok("yin"); Z.yloc_tok = S.tok("yloc")
    Z.gS_tok = S.shared_toks("gS", NTT, 4); Z.oF_tok = S.shared_toks("oF", NTT, 4)
    ident_f = S.sb("ident_f", [128, 128], F32)
    Z.ident = S.sb("ident_b", [128, 128], BF16)
    S.dma("sp", ident_f.t[:], ident_d, writes=[ident_f])
    S.op("dve", lambda: nc.vector.tensor_copy(Z.ident.t[:], ident_f.t[:]), reads=[ident_f], writes=[Z.ident])
    Z.epsc = S.sb("epsc", [128, 1], F32)
    S.op("dve", lambda: nc.vector.memset(Z.epsc.t[:], EPS), writes=[Z.epsc])
    Z.hmask = S.sb("hmask", [128, 4], F32); Z.wb = S.sb("wbsel", [128, 2], F32); Z.wnb = S.sb("wnb", [2, 8], F32)
    S.dma("sp", Z.hmask.t[:], hmask_d, writes=[Z.hmask])
    S.dma("sp", Z.wb.t[:], wb_d, writes=[Z.wb])
    S.dma("sp", Z.wnb.t[:], wnb_d, writes=[Z.wnb])
    zrow = S.sb("zrow", [1, D], F32)
    S.op("dve", lambda: nc.vector.memset(zrow.t[:], 0.0), writes=[zrow])
    yin3 = Z.yin_d.rearrange("(j n) d -> j n d", j=8)
    for b in range(NB):
        S.dma("sp", yin3[b * 4, HALO_BASE[0]:HALO_BASE[0] + 1, :], zrow.t[:], reads=[zrow], writes=[Z.yin_tok], indep=True)
        S.dma("sp", yin3[b * 4 + 3, HALO_BASE[1] + 1:HALO_BASE[1] + 2, :], zrow.t[:], reads=[zrow], writes=[Z.yin_tok], indep=True)
    emit_ADA(S, nc, Z)
    if test_mo:
        Z.xall_d = Z.xin_d
        dbg_yin = dr("dbg_yin", [NSH, D], kind="ExternalOutput")
        Z.dump_oA = dr("dump_oA", [128, 8, 256], BF16, kind="ExternalOutput")
        Z.dump_v = dr("dump_v", [128, 8, 256], BF16, kind="ExternalOutput")
        Z.dump_q = dr("dump_q", [128, 1024], BF16, kind="ExternalOutput")
        emit_MO_fused(S, nc, Z, 1)
        for k in range(0, NSH, 577):
            S.dma("sp", dbg_yin[k:k + 577, :], Z.yin_d[k:k + 577, :], reads=[Z.yin_tok], writes=[], out=True)
        S.finish()
        return nc
    if dbg:
        dbg_mod = dr("dbg_mod", [96, 768], kind="ExternalOutput")
        dbg_yloc = [dr(f"dbg_yloc{k}", [NSH, D], kind="ExternalOutput") for k in range(n_layers)]
        dbg_xloc = [dr(f"dbg_xloc{k}", [RB, D], kind="ExternalOutput") for k in range(n_layers)]
        S.dma("sp", dbg_mod, Z.modall_d, reads=[Z.modall_tok], writes=[], out=True)
    for i in range(n_layers):
        S.renew()
        if i % 2 == 0:
            emit_ME_fused(S, nc, Z, i)
        else:
            emit_MO_fused(S, nc, Z, i)
        S.coll("ReduceScatter", [Z.yin_d], [Z.yloc_d], reads=[Z.yin_tok], writes=[Z.yloc_tok], op=ALU.add)
        if dbg:
            S.dma("sp", dbg_yloc[i], Z.yloc_d, reads=[Z.yloc_tok], writes=[], out=True)
        S.renew()
        emit_F_fused(S, nc, Z, i, last=(i == 3))
        if dbg and i < 3:
            S.dma("sp", dbg_xloc[i], Z.xloc_d[i % 2], reads=[Z.xloc_tok[i % 2]], writes=[], out=True)
        if i < 3:
            for k in range(NAG):
                S.coll("AllGather", [Z.xloc_d[i % 2][k * AGC:(k + 1) * AGC, :]], [Z.xall_d[k * 8 * AGC:(k + 1) * 8 * AGC, :]],
                       reads=[Z.xloc_tok[i % 2]], writes=[Z.xall_tok])
    S.finish()
    return nc


def kernel(x, c, ctx, c_ctx, ada_w, ada_b, norm_g, hyb_w_in, na_rpb, sgu_w, sgu_b, hyb_w_out,
           ret_w_in, ret_log_decay, ret_gn_g, ret_w_out, ffn_w_in, ffn_conv_w, ffn_conv_b, ffn_w_out):
    f32 = lambda a: np.ascontiguousarray(np.asarray(a, dtype=np.float32))
    x, c, ctx, c_ctx, ada_w, ada_b, norm_g = map(f32, (x, c, ctx, c_ctx, ada_w, ada_b, norm_g))
    hyb_w_in, na_rpb, sgu_w, sgu_b, hyb_w_out = map(f32, (hyb_w_in, na_rpb, sgu_w, sgu_b, hyb_w_out))
    ret_w_in, ret_log_decay, ret_gn_g, ret_w_out = map(f32, (ret_w_in, ret_log_decay, ret_gn_g, ret_w_out))
    ffn_w_in, ffn_conv_w, ffn_conv_b, ffn_w_out = map(f32, (ffn_w_in, ffn_conv_w, ffn_conv_b, ffn_w_out))
    B, T, _ = x.shape
    cvec = np.stack([c[0], c[1], c_ctx], 0)
    cT = np.ascontiguousarray(cvec.T.reshape(8, 128, 3).transpose(1, 0, 2))
    xrk = np.stack([np.concatenate([x[r // 4, (r % 4) * 2048:(r % 4 + 1) * 2048], ctx[r // 4]], 0) for r in range(8)], 0)
    xin = np.ascontiguousarray(xrk.reshape(8, NAG, AGC, D).transpose(1, 0, 2, 3).reshape(8 * RB, D))
    ngfm = np.ascontiguousarray(norm_g.reshape(4, 4, 8, 128).transpose(3, 0, 1, 2))
    convw = np.ascontiguousarray(ffn_conv_w.reshape(4, 3, 2 * NFC, 128).transpose(0, 3, 2, 1))
    convb = np.ascontiguousarray(ffn_conv_b.reshape(4, 2 * NFC, 128).transpose(0, 2, 1))
    shared = {"cT": cT, "ngfm": ngfm, "ng": norm_g, "xin": xin, "ffn_w_in": ffn_w_in, "ffn_w_out": ffn_w_out,
              "convw": convw, "convb": convb, "ident": np.eye(128, dtype=np.float32)}
    maps = []
    for core in range(8):
        h = core
        b, q = divmod(core, 4)
        m = dict(shared)
        cs = slice(core * 768, (core + 1) * 768)
        m["ada_w"] = np.ascontiguousarray(ada_w[:, :, cs])
        m["ada_b"] = np.ascontiguousarray(ada_b[:, cs].reshape(4, 6, 128).transpose(2, 0, 1))
        m["xloc_in"] = np.ascontiguousarray(xrk[core])
        hs = slice(h * 64, (h + 1) * 64)
        wqk, wgu, wv, wsT, bs, bias, woh = [], [], [], [], [], [], []
        for j2 in range(2):
            w_in = hyb_w_in[j2]
            g_cols = w_in[:, 2048:2560]
            order = list(range(h * 64, (h + 1) * 64)) + [cc for cc in range(512) if not (h * 64 <= cc < (h + 1) * 64)]
            wqk.append(np.concatenate([w_in[:, 0:512][:, hs], w_in[:, 512:1024][:, hs]], 1))
            wgu.append(np.concatenate([g_cols[:, order], w_in[:, 1536:2048][:, hs]], 1))
            wv.append(w_in[:, 1024:1536][:, hs])
            wsT.append(sgu_w[j2, h].T)
            bs.append(sgu_b[j2, h][:, None])
            bias.append(na_bias_tables(na_rpb[j2, h], NLAT))
            woh.append(np.concatenate([hyb_w_out[j2][hs], hyb_w_out[j2][512 + h * 64:512 + (h + 1) * 64]], 0))
        m["me_wqk"], m["me_wgu"], m["me_wv"] = [np.ascontiguousarray(np.stack(v)) for v in (wqk, wgu, wv)]
        m["me_wsT"], m["me_bs"], m["me_bias"], m["me_woh"] = [np.ascontiguousarray(np.stack(v)) for v in (wsT, bs, bias, woh)]
        mo_w, mo_woh, mo_lg, mo_gn = [], [], [], []
        for j2 in range(2):
            w_in = ret_w_in[j2]
            mo_w.append(np.concatenate([w_in[:, h * 128:(h + 1) * 128], w_in[:, 1024 + h * 128:1024 + (h + 1) * 128],
                                        w_in[:, 2048 + h * 256:2048 + (h + 1) * 256], w_in[:, 4096 + h * 256:4096 + (h + 1) * 256]], 1))
            mo_woh.append(ret_w_out[j2][h * 256:(h + 1) * 256])
            mo_lg.append(np.broadcast_to(ret_log_decay[j2][:, h], (128, 2)))
            mo_gn.append(np.broadcast_to(ret_gn_g[j2][h * 256:(h + 1) * 256], (128, 256)))
        m["mo_w"], m["mo_woh"], m["mo_lg"], m["mo_gn"] = [np.ascontiguousarray(np.stack(v), dtype=np.float32) for v in (mo_w, mo_woh, mo_lg, mo_gn)]
        m.update(_mo_consts())
        hm = np.ones((128, 4), np.float32)
        hm[:, 0] = 1.0 if q > 0 else 0.0
        hm[:, 3] = 1.0 if q < 3 else 0.0
        m["hmask"] = hm
        wbv = np.zeros((128, 2), np.float32)
        wbv[:, b] = 1.0
        m["wb"] = wbv
        wnb = np.zeros((2, 8), np.float32)
        if q > 0:
            wnb[0, core - 1] = 1.0
        if q < 3:
            wnb[1, core + 1] = 1.0
        m["wnb"] = wnb
        maps.append(m)
    if _DBG.get("maps_only"):
        return maps
    res = _run(_prog("fused", build_fused), maps)
    out = np.empty((B, T, D), np.float32)
    for core in range(8):
        b, q = divmod(core, 4)
        out[b, q * 2048:(q + 1) * 2048] = res[core]["out"]
    return out


def _mo_consts():
    t = np.arange(NLAT * 128)
    row = (t // 64).astype(np.float32)
    col = (t % 64).astype(np.float32)
    inv = (10000.0 ** (-np.arange(32, dtype=np.float32) / 32)).astype(np.float32)
    ang = np.concatenate([row[:, None] * inv, col[:, None] * inv], -1).astype(np.float32)
    cos, sin = np.cos(ang).astype(np.float32), np.sin(ang).astype(np.float32)
    rope = np.concatenate([cos, cos, sin, sin], -1).reshape(NLAT, 128, 256)
    pos = np.arange(128, dtype=np.float32)
    kq = pos[None, :] - pos[:, None]
    cE = np.stack([np.where(kq >= 0, kq, 0), np.where(kq <= 0, -kq, 0)]).astype(np.float32)
    cM = np.stack([(kq >= 0), (kq <= 0)]).astype(np.float32)
    cQ = np.stack([np.broadcast_to(pos + 1, (128, 128)), np.broadcast_to(128 - pos, (128, 128))]).astype(np.float32)
    cK = np.stack([127 - pos, pos], 1).astype(np.float32)
    return {"rope": np.ascontiguousarray(rope), "cE": np.ascontiguousarray(cE), "cM": np.ascontiguousarray(cM),
            "cQ": np.ascontiguousarray(cQ), "cK": np.ascontiguousarray(cK)}
```

```python
import contextlib
import numpy as np
import concourse.bass as bass
import concourse.mybir as mybir
from concourse.bass_utils import run_bass_kernel_spmd

F32 = mybir.dt.float32
BF16 = mybir.dt.bfloat16
AF = mybir.ActivationFunctionType
ALU = mybir.AluOpType
AX = mybir.AxisListType


class Buf:
    def __init__(self, t=None, name=""):
        self.t = t
        self.name = name
        self.w = None
        self.r = []
        self.ld = None
        self.st = None


class SemCounter:
    def __init__(self, sem, key, shared=False):
        self.sem = sem
        self.key = key
        self.cnt = 0
        self.shared = shared


class Sched:
    def __init__(self, nc, strict_same=True):
        self.nc = nc
        self.es = contextlib.ExitStack()
        self.stack = [self.es]
        self.eng = {"pe": nc.tensor, "act": nc.scalar, "dve": nc.vector, "pool": nc.gpsimd, "sp": nc.sync}
        self.sem = {}
        self.cnt = {}
        for e in self.eng:
            self.sem[e] = self.es.enter_context(nc.semaphore("s_" + e))
            self.cnt[e] = 0
        self.waited = {e: {} for e in self.eng}
        self.ekey = {e: e for e in self.eng}
        self.gen = 0
        self.strict_same = strict_same
        self.relax_pe = False
        self.relax_war = False
        self.out_events = []
        self.nsem = 0
        self.n_ops = 0
        self.uid = 0
        self.coll_inc = 1
        self.scope_bufs = [[]]
        self.free_counters = []

    def sb(self, name, shape, dtype):
        self.uid += 1
        b = Buf(self.stack[-1].enter_context(self.nc.sbuf_tensor(f"sb_{name}_{self.uid}", list(shape), dtype)), name)
        self.scope_bufs[-1].append(b)
        return b

    def ps(self, name, shape, dtype=F32):
        self.uid += 1
        b = Buf(self.stack[-1].enter_context(self.nc.psum_tensor(f"ps_{name}_{self.uid}", list(shape), dtype)), name)
        self.scope_bufs[-1].append(b)
        return b

    @contextlib.contextmanager
    def scope(self):
        es = contextlib.ExitStack()
        self.stack.append(es)
        self.scope_bufs.append([])
        try:
            yield es
        finally:
            bufs = self.scope_bufs.pop()
            self.barrier(bufs)
            for b in bufs:
                for sc_ in (b.ld, b.st):
                    if sc_ is not None and not sc_.shared:
                        self.free_counters.append(sc_)
            self.stack.pop()
            es.close()

    def tok(self, name=""):
        return Buf(None, name)

    def new_sem(self, name):
        self.nsem += 1
        return self.es.enter_context(self.nc.semaphore(f"{name}_{self.nsem}"))

    def new_counter(self, name, shared=False):
        if not shared and self.free_counters:
            return self.free_counters.pop()
        sem = self.new_sem(name)
        return SemCounter(sem, f"D{self.nsem}", shared)

    def shared_toks(self, name, n, k=8):
        cs = [self.new_counter(f"{name}{i}", shared=True) for i in range(k)]
        toks = []
        for i in range(n):
            t = Buf(None, f"{name}{i}")
            t.ld = cs[i % k]
            toks.append(t)
        return toks

    def _deps(self, reads, writes):
        ev = []
        for b in reads:
            if b.w is not None:
                ev.append(b.w)
        for b in writes:
            if b.w is not None:
                ev.append(b.w)
            ev.extend(b.r)
        return ev

    def _emit_waits(self, e, events):
        need = {}
        for (k, sem, v) in events:
            if k == self.ekey[e] and (not self.strict_same or (e == "pe" and self.relax_pe)):
                continue
            if self.waited[e].get(k, 0) >= v:
                continue
            if k not in need or need[k][1] < v:
                need[k] = (sem, v)
        for k, (sem, v) in need.items():
            self.eng[e].wait_ge(sem, v)
            self.waited[e][k] = v

    def op(self, e, fn, reads=(), writes=()):
        if self.relax_war:
            ev = [b.w for b in reads if b.w is not None]
            for b in writes:
                for x in ([b.w] if b.w is not None else []) + b.r:
                    if x[0] != self.ekey[e]:
                        ev.append(x)
        else:
            ev = self._deps(reads, writes)
        self._emit_waits(e, ev)
        ins = fn()
        self.cnt[e] += 1
        ins.then_inc(self.sem[e], 1)
        evt = (self.ekey[e], self.sem[e], self.cnt[e])
        for b in writes:
            b.w = evt
            b.r = []
        for b in reads:
            if b not in writes:
                b.r.append(evt)
        self.n_ops += 1
        return ins

    def dma(self, q, out_ap, in_ap, reads=(), writes=(), out=False, indep=False, **kw):
        if writes:
            b = writes[0]
            if b.ld is None:
                b.ld = self.new_counter("ld_" + b.name)
            sc = b.ld
        else:
            b = reads[0]
            if b.st is None:
                b.st = self.new_counter("st_" + b.name)
            sc = b.st
        ev = self._deps(reads, ())
        for wb_ in writes:
            ev.extend(wb_.r)
            if wb_.w is not None and not (indep and wb_.w[0] == sc.key):
                ev.append(wb_.w)
        if sc.shared and sc.cnt > 0:
            ev.append((sc.key, sc.sem, sc.cnt))
        self._emit_waits(q, ev)
        ins = self.eng[q].dma_start(out=out_ap, in_=in_ap, **kw)
        sc.cnt += 16
        ins.then_inc(sc.sem, 16)
        evt = (sc.key, sc.sem, sc.cnt)
        for b in writes:
            if indep:
                if b.w is not None and b.w[0] != sc.key:
                    b.r.append(b.w)
                b.w = evt
            else:
                b.w = evt
                b.r = []
        for b in reads:
            if b not in writes:
                b.r.append(evt)
        if out:
            self.out_events.append(evt)
        return ins

    def coll(self, kind, ins, outs, reads=(), writes=(), groups=None, op=None):
        b = writes[0]
        if b.ld is None:
            b.ld = self.new_counter("cc_" + b.name)
        sc = b.ld
        ev = self._deps(reads, writes)
        if sc.shared and sc.cnt > 0:
            ev.append((sc.key, sc.sem, sc.cnt))
        self._emit_waits("pool", ev)
        ins_ = self.nc.gpsimd.collective_compute(kind, op or ALU.bypass, replica_groups=groups or [list(range(8))], ins=list(ins), outs=list(outs))
        sc.cnt += self.coll_inc
        ins_.then_inc(sc.sem, self.coll_inc)
        evt = (sc.key, sc.sem, sc.cnt)
        for w in writes:
            w.w = evt
            w.r = []
        for r in reads:
            if r not in writes:
                r.r.append(evt)
        return ins_

    def barrier(self, bufs=()):
        ev = [(self.ekey[e], self.sem[e], self.cnt[e]) for e in self.eng if self.cnt[e] > 0]
        for b in bufs:
            if b.w is not None:
                ev.append(b.w)
            ev.extend(b.r)
        for e in self.eng:
            self._emit_waits(e, [x for x in ev if x[0] != self.ekey[e]])

    def renew(self):
        self.barrier()
        self.gen += 1
        for e in list(self.eng):
            if self.cnt[e] == 0:
                continue
            self.sem[e] = self.new_sem(f"s_{e}_g{self.gen}")
            self.cnt[e] = 0
            self.ekey[e] = f"{e}#{self.gen}"

    def finish(self):
        self._emit_waits("sp", self.out_events)
        self.es.close()


D = 1024
EPS = 1e-6
D_FF = 2816
NFC = 22


class Rot:
    def __init__(self, bufs):
        self.bufs = bufs
        self.i = 0

    def next(self):
        b = self.bufs[self.i % len(self.bufs)]
        self.i += 1
        return b


def rot_sb(S, name, shape, dtype, n):
    return Rot([S.sb(f"{name}{i}", shape, dtype) for i in range(n)])


def rot_ps(S, name, shape, dtype, n):
    return Rot([S.ps(f"{name}{i}", shape, dtype) for i in range(n)])


class NormCtx:
    def __init__(self, S, ident, epsc, nrot=2):
        self.S = S
        self.ident = ident
        self.epsc = epsc
        self.sq = rot_sb(S, "nsq", [128, D], F32, 1)
        self.ss = rot_sb(S, "nss", [128, 1], F32, nrot)
        self.xn = rot_sb(S, "nxn", [128, D], BF16, nrot)
        self.pT = rot_ps(S, "npT", [128, 8, 128], BF16, 2)


def emit_rstd(S, nc, src_ap, src_buf, P, ss, sq, epsc, width=D):
    S.op("act", lambda: nc.scalar.activation(out=sq.t[:P, :width], in_=src_ap, func=AF.Square, accum_out=ss.t[:P, :]),
         reads=[src_buf], writes=[sq, ss])
    S.op("act", lambda: nc.scalar.activation(out=ss.t[:P, :], in_=ss.t[:P, :], func=AF.Sqrt, bias=epsc.t[:P, :], scale=1.0 / width),
         reads=[ss, epsc], writes=[ss])
    S.op("dve", lambda: nc.vector.reciprocal(ss.t[:P, :], ss.t[:P, :]), reads=[ss], writes=[ss])


def emit_norm_T(S, nc, N, xbuf, P, gain, shift, gi, hT, col0):
    ss = N.ss.next()
    sq = N.sq.next()
    xn = N.xn.next()
    pT = N.pT.next()
    emit_rstd(S, nc, xbuf.t[:P, :], xbuf, P, ss, sq, N.epsc)
    S.op("act", lambda: nc.scalar.activation(out=xn.t[:P, :], in_=xbuf.t[:P, :], func=AF.Copy, scale=ss.t[:P, :]),
         reads=[xbuf, ss], writes=[xn])
    for c in range(8):
        S.op("pe", lambda: nc.tensor.transpose(pT.t[:, c, :P], xn.t[:P, c * 128:(c + 1) * 128], N.ident.t[:P, :P]),
             reads=[xn, N.ident], writes=[pT])
    for c in range(8):
        e = "dve" if c % 2 == 0 else "pool_no"
        S.op("dve", lambda: nc.vector.tensor_scalar(hT.t[:, c, col0:col0 + P], pT.t[:, c, :P], gain.t[:, gi, c:c + 1], shift.t[:, gi, c:c + 1], ALU.mult, ALU.add),
             reads=[pT, gain, shift], writes=[hT])


def load_cast(S, nc, q, dst_ap, dst_buf, src_ap, stage, caste="pool"):
    S.dma(q, stage.t[:], src_ap, writes=[stage])
    eng = {"pool": nc.gpsimd, "dve": nc.vector, "act": nc.scalar}[caste]
    if caste == "act":
        S.op("act", lambda: nc.scalar.copy(dst_ap, stage.t[:]), reads=[stage], writes=[dst_buf])
    else:
        S.op(caste, lambda: eng.tensor_copy(dst_ap, stage.t[:]), reads=[stage], writes=[dst_buf])


def f_ntok(n_main, n_ctx):
    return n_main * 128 + 2 + n_ctx * 128


def build_F(KO, passes):
    nc = bass.Bass("TRN2", target_bir_lowering=False)
    KC = KO // 128
    dr = lambda name, shape, dt=F32, kind="ExternalInput": nc.dram_tensor(name, list(shape), dt, kind=kind).ap()
    w_o_d = dr("w_o", [KO, D])
    w_in_d = dr("w_in", [D, 2 * D_FF])
    w_out_d = dr("w_out", [D_FF, D])
    convw_d = dr("convw", [128, 2 * NFC, 3])
    convb_d = dr("convb", [128, 2 * NFC])
    rows_d = dr("rows", [6, 128, D])
    cols_d = dr("cols", [128, 5, 8])
    hmask_d = dr("hmask", [128, 2 * len(passes)])
    ident_d = dr("ident", [128, 128])
    S = Sched(nc)
    ident_f = S.sb("ident_f", [128, 128], F32)
    ident = S.sb("ident_b", [128, 128], BF16)
    S.dma("sp", ident_f.t[:], ident_d, writes=[ident_f])
    S.op("dve", lambda: nc.vector.tensor_copy(ident.t[:], ident_f.t[:]), reads=[ident_f], writes=[ident])
    epsc = S.sb("epsc", [128, 1], F32)
    S.op("dve", lambda: nc.vector.memset(epsc.t[:], EPS), writes=[epsc])
    cols = S.sb("cols", [128, 5, 8], F32)
    S.dma("sp", cols.t[:], cols_d, writes=[cols])
    gain = S.sb("gain", [128, 2, 8], F32)
    shift = S.sb("shift", [128, 2, 8], F32)
    for r in range(2):
        S.op("dve", lambda: nc.vector.scalar_tensor_tensor(out=gain.t[:, r, :], in0=cols.t[:, 1 + r, :], scalar=1.0, in1=cols.t[:, 0, :], op0=ALU.add, op1=ALU.mult),
             reads=[cols], writes=[gain])
        S.op("dve", lambda: nc.vector.tensor_copy(shift.t[:, r, :], cols.t[:, 3 + r, :]), reads=[cols], writes=[shift])
    convw = S.sb("convw", [128, 2 * NFC, 3], F32)
    convb = S.sb("convb", [128, 2 * NFC], F32)
    hmask = S.sb("hmask", [128, 2 * len(passes)], F32)
    S.dma("sp", convw.t[:], convw_d, writes=[convw])
    S.dma("sp", convb.t[:], convb_d, writes=[convb])
    S.dma("sp", hmask.t[:], hmask_d, writes=[hmask])
    GB = S.sb("GB", [128, 4, D], F32)
    with contextlib.ExitStack() as es0:
        tn = [Buf(es0.enter_context(nc.sbuf_tensor(f"sb_tn{i}", [128, D], F32)), f"tn{i}") for i in range(2)]
        for i in range(2):
            S.dma("sp", tn[i].t[:], rows_d[i], writes=[tn[i]])
        for k in range(4):
            S.dma("act", GB.t[:, k, :], rows_d[2 + k], writes=[GB])
        for k in range(4):
            S.op("dve", lambda: nc.vector.tensor_tensor(out=GB.t[:, k, :], in0=GB.t[:, k, :], in1=tn[k // 2].t[:], op=ALU.mult),
                 reads=[GB, tn[k // 2]], writes=[GB])
        S.barrier(tn + [GB])
    N = NormCtx(S, ident, epsc)
    stage = rot_sb(S, "stage", [128, 8, 256], F32, 2)
    pY = rot_ps(S, "pY", [128, D], F32, 2)
    pG = rot_ps(S, "pG", [128, 512], F32, 2)

    for pi, (n_main, n_ctx) in enumerate(passes):
        NTOK = f_ntok(n_main, n_ctx)
        NOUT = (n_main + n_ctx) * 128
        oT_d = dr(f"oT{pi}", [KO, NTOK], BF16)
        x_d = dr(f"x{pi}", [NTOK, D])
        xo_d = dr(f"xo{pi}", [NOUT, D], kind="ExternalOutput")
        xmid_d = dr(f"xmid{pi}", [NTOK, D], kind="Internal")
        xmid_tok = S.tok(f"xmid{pi}")
        hr = 1 + n_main * 128
        NA = n_main * 128 + 4 + n_ctx * 128
        tiles = [(t * 128, 128, 0, 1 + t * 128, t * 128) for t in range(n_main)]
        tiles.append((n_main * 128, 2, 0, None, None))
        tiles += [(n_main * 128 + 2 + j * 128, 128, 1, hr + 2 + j * 128, n_main * 128 + j * 128) for j in range(n_ctx)]
        with contextlib.ExitStack() as esP:
            sbp = lambda name, shape, dt: Buf(esP.enter_context(nc.sbuf_tensor(f"sb_{name}_{pi}", list(shape), dt)), f"{name}_{pi}")
            hid = sbp("hid", [128, NFC, NA], BF16)
            with contextlib.ExitStack() as es2:
                sb2 = lambda name, shape, dt: Buf(es2.enter_context(nc.sbuf_tensor(f"sb_{name}_{pi}", list(shape), dt)), f"{name}_{pi}")
                hT = sb2("hT", [128, 8, NTOK], BF16)
                with contextlib.ExitStack() as es1:
                    sb1 = lambda name, shape, dt: Buf(es1.enter_context(nc.sbuf_tensor(f"sb_{name}_{pi}", list(shape), dt)), f"{name}_{pi}")
                    oTs = sb1("oTs", [128, KC, NTOK], BF16)
                    oT_v = oT_d.rearrange("(c p) n -> p c n", p=128)
                    for c in range(KC):
                        S.dma("sp" if c % 2 == 0 else "act", oTs.t[:, c, :], oT_v[:, c, :], writes=[oTs])
                    wo = sb1("wo", [128, KC, D], BF16)
                    wo_v = w_o_d.rearrange("(c p) n -> p c n", p=128)
                    for c in range(KC):
                        for h4 in range(4):
                            st = stage.next()
                            S.dma("sp", st.t[:, 0, :], wo_v[:, c, h4 * 256:(h4 + 1) * 256], writes=[st])
                            S.op("pool", lambda: nc.gpsimd.tensor_copy(wo.t[:, c, h4 * 256:(h4 + 1) * 256], st.t[:, 0, :]), reads=[st], writes=[wo])
                    xts = [sb1(f"xt{i}", [128, D], F32) for i in range(2)]
                    xms = [sb1(f"xm{i}", [128, D], F32) for i in range(2)]
                    for ti, (col0, P, r, acol, orow) in enumerate(tiles):
                        xt = xts[ti % 2]
                        xm = xms[ti % 2]
                        S.dma("sp", xt.t[:P, :], x_d[col0:col0 + P, :], writes=[xt])
                        y = pY.next()
                        for half in range(2):
                            for c in range(KC):
                                S.op("pe", lambda: nc.tensor.matmul(y.t[:P, half * 512:(half + 1) * 512], lhsT=oTs.t[:, c, col0:col0 + P], rhs=wo.t[:, c, half * 512:(half + 1) * 512],
                                                                    start=(c == 0), stop=(c == KC - 1)), reads=[oTs, wo], writes=[y])
                        ss = N.ss.next()
                        sq = N.sq.next()
                        emit_rstd(S, nc, y.t[:P, :], y, P, ss, sq, epsc)
                        S.op("dve", lambda: nc.vector.scalar_tensor_tensor(out=xm.t[:P, :], in0=y.t[:P, :], scalar=ss.t[:P, 0:1], in1=GB.t[:P, r, :], op0=ALU.mult, op1=ALU.mult),
                             reads=[y, ss, GB], writes=[xm])
                        S.op("pool", lambda: nc.gpsimd.tensor_tensor(out=xm.t[:P, :], in0=xm.t[:P, :], in1=xt.t[:P, :], op=ALU.add), reads=[xm, xt], writes=[xm])
                        if acol is not None:
                            S.dma("act", xmid_d[col0:col0 + P, :], xm.t[:P, :], reads=[xm], writes=[xmid_tok])
                        emit_norm_T(S, nc, N, xm, P, gain, shift, r, hT, col0)
                    S.barrier([oTs, wo] + xts + xms)
                with contextlib.ExitStack() as es1:
                    sb1 = lambda name, shape, dt: Buf(es1.enter_context(nc.sbuf_tensor(f"sb_{name}_{pi}", list(shape), dt)), f"{name}_{pi}")
                    wblk = [sb1(f"wblk{i}", [128, 8, 256], BF16) for i in range(2)]
                    ab = [sb1("abg", [128, NA], F32), sb1("abu", [128, NA], F32)]
                    cg = sb1("cg", [128, NA], F32)
                    cu = sb1("cu", [128, NA], F32)
                    tp = sb1("tp", [128, NA], F32)
                    for a in ab:
                        S.op("pool", lambda: nc.gpsimd.memset(a.t[:], 0.0), writes=[a])
                    win_v = w_in_d.rearrange("(c p) n -> p c n", p=128)
                    groups = [(g * 512, 512, 1 + g * 512) for g in range(n_main // 4)]
                    if n_ctx:
                        groups.append((n_main * 128 + 2, n_ctx * 128, hr + 2))
                    wi = 0
                    for jb in range(NFC // 2):
                        blk = []
                        for which in range(2):
                            st = stage.next()
                            wb = wblk[which]
                            c0 = which * D_FF + jb * 256
                            S.dma("sp" if which == 0 else "act", st.t[:], win_v[:, :, c0:c0 + 256], writes=[st])
                            S.op("pool", lambda: nc.gpsimd.tensor_copy(wb.t[:], st.t[:]), reads=[st], writes=[wb])
                            blk.append(wb)
                        for jl in range(2):
                            j = jb * 2 + jl
                            for which in range(2):
                                wb = blk[which]
                                a = ab[which]
                                fc = which * NFC + j
                                for (tc0, n, ac0) in groups:
                                    pg = pG.next()
                                    for c in range(8):
                                        S.op("pe", lambda: nc.tensor.matmul(pg.t[:, :n], lhsT=wb.t[:, c, jl * 128:(jl + 1) * 128], rhs=hT.t[:, c, tc0:tc0 + n], start=(c == 0), stop=(c == 7)),
                                             reads=[wb, hT], writes=[pg])
                                    if wi % 2 == 0:
                                        S.op("act", lambda: nc.scalar.copy(a.t[:, ac0:ac0 + n], pg.t[:, :n]), reads=[pg], writes=[a])
                                    else:
                                        S.op("dve", lambda: nc.vector.tensor_copy(a.t[:, ac0:ac0 + n], pg.t[:, :n]), reads=[pg], writes=[a])
                                    wi += 1
                                pg = pG.next()
                                hc = n_main * 128
                                for c in range(8):
                                    S.op("pe", lambda: nc.tensor.matmul(pg.t[:, :2], lhsT=wb.t[:, c, jl * 128:(jl + 1) * 128], rhs=hT.t[:, c, hc:hc + 2], start=(c == 0), stop=(c == 7)),
                                         reads=[wb, hT], writes=[pg])
                                S.op("dve", lambda: nc.vector.tensor_tensor(out=a.t[:, 0:1], in0=pg.t[:, 0:1], in1=hmask.t[:, 2 * pi:2 * pi + 1], op=ALU.mult), reads=[pg, hmask], writes=[a])
                                S.op("dve", lambda: nc.vector.tensor_tensor(out=a.t[:, hr:hr + 1], in0=pg.t[:, 1:2], in1=hmask.t[:, 2 * pi + 1:2 * pi + 2], op=ALU.mult), reads=[pg, hmask], writes=[a])
                                cc = cg if which == 0 else cu
                                S.op("act", lambda: nc.scalar.activation(out=cc.t[:, 1:NA - 1], in_=a.t[:, 1:NA - 1], func=AF.Identity, bias=convb.t[:, fc:fc + 1], scale=convw.t[:, fc, 1:2]),
                                     reads=[a, convw, convb], writes=[cc])
                                S.op("dve", lambda: nc.vector.scalar_tensor_tensor(out=cc.t[:, 1:NA - 1], in0=a.t[:, 0:NA - 2], scalar=convw.t[:, fc, 0:1], in1=cc.t[:, 1:NA - 1], op0=ALU.mult, op1=ALU.add),
                                     reads=[a, convw, cc], writes=[cc])
                                if which == 0:
                                    S.op("dve", lambda: nc.vector.scalar_tensor_tensor(out=cc.t[:, 1:NA - 1], in0=a.t[:, 2:NA], scalar=convw.t[:, fc, 2:3], in1=cc.t[:, 1:NA - 1], op0=ALU.mult, op1=ALU.add),
                                         reads=[a, convw, cc], writes=[cc])
                                else:
                                    S.op("pool", lambda: nc.gpsimd.tensor_scalar(tp.t[:, 1:NA - 1], a.t[:, 2:NA], convw.t[:, fc, 2:3], None, ALU.mult), reads=[a, convw], writes=[tp])
                                    S.op("pool", lambda: nc.gpsimd.tensor_tensor(out=cc.t[:, 1:NA - 1], in0=cc.t[:, 1:NA - 1], in1=tp.t[:, 1:NA - 1], op=ALU.add), reads=[cc, tp], writes=[cc])
                            S.op("act", lambda: nc.scalar.activation(out=cg.t[:, 1:NA - 1], in_=cg.t[:, 1:NA - 1], func=AF.Silu), reads=[cg], writes=[cg])
                            S.op("pool", lambda: nc.gpsimd.tensor_tensor(out=hid.t[:, j, 1:NA - 1], in0=cg.t[:, 1:NA - 1], in1=cu.t[:, 1:NA - 1], op=ALU.mult), reads=[cg, cu], writes=[hid])
                    S.barrier([cg, cu, tp, hT] + ab + wblk)
            with contextlib.ExitStack() as es1:
                sb1 = lambda name, shape, dt: Buf(es1.enter_context(nc.sbuf_tensor(f"sb_{name}_{pi}", list(shape), dt)), f"{name}_{pi}")
                wout = sb1("wout", [128, NFC, D], BF16)
                wout_v = w_out_d.rearrange("(c p) n -> p c n", p=128)
                for j in range(NFC):
                    for h4 in range(4):
                        st = stage.next()
                        S.dma("sp" if h4 % 2 == 0 else "act", st.t[:, 0, :], wout_v[:, j, h4 * 256:(h4 + 1) * 256], writes=[st])
                        S.op("pool", lambda: nc.gpsimd.tensor_copy(wout.t[:, j, h4 * 256:(h4 + 1) * 256], st.t[:, 0, :]), reads=[st], writes=[wout])
                xms = [sb1(f"xm3{i}", [128, D], F32) for i in range(2)]
                xos = [sb1(f"xo3{i}", [128, D], F32) for i in range(2)]
                k3 = 0
                for (col0, P, r, acol, orow) in tiles:
                    if acol is None:
                        continue
                    xm = xms[k3 % 2]
                    xo = xos[k3 % 2]
                    k3 += 1
                    S.dma("sp", xm.t[:], xmid_d[col0:col0 + 128, :], reads=[xmid_tok], writes=[xm])
                    y = pY.next()
                    for half in range(2):
                        for j in range(NFC):
                            S.op("pe", lambda: nc.tensor.matmul(y.t[:, half * 512:(half + 1) * 512], lhsT=hid.t[:, j, acol:acol + 128], rhs=wout.t[:, j, half * 512:(half + 1) * 512],
                                                                start=(j == 0), stop=(j == NFC - 1)), reads=[hid, wout], writes=[y])
                    ss = N.ss.next()
                    sq = N.sq.next()
                    emit_rstd(S, nc, y.t[:, :], y, 128, ss, sq, epsc)
                    S.op("dve", lambda: nc.vector.scalar_tensor_tensor(out=xo.t[:], in0=y.t[:], scalar=ss.t[:, 0:1], in1=GB.t[:, 2 + r, :], op0=ALU.mult, op1=ALU.mult),
                         reads=[y, ss, GB], writes=[xo])
                    S.op("pool", lambda: nc.gpsimd.tensor_tensor(out=xo.t[:], in0=xo.t[:], in1=xm.t[:], op=ALU.add), reads=[xo, xm], writes=[xo])
                    S.dma("act", xo_d[orow:orow + 128, :], xo.t[:], reads=[xo], writes=[], out=True)
                S.barrier([wout, hid] + xms + xos)
    S.finish()
    return nc


GELU_C = 1.5957691216057308


def build_ME(n_lat=64, n_ctx=2, n_b=2):
    nc = bass.Bass("TRN2", target_bir_lowering=False)
    TPB = n_lat + n_ctx
    NT = n_b * TPB
    dr = lambda name, shape, dt=F32, kind="ExternalInput": nc.dram_tensor(name, list(shape), dt, kind=kind).ap()
    x_d = dr("x", [NT * 128, D])
    wqk_d = dr("wqk", [D, 128])
    wgu_d = dr("wgu", [D, 576])
    wv_d = dr("wv", [D, 64])
    wsT_d = dr("wsT", [128, 128])
    bs_d = dr("bs", [128, 1])
    bias_d = dr("bias", [5, 128, 576])
    cols_d = dr("cols", [128, 1 + 2 * (n_b + 1), 8])
    ident_d = dr("ident", [128, 128])
    out_d = dr("out", [NT * 128, 128], BF16, kind="ExternalOutput")
    S = Sched(nc)
    ident_f = S.sb("ident_f", [128, 128], F32)
    ident = S.sb("ident_b", [128, 128], BF16)
    S.dma("sp", ident_f.t[:], ident_d, writes=[ident_f])
    S.op("dve", lambda: nc.vector.tensor_copy(ident.t[:], ident_f.t[:]), reads=[ident_f], writes=[ident])
    epsc = S.sb("epsc", [128, 1], F32)
    S.op("dve", lambda: nc.vector.memset(epsc.t[:], EPS), writes=[epsc])
    NR = n_b + 1
    cols = S.sb("cols", [128, 1 + 2 * NR, 8], F32)
    S.dma("sp", cols.t[:], cols_d, writes=[cols])
    gain = S.sb("gain", [128, NR, 8], F32)
    shift = S.sb("shift", [128, NR, 8], F32)
    for r in range(NR):
        S.op("dve", lambda: nc.vector.scalar_tensor_tensor(out=gain.t[:, r, :], in0=cols.t[:, 1 + r, :], scalar=1.0, in1=cols.t[:, 0, :], op0=ALU.add, op1=ALU.mult),
             reads=[cols], writes=[gain])
        S.op("dve", lambda: nc.vector.tensor_copy(shift.t[:, r, :], cols.t[:, 1 + NR + r, :]), reads=[cols], writes=[shift])
    stage = S.sb("stage", [128, 8, 576], F32)
    wqk = S.sb("wqk", [128, 8, 128], BF16)
    wgu = S.sb("wgu", [128, 8, 576], BF16)
    wv = S.sb("wv", [128, 8, 64], BF16)
    for (wb, wd, n) in ((wqk, wqk_d, 128), (wgu, wgu_d, 576), (wv, wv_d, 64)):
        S.dma("sp", stage.t[:, :, :n], wd.rearrange("(c p) n -> p c n", p=128), writes=[stage])
        S.op("dve", lambda: nc.vector.tensor_copy(wb.t[:], stage.t[:, :, :n]), reads=[stage], writes=[wb])
    wsT = S.sb("wsT", [128, 128], BF16)
    S.dma("sp", stage.t[:, 0, :128], wsT_d, writes=[stage])
    S.op("dve", lambda: nc.vector.tensor_copy(wsT.t[:], stage.t[:, 0, :128]), reads=[stage], writes=[wsT])
    bs = S.sb("bs", [128, 1], F32)
    S.dma("sp", bs.t[:], bs_d, writes=[bs])
    biasT = S.sb("biasT", [128, 5, 576], F32)
    for k in range(5):
        S.dma("act", biasT.t[:, k, :], bias_d[k], writes=[biasT])
    qT = S.sb("qT", [64, NT * 128], BF16)
    kT = S.sb("kT", [64, NT * 128], BF16)
    vA = S.sb("vA", [128, NT, 64], BF16)
    oA = S.sb("oA", [128, NT, 128], BF16)
    with S.scope() as esA:
        def sbA(name, shape, dt):
            return Buf(esA.enter_context(nc.sbuf_tensor("sbA_" + name, list(shape), dt)), name)
        def psA(name, shape, dt=F32):
            return Buf(esA.enter_context(nc.psum_tensor("psA_" + name, list(shape), dt)), name)
        N = NormCtx(S, ident, epsc)
        xts = [sbA(f"xt{i}", [128, D], F32) for i in range(2)]
        hTs = [sbA(f"hT{i}", [128, 8, 128], BF16) for i in range(2)]
        pQK = psA("pQK", [64, 2, 128])
        pGU = psA("pGU", [128, 1024])
        pSG = psA("pSG", [128, 128])
        xg = sbA("xg", [128, 576], F32)
        t1 = sbA("t1", [128, 576], F32)
        gl = sbA("gl", [128, 576], F32)
        junk = sbA("junk", [128, 512], F32)
        st = sbA("st", [128, 4], F32)
        vn = sbA("vn", [128, 64], BF16)
        for ti in range(NT):
            b, tl = divmod(ti, TPB)
            r = b if tl < n_lat else n_b
            xt = xts[ti % 2]
            hT = hTs[ti % 2]
            S.dma("sp" if ti % 2 == 0 else "act", xt.t[:], x_d[ti * 128:(ti + 1) * 128, :], writes=[xt])
            emit_norm_T(S, nc, N, xt, 128, gain, shift, r, hT, 0)
            for w in range(2):
                for c in range(8):
                    S.op("pe", lambda: nc.tensor.matmul(pQK.t[:, w, :], lhsT=wqk.t[:, c, w * 64:(w + 1) * 64], rhs=hT.t[:, c, :], start=(c == 0), stop=(c == 7)), reads=[wqk, hT], writes=[pQK])
            S.op("act", lambda: nc.scalar.activation(out=qT.t[:, ti * 128:(ti + 1) * 128], in_=pQK.t[:, 0, :], func=AF.Copy, scale=0.125), reads=[pQK], writes=[qT])
            S.op("act", lambda: nc.scalar.copy(kT.t[:, ti * 128:(ti + 1) * 128], pQK.t[:, 1, :]), reads=[pQK], writes=[kT])
            for (o0, n, wb, wo0) in ((0, 512, wgu, 0), (512, 64, wgu, 512), (576, 64, wv, 0)):
                for c in range(8):
                    S.op("pe", lambda: nc.tensor.matmul(pGU.t[:, o0:o0 + n], lhsT=hT.t[:, c, :], rhs=wb.t[:, c, wo0:wo0 + n], start=(c == 0), stop=(c == 7)), reads=[wb, hT], writes=[pGU])
            S.op("act", lambda: nc.scalar.copy(vA.t[:, ti, :], pGU.t[:, 576:640]), reads=[pGU], writes=[vA])
            S.op("act", lambda: nc.scalar.copy(xg.t[:], pGU.t[:, 0:576]), reads=[pGU], writes=[xg])
            S.op("dve", lambda: nc.vector.tensor_tensor(out=t1.t[:], in0=xg.t[:], in1=xg.t[:], op=ALU.mult), reads=[xg], writes=[t1])
            S.op("dve", lambda: nc.vector.tensor_scalar(t1.t[:], t1.t[:], 0.044715, 1.0, ALU.mult, ALU.add), reads=[t1], writes=[t1])
            S.op("pool", lambda: nc.gpsimd.tensor_tensor(out=t1.t[:], in0=t1.t[:], in1=xg.t[:], op=ALU.mult), reads=[t1, xg], writes=[t1])
            S.op("act", lambda: nc.scalar.activation(out=t1.t[:], in_=t1.t[:], func=AF.Sigmoid, scale=GELU_C), reads=[t1], writes=[t1])
            S.op("pool", lambda: nc.gpsimd.tensor_tensor(out=gl.t[:], in0=t1.t[:], in1=xg.t[:], op=ALU.mult), reads=[t1, xg], writes=[gl])
            S.op("act", lambda: nc.scalar.activation(out=junk.t[:], in_=gl.t[:, 0:512], func=AF.Identity, accum_out=st.t[:, 0:1]), reads=[gl], writes=[junk, st])
            S.op("act", lambda: nc.scalar.activation(out=junk.t[:], in_=gl.t[:, 0:512], func=AF.Square, accum_out=st.t[:, 1:2]), reads=[gl], writes=[junk, st])
            S.op("dve", lambda: nc.vector.tensor_scalar(st.t[:, 0:2], st.t[:, 0:2], 1.0 / 512, None, ALU.mult), reads=[st], writes=[st])
            S.op("dve", lambda: nc.vector.tensor_tensor(out=st.t[:, 2:3], in0=st.t[:, 0:1], in1=st.t[:, 0:1], op=ALU.mult), reads=[st], writes=[st])
            S.op("dve", lambda: nc.vector.tensor_tensor(out=st.t[:, 2:3], in0=st.t[:, 1:2], in1=st.t[:, 2:3], op=ALU.subtract), reads=[st], writes=[st])
            S.op("act", lambda: nc.scalar.activation(out=st.t[:, 2:3], in_=st.t[:, 2:3], func=AF.Sqrt, bias=epsc.t[:], scale=1.0), reads=[st, epsc], writes=[st])
            S.op("dve", lambda: nc.vector.reciprocal(st.t[:, 2:3], st.t[:, 2:3]), reads=[st], writes=[st])
            S.op("dve", lambda: nc.vector.tensor_scalar(vn.t[:], gl.t[:, 0:64], st.t[:, 0:1], st.t[:, 2:3], ALU.subtract, ALU.mult), reads=[gl, st], writes=[vn])
            S.op("pe", lambda: nc.tensor.matmul(pSG.t[:, 0:64], lhsT=wsT.t[:], rhs=vn.t[:], start=True, stop=True), reads=[wsT, vn], writes=[pSG])
            S.op("dve", lambda: nc.vector.scalar_tensor_tensor(out=oA.t[:, ti, 64:128], in0=pSG.t[:, 0:64], scalar=bs.t[:, 0:1], in1=gl.t[:, 512:576], op0=ALU.add, op1=ALU.mult),
                 reads=[pSG, bs, gl], writes=[oA])
        S.barrier(xts + hTs + [xg, t1, gl, junk, st, vn, pQK, pGU, pSG] + N.sq.bufs + N.ss.bufs + N.xn.bufs + N.pT.bufs)
    with contextlib.ExitStack() as esB:
        def sbB(name, shape, dt):
            return Buf(esB.enter_context(nc.sbuf_tensor("sbB_" + name, list(shape), dt)), name)
        def psB(name, shape, dt=F32):
            return Buf(esB.enter_context(nc.psum_tensor("psB_" + name, list(shape), dt)), name)
        pSs = [psB(f"pS{i}", [128, 1024]) for i in range(2)]
        pPTs = [psB(f"pPT{i}", [128, 7, 128], BF16) for i in range(2)]
        pOs = [psB(f"pO{i}", [128, 64]) for i in range(2)]
        Ts = [sbB(f"T{i}", [128, 832], F32) for i in range(2)]
        Ps = [sbB(f"P{i}", [128, 832], BF16) for i in range(2)]
        PTs = [sbB(f"PT{i}", [128, 7, 128], BF16) for i in range(2)]
        mxs = [sbB(f"mx{i}", [128, 2], F32) for i in range(2)]
        step = 0
        for b in range(n_b):
            base = b * TPB
            ctx_tiles = [base + n_lat + j for j in range(n_ctx)]
            for tl in range(TPB):
                ti = base + tl
                pS, pPT, pO, T, P, PT, mx = [x[step % 2] for x in (pSs, pPTs, pOs, Ts, Ps, PTs, mxs)]
                step += 1
                if tl < n_lat:
                    tw = min(max(tl - 2, 0), n_lat - 4)
                    full = [base + tw + k for k in range(4)]
                    extra = (base + tw + 4) if (2 <= tl <= n_lat - 3) else None
                    cls = 0 if tl == 0 else 1 if tl == 1 else 3 if tl == n_lat - 2 else 4 if tl == n_lat - 1 else 2
                    nnb = 512 + (64 if extra is not None else 0)
                else:
                    full, extra, cls, nnb = [], None, None, 0
                ncx = n_ctx * 128
                W = nnb + ncx
                qs = qT.t[:, ti * 128:(ti + 1) * 128]
                if full:
                    S.op("pe", lambda: nc.tensor.matmul(pS.t[:, 0:512], lhsT=qs, rhs=kT.t[:, full[0] * 128:(full[0] + 4) * 128], start=True, stop=True), reads=[qT, kT], writes=[pS])
                    if extra is not None:
                        S.op("pe", lambda: nc.tensor.matmul(pS.t[:, 512:576], lhsT=qs, rhs=kT.t[:, extra * 128:extra * 128 + 64], start=True, stop=True), reads=[qT, kT], writes=[pS])
                c0 = 512 + (64 if extra is not None else 0) if full else 0
                S.op("pe", lambda: nc.tensor.matmul(pS.t[:, c0:c0 + ncx], lhsT=qs, rhs=kT.t[:, ctx_tiles[0] * 128:ctx_tiles[0] * 128 + ncx], start=True, stop=True), reads=[qT, kT], writes=[pS])
                if full:
                    S.op("dve", lambda: nc.vector.tensor_tensor(out=T.t[:, 0:nnb], in0=pS.t[:, 0:nnb], in1=biasT.t[:, cls, 0:nnb], op=ALU.add), reads=[pS, biasT], writes=[T])
                S.op("act", lambda: nc.scalar.copy(T.t[:, nnb:W], pS.t[:, c0:c0 + ncx]), reads=[pS], writes=[T])
                S.op("dve", lambda: nc.vector.tensor_reduce(out=mx.t[:, 0:1], in_=T.t[:, 0:W], axis=AX.X, op=ALU.max), reads=[T], writes=[mx])
                S.op("dve", lambda: nc.vector.tensor_scalar(mx.t[:, 0:1], mx.t[:, 0:1], -1.0, None, ALU.mult), reads=[mx], writes=[mx])
                S.op("act", lambda: nc.scalar.activation(out=P.t[:, 0:W], in_=T.t[:, 0:W], func=AF.Exp, bias=mx.t[:, 0:1], scale=1.0, accum_out=mx.t[:, 1:2]), reads=[T, mx], writes=[P, mx])
                S.op("dve", lambda: nc.vector.reciprocal(mx.t[:, 1:2], mx.t[:, 1:2]), reads=[mx], writes=[mx])
                chunks = [(k * 128, 128, full[k]) for k in range(len(full))]
                if extra is not None:
                    chunks.append((512, 64, extra))
                chunks += [(nnb + j * 128, 128, ctx_tiles[j]) for j in range(n_ctx)]
                for ci, (pc0, n, vt) in enumerate(chunks):
                    S.op("pe", lambda: nc.tensor.transpose(pPT.t[:n, ci, :], P.t[:, pc0:pc0 + n], ident.t[:]), reads=[P, ident], writes=[pPT])
                ncf = len(chunks)
                S.op("dve", lambda: nc.vector.tensor_copy(PT.t[:, 0:ncf, :], pPT.t[:, 0:ncf, :]), reads=[pPT], writes=[PT])
                for ci, (pc0, n, vt) in enumerate(chunks):
                    S.op("pe", lambda: nc.tensor.matmul(pO.t[:, :], lhsT=PT.t[:n, ci, :], rhs=vA.t[:n, vt, :], start=(ci == 0), stop=(ci == ncf - 1)), reads=[PT, vA], writes=[pO])
                S.op("act", lambda: nc.scalar.activation(out=oA.t[:, ti, 0:64], in_=pO.t[:, :], func=AF.Copy, scale=mx.t[:, 1:2]), reads=[pO, mx], writes=[oA])
        S.barrier(pSs + pPTs + pOs + Ts + Ps + PTs + mxs)
    out_v = out_d.rearrange("(t p) n -> p t n", p=128)
    nchunk = 4
    per = (NT + nchunk - 1) // nchunk
    for k in range(nchunk):
        a, bnd = k * per, min(NT, (k + 1) * per)
        if a < bnd:
            S.dma("sp" if k % 2 == 0 else "act", out_v[:, a:bnd, :], oA.t[:, a:bnd, :], reads=[oA], writes=[], out=True)
    S.finish()
    return nc


def _fm(v):
    return np.ascontiguousarray(np.asarray(v, np.float32).reshape(-1, 128).T)


def na_bias_tables(rpb_h, n_lat):
    rows = 2 * n_lat
    out = np.full((5, 128, 576), -1e30, np.float32)
    reps = [0, 1, 2, n_lat - 2, n_lat - 1]
    qi = np.arange(128)
    kj = np.arange(576)
    for cls, tl in enumerate(reps):
        tw = min(max(tl - 2, 0), n_lat - 4)
        r = 2 * tl + qi // 64
        qc = qi % 64
        r0 = np.clip(r - 4, 0, rows - 8)
        c0 = np.clip(qc - 8, 0, 64 - 16)
        kr = 2 * tw + kj // 64
        kc = kj % 64
        valid = ((kr[None, :] >= r0[:, None]) & (kr[None, :] < r0[:, None] + 8)
                 & (kc[None, :] >= c0[:, None]) & (kc[None, :] < c0[:, None] + 16))
        dr_ = np.clip(kr[None, :] - r[:, None] + 7, 0, 14)
        dc_ = np.clip(kc[None, :] - qc[:, None], -15, 15) + 15
        out[cls] = np.where(valid, rpb_h[dr_, dc_], np.float32(-1e30))
    return out


def prep_ME(h, x, xc, w_in, rpb, w_s, b_s, n0, sc1, sh1):
    n_b = x.shape[0]
    n_lat = x.shape[1] // 128
    x_all = np.concatenate([np.concatenate([x[b], xc[b]], 0) for b in range(n_b)], 0)
    hs = slice(h * 64, (h + 1) * 64)
    g_cols = w_in[:, 2048:2560]
    order = list(range(h * 64, (h + 1) * 64)) + [c for c in range(512) if not (h * 64 <= c < (h + 1) * 64)]
    ins = {
        "x": np.ascontiguousarray(x_all, np.float32),
        "wqk": np.ascontiguousarray(np.concatenate([w_in[:, 0:512][:, hs], w_in[:, 512:1024][:, hs]], 1)),
        "wgu": np.ascontiguousarray(np.concatenate([g_cols[:, order], w_in[:, 1536:2048][:, hs]], 1)),
        "wv": np.ascontiguousarray(w_in[:, 1024:1536][:, hs]),
        "wsT": np.ascontiguousarray(w_s[h].T),
        "bs": np.ascontiguousarray(b_s[h][:, None]),
        "bias": na_bias_tables(rpb[h], n_lat),
        "cols": np.ascontiguousarray(np.stack([_fm(n0)] + [_fm(v) for v in sc1] + [_fm(v) for v in sh1], 1)),
        "ident": np.eye(128, dtype=np.float32),
    }
    return ins, None


def build_MO(n_lat=64, n_ctx=2, n_b=2, dbg=None, lvl=99):
    nc = bass.Bass("TRN2", target_bir_lowering=False)
    TPB = n_lat + n_ctx
    NT = n_b * TPB
    NR = n_b + 1
    dr = lambda name, shape, dt=F32, kind="ExternalInput": nc.dram_tensor(name, list(shape), dt, kind=kind).ap()
    x_d = dr("x", [NT * 128, D])
    w_d = dr("w", [D, 768])
    cols_d = dr("cols", [128, 1 + 2 * NR, 8])
    rope_d = dr("rope", [n_lat, 128, 256])
    lg_d = dr("lg", [128, 2])
    cE_d = dr("cE", [2, 128, 128])
    cM_d = dr("cM", [2, 128, 128])
    cQ_d = dr("cQ", [2, 128, 128])
    cK_d = dr("cK", [128, 2])
    gn_d = dr("gn", [128, 256])
    ident_d = dr("ident", [128, 128])
    out_d = dr("out", [NT * 128, 256], BF16, kind="ExternalOutput")
    gS_d = dr("gS", [NT * 128, 256], kind="Internal")
    oF_d = dr("oF", [NT * 128, 256], kind="Internal")
    S = Sched(nc)
    ident_f = S.sb("ident_f", [128, 128], F32)
    ident = S.sb("ident_b", [128, 128], BF16)
    S.dma("sp", ident_f.t[:], ident_d, writes=[ident_f])
    S.op("dve", lambda: nc.vector.tensor_copy(ident.t[:], ident_f.t[:]), reads=[ident_f], writes=[ident])
    epsc = S.sb("epsc", [128, 1], F32)
    S.op("dve", lambda: nc.vector.memset(epsc.t[:], EPS), writes=[epsc])
    cols = S.sb("cols", [128, 1 + 2 * NR, 8], F32)
    S.dma("sp", cols.t[:], cols_d, writes=[cols])
    gain = S.sb("gain", [128, NR, 8], F32)
    shift = S.sb("shift", [128, NR, 8], F32)
    for r in range(NR):
        S.op("dve", lambda: nc.vector.scalar_tensor_tensor(out=gain.t[:, r, :], in0=cols.t[:, 1 + r, :], scalar=1.0, in1=cols.t[:, 0, :], op0=ALU.add, op1=ALU.mult),
             reads=[cols], writes=[gain])
        S.op("dve", lambda: nc.vector.tensor_copy(shift.t[:, r, :], cols.t[:, 1 + NR + r, :]), reads=[cols], writes=[shift])
    wb = S.sb("wb", [128, 8, 768], BF16)
    with S.scope():
        stage = S.sb("stage", [128, 8, 768], F32)
        S.dma("sp", stage.t[:], w_d.rearrange("(c p) n -> p c n", p=128), writes=[stage])
        S.op("dve", lambda: nc.vector.tensor_copy(wb.t[:], stage.t[:]), reads=[stage], writes=[wb])
        S.barrier([stage])
    lg = S.sb("lg", [128, 2], F32)
    S.dma("sp", lg.t[:], lg_d, writes=[lg])
    DT = S.sb("DT", [128, 2, 128], F32)
    DQ = S.sb("DQ", [128, 2, 128], F32)
    DK = S.sb("DK", [128, 2], F32)
    GC = S.sb("GC", [128, 2], F32)
    gn = S.sb("gn", [128, 256], F32)
    S.dma("sp", gn.t[:], gn_d, writes=[gn])
    with S.scope():
        cE = S.sb("cE", [128, 2, 128], F32)
        cM = S.sb("cM", [128, 2, 128], F32)
        cQ = S.sb("cQ", [128, 2, 128], F32)
        cK = S.sb("cK", [128, 2], F32)
        c128 = S.sb("c128", [128, 1], F32)
        S.op("dve", lambda: nc.vector.memset(c128.t[:], 128.0), writes=[c128])
        S.dma("sp", cK.t[:], cK_d, writes=[cK])
        for d in range(2):
            S.dma("sp", cE.t[:, d, :], cE_d[d], writes=[cE])
            S.dma("act", cM.t[:, d, :], cM_d[d], writes=[cM])
            S.dma("sp", cQ.t[:, d, :], cQ_d[d], writes=[cQ])
        for d in range(2):
            S.op("act", lambda: nc.scalar.activation(out=DT.t[:, d, :], in_=cE.t[:, d, :], func=AF.Exp, scale=lg.t[:, d:d + 1]), reads=[cE, lg], writes=[DT])
            S.op("dve", lambda: nc.vector.tensor_tensor(out=DT.t[:, d, :], in0=DT.t[:, d, :], in1=cM.t[:, d, :], op=ALU.mult), reads=[DT, cM], writes=[DT])
            S.op("act", lambda: nc.scalar.activation(out=DQ.t[:, d, :], in_=cQ.t[:, d, :], func=AF.Exp, scale=lg.t[:, d:d + 1]), reads=[cQ, lg], writes=[DQ])
            S.op("act", lambda: nc.scalar.activation(out=DK.t[:, d:d + 1], in_=cK.t[:, d:d + 1], func=AF.Exp, scale=lg.t[:, d:d + 1]), reads=[cK, lg], writes=[DK])
            S.op("act", lambda: nc.scalar.activation(out=GC.t[:, d:d + 1], in_=c128.t[:], func=AF.Exp, scale=lg.t[:, d:d + 1]), reads=[c128, lg], writes=[GC])
        S.barrier([cE, cM, cQ, cK, c128])
    if dbg == "P":
        S.barrier([DT, DQ, DK, GC, gn, wb])
        S.finish()
        return nc
    gS_tok = S.shared_toks("gS", NT, 4)
    oF_tok = S.shared_toks("oF", NT, 4)
    for b in range(n_b):
        base = b * TPB
        with S.scope():
            qT = S.sb(f"qT{b}", [128, TPB * 128], BF16)
            kT = S.sb(f"kT{b}", [128, TPB * 128], BF16)
            kK = S.sb(f"kK{b}", [128, TPB, 128], BF16)
            vA = S.sb(f"vA{b}", [128, TPB, 256], BF16)
            with S.scope():
                N = NormCtx(S, ident, epsc)
                xts = [S.sb(f"xt{b}_{i}", [128, D], F32) for i in range(2)]
                hTs = [S.sb(f"hT{b}_{i}", [128, 8, 128], BF16) for i in range(2)]
                pIn = S.ps(f"pIn{b}", [128, 1024])
                pT2 = S.ps(f"pT2{b}", [128, 2, 128], BF16)
                qk = S.sb(f"qk{b}", [128, 256], F32)
                rp = [S.sb(f"rp{b}_{i}", [128, 256], F32) for i in range(2)]
                ta = S.sb(f"ta{b}", [128, 2, 64], F32)
                tb = S.sb(f"tb{b}", [128, 2, 64], F32)
                rq = S.sb(f"rq{b}", [128, 256], BF16)
                gs = [S.sb(f"gs{b}_{i}", [128, 256], F32) for i in range(2)]
                for tl in range(TPB):
                    ti = base + tl
                    is_lat = tl < n_lat
                    r = b if is_lat else n_b
                    xt = xts[tl % 2]
                    hT = hTs[tl % 2]
                    S.dma("sp" if tl % 2 == 0 else "act", xt.t[:], x_d[ti * 128:(ti + 1) * 128, :], writes=[xt])
                    if lvl < 1:
                        continue
                    emit_norm_T(S, nc, N, xt, 128, gain, shift, r, hT, 0)
                    if lvl < 2:
                        continue
                    for (o0, n) in ((0, 512), (512, 256)):
                        for c in range(8):
                            S.op("pe", lambda: nc.tensor.matmul(pIn.t[:, o0:o0 + n], lhsT=hT.t[:, c, :], rhs=wb.t[:, c, o0:o0 + n], start=(c == 0), stop=(c == 7)), reads=[wb, hT], writes=[pIn])
                    S.op("act", lambda: nc.scalar.copy(vA.t[:, tl, :], pIn.t[:, 256:512]), reads=[pIn], writes=[vA])
                    if lvl < 3:
                        continue
                    g_ = gs[tl % 2]
                    S.op("act", lambda: nc.scalar.activation(out=g_.t[:], in_=pIn.t[:, 512:768], func=AF.Silu), reads=[pIn], writes=[g_])
                    if dbg != "A2":
                        S.dma("act", gS_d[ti * 128:(ti + 1) * 128, :], g_.t[:], reads=[g_], writes=[gS_tok[ti]])
                    if is_lat and dbg != "A1":
                        S.op("act", lambda: nc.scalar.copy(qk.t[:, 0:128], pIn.t[:, 0:128]), reads=[pIn], writes=[qk])
                        S.op("act", lambda: nc.scalar.activation(out=qk.t[:, 128:256], in_=pIn.t[:, 128:256], func=AF.Copy, scale=128.0 ** -0.5), reads=[pIn], writes=[qk])
                        rpt = rp[tl % 2]
                        S.dma("sp", rpt.t[:], rope_d[tl], writes=[rpt])
                        q4 = qk.t[:].rearrange("p (a h c) -> p a h c", a=2, h=2)
                        o4 = rq.t[:].rearrange("p (a h c) -> p a h c", a=2, h=2)
                        cs = rpt.t[:, 0:128].rearrange("p (a c) -> p a c", a=2)
                        sn = rpt.t[:, 128:256].rearrange("p (a c) -> p a c", a=2)
                        S.op("dve", lambda: nc.vector.tensor_tensor(out=ta.t[:], in0=q4[:, :, 0, :], in1=cs, op=ALU.mult), reads=[qk, rpt], writes=[ta])
                        S.op("pool", lambda: nc.gpsimd.tensor_tensor(out=tb.t[:], in0=q4[:, :, 1, :], in1=sn, op=ALU.mult), reads=[qk, rpt], writes=[tb])
                        S.op("dve", lambda: nc.vector.tensor_tensor(out=o4[:, :, 0, :], in0=ta.t[:], in1=tb.t[:], op=ALU.subtract), reads=[ta, tb], writes=[rq])
                        S.op("dve", lambda: nc.vector.tensor_tensor(out=ta.t[:], in0=q4[:, :, 0, :], in1=sn, op=ALU.mult), reads=[qk, rpt, rq], writes=[ta])
                        S.op("pool", lambda: nc.gpsimd.tensor_tensor(out=tb.t[:], in0=q4[:, :, 1, :], in1=cs, op=ALU.mult), reads=[qk, rpt, rq], writes=[tb])
                        S.op("dve", lambda: nc.vector.tensor_tensor(out=o4[:, :, 1, :], in0=ta.t[:], in1=tb.t[:], op=ALU.add), reads=[ta, tb], writes=[rq])
                    else:
                        S.op("act", lambda: nc.scalar.copy(rq.t[:, 0:128], pIn.t[:, 0:128]), reads=[pIn], writes=[rq])
                        S.op("act", lambda: nc.scalar.activation(out=rq.t[:, 128:256], in_=pIn.t[:, 128:256], func=AF.Copy, scale=128.0 ** -0.5), reads=[pIn], writes=[rq])
                    if lvl < 4:
                        continue
                    S.op("pool", lambda: nc.gpsimd.tensor_copy(kK.t[:, tl, :], rq.t[:, 128:256]), reads=[rq], writes=[kK])
                    if lvl < 5:
                        continue
                    for w in range(2):
                        S.op("pe", lambda: nc.tensor.transpose(pT2.t[:, w, :], rq.t[:, w * 128:(w + 1) * 128], ident.t[:]), reads=[rq, ident], writes=[pT2])
                    if lvl < 6:
                        continue
                    S.op("dve", lambda: nc.vector.tensor_copy(qT.t[:, tl * 128:(tl + 1) * 128], pT2.t[:, 0, :]), reads=[pT2], writes=[qT])
                    if lvl < 7:
                        continue
                    S.op("dve", lambda: nc.vector.tensor_copy(kT.t[:, tl * 128:(tl + 1) * 128], pT2.t[:, 1, :]), reads=[pT2], writes=[kT])
                S.barrier(xts + hTs + [pIn, pT2, qk, ta, tb, rq] + rp + gs + N.sq.bufs + N.ss.bufs + N.xn.bufs + N.pT.bufs)
            if dbg in ("A", "A1", "A2"):
                S.barrier(gS_tok + [qT, kT, kK, vA])
                continue
            with S.scope():
                Sf = [S.sb(f"Sf{b}_{d}", [128, 256], F32) for d in range(2)]
                Sb = [S.sb(f"Sb{b}_{d}", [128, 256], BF16) for d in range(2)]
                for d in range(2):
                    S.op("dve", lambda: nc.vector.memset(Sf[d].t[:], 0.0), writes=[Sf[d]])
                    S.op("dve", lambda: nc.vector.memset(Sb[d].t[:], 0.0), writes=[Sb[d]])
                pST = [S.ps(f"pST{b}_{i}", [128, 128]) for i in range(2)]
                pOo = [S.ps(f"pOo{b}_{i}", [128, 256]) for i in range(2)]
                pDS = [S.ps(f"pDS{b}_{i}", [128, 256]) for i in range(2)]
                sTm = [S.sb(f"sTm{b}_{i}", [128, 128], BF16) for i in range(2)]
                qd = [S.sb(f"qd{b}_{i}", [128, 128], BF16) for i in range(2)]
                kd = [S.sb(f"kd{b}_{i}", [128, 128], BF16) for i in range(2)]
                of_ = [S.sb(f"of{b}_{i}", [128, 256], F32) for i in range(2)]
                ofl = [S.sb(f"ofl{b}_{i}", [128, 256], F32) for i in range(2)]
                gl_ = [S.sb(f"gl{b}_{i}", [128, 256], F32) for i in range(2)]
                junk = S.sb(f"junk{b}", [128, 256], F32)
                st = [S.sb(f"st{b}_{i}", [128, 4], F32) for i in range(2)]
                ot = [S.sb(f"ot{b}_{i}", [128, 256], BF16) for i in range(2)]
                order = [list(range(n_lat, TPB)) + list(range(n_lat)),
                         list(range(TPB - 1, n_lat - 1, -1)) + list(range(n_lat - 1, -1, -1))]
                for i in range(TPB):
                    for d in range(2):
                        tl = order[d][i]
                        ti = base + tl
                        k2 = d
                        cs_ = slice(tl * 128, (tl + 1) * 128)
                        S.op("pe", lambda: nc.tensor.matmul(pST[d].t[:], lhsT=kT.t[:, cs_], rhs=qT.t[:, cs_], start=True, stop=True), reads=[kT, qT], writes=[pST[d]])
                        S.op("dve", lambda: nc.vector.tensor_tensor(out=sTm[d].t[:], in0=pST[d].t[:], in1=DT.t[:, d, :], op=ALU.mult), reads=[pST[d], DT], writes=[sTm[d]])
                        S.op("pool", lambda: nc.gpsimd.tensor_tensor(out=qd[d].t[:], in0=qT.t[:, cs_], in1=DQ.t[:, d, :], op=ALU.mult), reads=[qT, DQ], writes=[qd[d]])
                        S.op("pool", lambda: nc.gpsimd.tensor_scalar(kd[d].t[:], kK.t[:, tl, :], DK.t[:, d:d + 1], None, ALU.mult), reads=[kK, DK], writes=[kd[d]])
                        S.op("pe", lambda: nc.tensor.matmul(pOo[d].t[:], lhsT=sTm[d].t[:], rhs=vA.t[:, tl, :], start=True, stop=False), reads=[sTm[d], vA], writes=[pOo[d]])
                        S.op("pe", lambda: nc.tensor.matmul(pOo[d].t[:], lhsT=qd[d].t[:], rhs=Sb[d].t[:], start=False, stop=True), reads=[qd[d], Sb[d]], writes=[pOo[d]])
                        S.op("pe", lambda: nc.tensor.matmul(pDS[d].t[:], lhsT=kd[d].t[:], rhs=vA.t[:, tl, :], start=True, stop=True), reads=[kd[d], vA], writes=[pDS[d]])
                        S.op("dve", lambda: nc.vector.scalar_tensor_tensor(out=Sf[d].t[:], in0=Sf[d].t[:], scalar=GC.t[:, d:d + 1], in1=pDS[d].t[:], op0=ALU.mult, op1=ALU.add),
                             reads=[Sf[d], GC, pDS[d]], writes=[Sf[d]])
                        S.op("act", lambda: nc.scalar.copy(Sb[d].t[:], Sf[d].t[:]), reads=[Sf[d]], writes=[Sb[d]])
                        i_other = order[1 - d].index(tl)
                        if i < i_other or (i == i_other and d == 0):
                            o_ = of_[k2]
                            S.op("act", lambda: nc.scalar.copy(o_.t[:], pOo[d].t[:]), reads=[pOo[d]], writes=[o_])
                            S.dma("sp", oF_d[ti * 128:(ti + 1) * 128, :], o_.t[:], reads=[o_], writes=[oF_tok[ti]])
                        else:
                            o_ = ofl[k2]
                            g_ = gl_[k2]
                            s_ = st[k2]
                            S.dma("sp", o_.t[:], oF_d[ti * 128:(ti + 1) * 128, :], reads=[oF_tok[ti]], writes=[o_])
                            S.dma("act", g_.t[:], gS_d[ti * 128:(ti + 1) * 128, :], reads=[gS_tok[ti]], writes=[g_])
                            S.op("dve", lambda: nc.vector.tensor_tensor(out=o_.t[:], in0=pOo[d].t[:], in1=o_.t[:], op=ALU.add), reads=[pOo[d], o_], writes=[o_])
                            S.op("act", lambda: nc.scalar.activation(out=junk.t[:], in_=o_.t[:], func=AF.Identity, accum_out=s_.t[:, 0:1]), reads=[o_], writes=[junk, s_])
                            S.op("act", lambda: nc.scalar.activation(out=junk.t[:], in_=o_.t[:], func=AF.Square, accum_out=s_.t[:, 1:2]), reads=[o_], writes=[junk, s_])
                            S.op("dve", lambda: nc.vector.tensor_scalar(s_.t[:, 0:2], s_.t[:, 0:2], 1.0 / 256, None, ALU.mult), reads=[s_], writes=[s_])
                            S.op("dve", lambda: nc.vector.tensor_tensor(out=s_.t[:, 2:3], in0=s_.t[:, 0:1], in1=s_.t[:, 0:1], op=ALU.mult), reads=[s_], writes=[s_])
                            S.op("dve", lambda: nc.vector.tensor_tensor(out=s_.t[:, 2:3], in0=s_.t[:, 1:2], in1=s_.t[:, 2:3], op=ALU.subtract), reads=[s_], writes=[s_])
                            S.op("act", lambda: nc.scalar.activation(out=s_.t[:, 2:3], in_=s_.t[:, 2:3], func=AF.Sqrt, bias=epsc.t[:], scale=1.0), reads=[s_, epsc], writes=[s_])
                            S.op("dve", lambda: nc.vector.reciprocal(s_.t[:, 2:3], s_.t[:, 2:3]), reads=[s_], writes=[s_])
                            S.op("dve", lambda: nc.vector.tensor_scalar(o_.t[:], o_.t[:], s_.t[:, 0:1], s_.t[:, 2:3], ALU.subtract, ALU.mult), reads=[o_, s_], writes=[o_])
                            S.op("pool", lambda: nc.gpsimd.tensor_tensor(out=g_.t[:], in0=g_.t[:], in1=gn.t[:], op=ALU.mult), reads=[g_, gn], writes=[g_])
                            ob = ot[k2]
                            S.op("pool", lambda: nc.gpsimd.tensor_tensor(out=ob.t[:], in0=o_.t[:], in1=g_.t[:], op=ALU.mult), reads=[o_, g_], writes=[ob])
                            S.dma("act", out_d[ti * 128:(ti + 1) * 128, :], ob.t[:], reads=[ob], writes=[], out=True)
                S.barrier(Sf + Sb + pST + pOo + pDS + sTm + qd + kd + of_ + ofl + gl_ + [junk] + st + ot)
            S.barrier([qT, kT, kK, vA])
    S.finish()
    return nc


def prep_MO(h, x, xc, w_in, log_decay, gn_g, n0, sc1, sh1):
    n_b = x.shape[0]
    n_lat = x.shape[1] // 128
    x_all = np.concatenate([np.concatenate([x[b], xc[b]], 0) for b in range(n_b)], 0)
    w = np.concatenate([w_in[:, h * 128:(h + 1) * 128], w_in[:, 1024 + h * 128:1024 + (h + 1) * 128],
                        w_in[:, 2048 + h * 256:2048 + (h + 1) * 256], w_in[:, 4096 + h * 256:4096 + (h + 1) * 256]], 1)
    t = np.arange(n_lat * 128)
    row = (t // 64).astype(np.float32)
    col = (t % 64).astype(np.float32)
    inv = (10000.0 ** (-np.arange(32, dtype=np.float32) / 32)).astype(np.float32)
    ang = np.concatenate([row[:, None] * inv, col[:, None] * inv], -1).astype(np.float32)
    cos, sin = np.cos(ang).astype(np.float32), np.sin(ang).astype(np.float32)
    rope = np.concatenate([cos, cos, sin, sin], -1).reshape(n_lat, 128, 256)
    pos = np.arange(128, dtype=np.float32)
    kq = pos[None, :] - pos[:, None]
    cE = np.stack([np.where(kq >= 0, kq, 0), np.where(kq <= 0, -kq, 0)]).astype(np.float32)
    cM = np.stack([(kq >= 0), (kq <= 0)]).astype(np.float32)
    cQ = np.stack([np.broadcast_to(pos + 1, (128, 128)), np.broadcast_to(128 - pos, (128, 128))]).astype(np.float32)
    cK = np.stack([127 - pos, pos], 1).astype(np.float32)
    ins = {
        "x": np.ascontiguousarray(x_all, np.float32), "w": np.ascontiguousarray(w),
        "cols": np.ascontiguousarray(np.stack([_fm(n0)] + [_fm(v) for v in sc1] + [_fm(v) for v in sh1], 1)),
        "rope": np.ascontiguousarray(rope), "lg": np.ascontiguousarray(np.broadcast_to(log_decay[:, h].astype(np.float32), (128, 2))),
        "cE": np.ascontiguousarray(cE), "cM": np.ascontiguousarray(cM), "cQ": np.ascontiguousarray(cQ), "cK": np.ascontiguousarray(cK),
        "gn": np.ascontiguousarray(np.broadcast_to(gn_g[h * 256:(h + 1) * 256].astype(np.float32), (128, 256))),
        "ident": np.eye(128, dtype=np.float32),
    }
    return ins, None


def build_ADA():
    nc = bass.Bass("TRN2", target_bir_lowering=False)
    dr = lambda name, shape, dt=F32, kind="ExternalInput": nc.dram_tensor(name, list(shape), dt, kind=kind).ap()
    c_d = dr("cT", [128, 8, 3])
    w_d = dr("w", [4, D, 768])
    b_d = dr("b", [128, 4, 6])
    o_d = dr("out", [128, 4, 6, 3], kind="ExternalOutput")
    S = Sched(nc)
    cT = S.sb("cT", [128, 8, 3], F32)
    sc = S.sb("sc", [128, 8, 3], BF16)
    bb = S.sb("bb", [128, 4, 6], F32)
    ot = S.sb("ot", [128, 4, 6, 3], F32)
    S.dma("sp", cT.t[:], c_d, writes=[cT])
    S.dma("sp", bb.t[:], b_d, writes=[bb])
    S.op("act", lambda: nc.scalar.activation(out=sc.t[:], in_=cT.t[:], func=AF.Silu), reads=[cT], writes=[sc])
    stg = [S.sb(f"stg{i}", [128, 8, 768], F32) for i in range(2)]
    wl = [S.sb(f"wl{i}", [128, 8, 768], BF16) for i in range(2)]
    ps = [S.ps(f"ps{i}", [128, 4]) for i in range(2)]
    k = 0
    for l in range(4):
        st, w = stg[l % 2], wl[l % 2]
        S.dma("sp" if l % 2 == 0 else "act", st.t[:], w_d[l].rearrange("(c p) n -> p c n", p=128), writes=[st])
        S.op("dve" if l % 2 == 0 else "pool", lambda: (nc.vector if l % 2 == 0 else nc.gpsimd).tensor_copy(w.t[:], st.t[:]), reads=[st], writes=[w])
        for j in range(6):
            p = ps[k % 2]
            k += 1
            for c in range(8):
                S.op("pe", lambda: nc.tensor.matmul(p.t[:, 0:3], lhsT=w.t[:, c, j * 128:(j + 1) * 128], rhs=sc.t[:, c, :], start=(c == 0), stop=(c == 7)), reads=[w, sc], writes=[p])
            S.op("dve", lambda: nc.vector.tensor_scalar(ot.t[:, l, j, :], p.t[:, 0:3], bb.t[:, l, j:j + 1], None, ALU.add), reads=[p, bb], writes=[ot])
    S.dma("sp", o_d, ot.t[:], reads=[ot], writes=[], out=True)
    S.finish()
    return nc


_PROGS = {}
_DBG = {}


def _prog(key, fn):
    if key not in _PROGS:
        _PROGS[key] = fn()
    return _PROGS[key]


def _run(nc, in_maps):
    res = run_bass_kernel_spmd(nc, in_maps, core_ids=list(range(len(in_maps))))
    return res.results


def kernel_unfused(x, c, ctx, c_ctx, ada_w, ada_b, norm_g, hyb_w_in, na_rpb, sgu_w, sgu_b, hyb_w_out,
           ret_w_in, ret_log_decay, ret_gn_g, ret_w_out, ffn_w_in, ffn_conv_w, ffn_conv_b, ffn_w_out):
    import ml_dtypes
    f32 = lambda a: np.ascontiguousarray(np.asarray(a, dtype=np.float32))
    x, c, ctx, c_ctx, ada_w, ada_b, norm_g = map(f32, (x, c, ctx, c_ctx, ada_w, ada_b, norm_g))
    hyb_w_in, na_rpb, sgu_w, sgu_b, hyb_w_out = map(f32, (hyb_w_in, na_rpb, sgu_w, sgu_b, hyb_w_out))
    ret_w_in, ret_log_decay, ret_gn_g, ret_w_out = map(f32, (ret_w_in, ret_log_decay, ret_gn_g, ret_w_out))
    ffn_w_in, ffn_conv_w, ffn_conv_b, ffn_w_out = map(f32, (ffn_w_in, ffn_conv_w, ffn_conv_b, ffn_w_out))
    B, T, _ = x.shape
    L = ctx.shape[1]
    depth = ada_w.shape[0]
    ident = np.eye(128, dtype=np.float32)
    cvec = np.stack([c[0], c[1], c_ctx], 0)
    cT = np.ascontiguousarray(cvec.T.reshape(8, 128, 3).transpose(1, 0, 2))
    ada_maps = []
    for j in range(8):
        cs = slice(j * 768, (j + 1) * 768)
        ada_maps.append({"cT": cT, "w": np.ascontiguousarray(ada_w[:, :, cs]),
                         "b": np.ascontiguousarray(ada_b[:, cs].reshape(4, 6, 128).transpose(2, 0, 1))})
    res = _run(_prog("ada", build_ADA), ada_maps)
    mod = np.zeros((depth, 3, 6 * D), np.float32)
    for j in range(8):
        o = res[j]["out"]
        mod[:, :, j * 768:(j + 1) * 768] = o.transpose(1, 3, 2, 0).reshape(4, 3, 768)
    xcur = x.copy()
    ccur = ctx.copy()
    n_lat = T // 128
    n_ctx = L // 128
    TPB = n_lat + n_ctx
    for i in range(depth):
        j2 = i // 2
        sh1, sc1, g1, sh2, sc2, g2 = [mod[i][:, k * D:(k + 1) * D] for k in range(6)]
        if i % 2 == 0:
            maps = [prep_ME(h, xcur, ccur, hyb_w_in[j2], na_rpb[j2], sgu_w[j2], sgu_b[j2], norm_g[i, 0], sc1, sh1)[0] for h in range(8)]
            res = _run(_prog("me", lambda: build_ME(n_lat, n_ctx, B)), maps)
            KO = D
            o_full = np.zeros((B * TPB * 128, KO), ml_dtypes.bfloat16)
            for h in range(8):
                o = res[h]["out"]
                o_full[:, h * 64:(h + 1) * 64] = o[:, 0:64]
                o_full[:, 512 + h * 64:512 + (h + 1) * 64] = o[:, 64:128]
            w_o = hyb_w_out[j2]
        else:
            maps = [prep_MO(h, xcur, ccur, ret_w_in[j2], ret_log_decay[j2], ret_gn_g[j2], norm_g[i, 0], sc1, sh1)[0] for h in range(8)]
            res = _run(_prog("mo", lambda: build_MO(n_lat, n_ctx, B)), maps)
            KO = 2 * D
            o_full = np.zeros((B * TPB * 128, KO), ml_dtypes.bfloat16)
            for h in range(8):
                o_full[:, h * 256:(h + 1) * 256] = res[h]["out"]
            w_o = ret_w_out[j2]
        del maps, res
        o_full = o_full.reshape(B, TPB * 128, KO)
        passes = [(8, n_ctx), (8, 0)]
        bc = lambda v: np.ascontiguousarray(np.broadcast_to(v, (128, D)))
        convw = np.ascontiguousarray(ffn_conv_w[i].reshape(3, 2 * NFC, 128).transpose(2, 1, 0))
        convb = np.ascontiguousarray(ffn_conv_b[i].reshape(2 * NFC, 128).T)
        maps = []
        QT = T // 4
        for core in range(8):
            b, q = divmod(core, 4)
            rows = np.stack([bc(norm_g[i, 1]), bc(norm_g[i, 3]), bc(g1[b]), bc(g1[2]), bc(g2[b]), bc(g2[2])])
            cols = np.ascontiguousarray(np.stack([_fm(norm_g[i, 2]), _fm(sc2[b]), _fm(sc2[2]), _fm(sh2[b]), _fm(sh2[2])], 1))
            m = {"w_o": w_o, "w_in": ffn_w_in[i], "w_out": ffn_w_out[i], "convw": convw, "convb": convb,
                 "rows": rows, "cols": cols, "ident": ident}
            hm = np.zeros((128, 4), np.float32)
            for p in range(2):
                m0 = q * QT + p * (QT // 2)
                idx_main = np.arange(m0, m0 + QT // 2)
                hl, hr = m0 - 1, m0 + QT // 2
                hm[:, 2 * p] = 1.0 if hl >= 0 else 0.0
                hm[:, 2 * p + 1] = 1.0 if hr < T else 0.0
                idx = np.concatenate([idx_main, [max(hl, 0)], [min(hr, T - 1)]])
                o_rows = o_full[b, idx]
                x_rows = xcur[b, idx]
                if p == 0:
                    o_rows = np.concatenate([o_rows, o_full[b, T:T + L]], 0)
                    x_rows = np.concatenate([x_rows, ccur[b]], 0)
                m[f"oT{p}"] = np.ascontiguousarray(o_rows.T)
                m[f"x{p}"] = np.ascontiguousarray(x_rows)
            m["hmask"] = hm
            maps.append(m)
        res = _run(_prog(("f", KO), lambda: build_F(KO, passes)), maps)
        xn = np.empty_like(xcur)
        cn = np.empty_like(ccur)
        for core in range(8):
            b, q = divmod(core, 4)
            o0, o1 = res[core]["xo0"], res[core]["xo1"]
            xn[b, q * QT:q * QT + QT // 2] = o0[:QT // 2]
            xn[b, q * QT + QT // 2:(q + 1) * QT] = o1
            if q == 0:
                cn[b] = o0[QT // 2:]
        xcur, ccur = xn, cn
        if _DBG.get("stash") is not None:
            _DBG["stash"].append((xcur.copy(), ccur.copy()))
        del maps, res
    return xcur


NLAT, NCTX, NB = 64, 2, 2
TPB_ = NLAT + NCTX
RB = 2048 + 256
NSH = 1282 + 1026
PASS_BASE = (0, 1282)
HALO_BASE = (1024, 2306)
CTX_BASE = 1026


AGC = 256
NAG = RB // AGC


def xall_rn(r, n):
    return ((n // AGC) * 8 + r) * AGC + n % AGC


def xall_row(b, tl):
    if tl < NLAT:
        return xall_rn(b * 4 + tl // 16, (tl % 16) * 128)
    return xall_rn(b * 4, 2048 + (tl - NLAT) * 128)


class Fz:
    pass


def mod_col_loads(S, nc, Z, dst, dst_idx, l, k, r):
    for c in range(8):
        gc = k * 8 + c
        rank, loc0 = gc // 6, (gc % 6) * 128
        src = Z.modall_d[rank * 12 + l * 3 + r, loc0:loc0 + 128].rearrange("(p o) -> p o", o=1)
        S.dma("sp" if c % 2 == 0 else "act", dst.t[:, dst_idx, c:c + 1], src, reads=[Z.modall_tok], writes=[dst], indep=True)


def mod_row_bcast(S, nc, Z, dst, dst_idx, l, k, r):
    c0 = k * 1024
    qi = 0
    while c0 < (k + 1) * 1024:
        rank = c0 // 768
        c1 = min((rank + 1) * 768, (k + 1) * 1024)
        src = Z.modall_d[rank * 12 + l * 3 + r:rank * 12 + l * 3 + r + 1, c0 - rank * 768:c1 - rank * 768].partition_broadcast(128)
        S.dma("sp" if qi % 2 == 0 else "act", dst.t[:, dst_idx, c0 - k * 1024:c1 - k * 1024], src, reads=[Z.modall_tok], writes=[dst], indep=True)
        qi += 1
        c0 = c1


def emit_ADA(S, nc, Z):
    dr = Z.dr
    c_d = dr("cT", [128, 8, 3])
    w_d = dr("ada_w", [4, D, 768])
    b_d = dr("ada_b", [128, 4, 6])
    with S.scope():
        cT = S.sb("cT", [128, 8, 3], F32)
        sc = S.sb("sc", [128, 8, 3], BF16)
        bb = S.sb("bb", [128, 4, 6], F32)
        ot = S.sb("ot", [128, 4, 6, 3], F32)
        S.dma("sp", cT.t[:], c_d, writes=[cT])
        S.dma("sp", bb.t[:], b_d, writes=[bb])
        S.op("act", lambda: nc.scalar.activation(out=sc.t[:], in_=cT.t[:], func=AF.Silu), reads=[cT], writes=[sc])
        stg = [S.sb(f"stg{i}", [128, 8, 768], F32) for i in range(2)]
        wl = [S.sb(f"wl{i}", [128, 8, 768], BF16) for i in range(2)]
        ps = [S.ps(f"ps{i}", [128, 4]) for i in range(2)]
        k = 0
        for l in range(4):
            st, w = stg[l % 2], wl[l % 2]
            wlv = w_d[l].rearrange("(c p) n -> p c n", p=128)
            for c8 in range(8):
                S.dma("sp" if c8 % 2 == 0 else "act", st.t[:, c8, :], wlv[:, c8, :], writes=[st], indep=True)
            S.op("dve" if l % 2 == 0 else "pool", lambda: (nc.vector if l % 2 == 0 else nc.gpsimd).tensor_copy(w.t[:], st.t[:]), reads=[st], writes=[w])
            for j in range(6):
                p = ps[k % 2]
                k += 1
                for c in range(8):
                    S.op("pe", lambda: nc.tensor.matmul(p.t[:, 0:3], lhsT=w.t[:, c, j * 128:(j + 1) * 128], rhs=sc.t[:, c, :], start=(c == 0), stop=(c == 7)), reads=[w, sc], writes=[p])
                S.op("dve", lambda: nc.vector.tensor_scalar(ot.t[:, l, j, :], p.t[:, 0:3], bb.t[:, l, j:j + 1], None, ALU.add), reads=[p, bb], writes=[ot])
        q = 0
        for l in range(4):
            for j in range(6):
                for r in range(3):
                    dst = Z.modloc_d[l * 3 + r, j * 128:(j + 1) * 128].rearrange("(p o) -> p o", o=1)
                    S.dma("sp" if q % 2 == 0 else "act", dst, ot.t[:, l, j, r:r + 1], reads=[ot], writes=[Z.modloc_tok], indep=True)
                    q += 1
        S.coll("AllGather", [Z.modloc_d], [Z.modall_d], reads=[Z.modloc_tok], writes=[Z.modall_tok])
        S.barrier([cT, sc, bb, ot, Z.modloc_tok, Z.modall_tok] + stg + wl + ps)


def emit_gain_shift(S, nc, Z, i, which, k_sc, k_sh, rows):
    R = len(rows)
    cols = S.sb("gcols", [128, 1 + R, 8], F32)
    gain = S.sb("gain", [128, R, 8], F32)
    shift = S.sb("shift", [128, R, 8], F32)
    S.dma("sp", cols.t[:, 0, :], Z.ngfm_d[:, i, which, :], writes=[cols], indep=True)
    for ri, r in enumerate(rows):
        mod_col_loads(S, nc, Z, cols, 1 + ri, i, k_sc, r)
        mod_col_loads(S, nc, Z, shift, ri, i, k_sh, r)
    for ri in range(R):
        S.op("dve", lambda: nc.vector.scalar_tensor_tensor(out=gain.t[:, ri, :], in0=cols.t[:, 1 + ri, :], scalar=1.0, in1=cols.t[:, 0, :], op0=ALU.add, op1=ALU.mult),
             reads=[cols], writes=[gain])
    return gain, shift, cols


def emit_stageC(S, nc, Z, oA, tiles, C, woh):
    KC = C // 128
    with S.scope():
        pOT = [S.ps(f"pOT{i}", [128, KC, 128], BF16) for i in range(2)]
        pY = [S.ps(f"pYc{i}", [128, D]) for i in range(2)]
        oTt = [S.sb(f"oTt{i}", [128, KC, 128], BF16) for i in range(2)]
        ysb = [S.sb(f"ysb{i}", [128, D], F32) for i in range(3)]
        yin3 = Z.yin_d.rearrange("(j n) d -> j n d", j=8)
        for n, (b, tl, oidx) in enumerate(tiles):
            pt, py, ot_, ys = pOT[n % 2], pY[n % 2], oTt[n % 2], ysb[n % 3]
            for kc in range(KC):
                S.op("pe", lambda: nc.tensor.transpose(pt.t[:, kc, :], oA.t[:, oidx, kc * 128:(kc + 1) * 128], Z.ident.t[:]), reads=[oA, Z.ident], writes=[pt])
            S.op("dve", lambda: nc.vector.tensor_copy(ot_.t[:], pt.t[:]), reads=[pt], writes=[ot_])
            for half in range(2):
                for kc in range(KC):
                    S.op("pe", lambda: nc.tensor.matmul(py.t[:, half * 512:(half + 1) * 512], lhsT=ot_.t[:, kc, :], rhs=woh.t[:, kc, half * 512:(half + 1) * 512],
                                                        start=(kc == 0), stop=(kc == KC - 1)), reads=[ot_, woh], writes=[py])
            if n % 2 == 0:
                S.op("act", lambda: nc.scalar.copy(ys.t[:], py.t[:]), reads=[py], writes=[ys])
            else:
                S.op("dve", lambda: nc.vector.tensor_copy(ys.t[:], py.t[:]), reads=[py], writes=[ys])
            wr = lambda q, dst, src: S.dma(q, dst, src, reads=[ys], writes=[Z.yin_tok], indep=True)
            if tl < NLAT:
                blk, m = divmod(tl * 128, 1024)
                j, p = b * 4 + blk // 2, blk % 2
                wr("sp" if n % 2 == 0 else "act", yin3[j, PASS_BASE[p] + m:PASS_BASE[p] + m + 128, :], ys.t[:])
                if m == 0 and blk > 0:
                    jp, pp = b * 4 + (blk - 1) // 2, (blk - 1) % 2
                    wr("act", yin3[jp, HALO_BASE[pp] + 1:HALO_BASE[pp] + 2, :], ys.t[0:1, :])
                if m == 1024 - 128 and blk < 7:
                    jn, pn = b * 4 + (blk + 1) // 2, (blk + 1) % 2
                    wr("sp", yin3[jn, HALO_BASE[pn]:HALO_BASE[pn] + 1, :], ys.t[127:128, :])
            else:
                c = tl - NLAT
                for q4 in range(4):
                    wr("sp" if q4 % 2 == 0 else "act", yin3[b * 4 + q4, CTX_BASE + c * 128:CTX_BASE + (c + 1) * 128, :], ys.t[:])
        S.barrier(pOT + pY + oTt + ysb)


def emit_ME_fused(S, nc, Z, i):
    j2 = i // 2
    n_lat, n_ctx, n_b = NLAT, NCTX, NB
    TPB = TPB_
    NT = n_b * TPB
    x_src = Z.xin_d if i == 0 else Z.xall_d
    with S.scope():
        gain, shift, _ = emit_gain_shift(S, nc, Z, i, 0, 1, 0, [0, 1, 2])
        wqk = S.sb("wqk", [128, 8, 128], BF16)
        wgu = S.sb("wgu", [128, 8, 576], BF16)
        wv = S.sb("wv", [128, 8, 64], BF16)
        wsT = S.sb("wsT", [128, 128], BF16)
        woh = S.sb("woh", [128, 1, D], BF16)
        with S.scope():
            stage = S.sb("stage", [128, 8, 576], F32)
            for (wb, wd, n) in ((wqk, Z.me_wqk_d[j2], 128), (wgu, Z.me_wgu_d[j2], 576), (wv, Z.me_wv_d[j2], 64)):
                wdv = wd.rearrange("(c p) n -> p c n", p=128)
                stg_ = S.sb(f"stg{n}", [128, 8, n], F32)
                for c8 in range(8):
                    S.dma("sp" if c8 % 2 == 0 else "act", stg_.t[:, c8, :], wdv[:, c8, :], writes=[stg_], indep=True)
                S.op("dve", lambda: nc.vector.tensor_copy(wb.t[:], stg_.t[:]), reads=[stg_], writes=[wb])
            S.dma("sp", stage.t[:, 0, :128], Z.me_wsT_d[j2], writes=[stage])
            S.op("dve", lambda: nc.vector.tensor_copy(wsT.t[:], stage.t[:, 0, :128]), reads=[stage], writes=[wsT])
            for h2 in range(2):
                S.dma("sp", stage.t[:, h2, :512], Z.me_woh_d[j2][:, h2 * 512:(h2 + 1) * 512], writes=[stage])
            S.op("dve", lambda: nc.vector.tensor_copy(woh.t[:, 0, :].rearrange("p (a n) -> p a n", a=2), stage.t[:, 0:2, :512]), reads=[stage], writes=[woh])
        bs = S.sb("bs", [128, 1], F32)
        S.dma("sp", bs.t[:], Z.me_bs_d[j2], writes=[bs])
        biasT = S.sb("biasT", [128, 5, 576], F32)
        for k in range(5):
            S.dma("act", biasT.t[:, k, :], Z.me_bias_d[j2, k], writes=[biasT], indep=True)
        qT = S.sb("qT", [64, NT * 128], BF16)
        kT = S.sb("kT", [64, NT * 128], BF16)
        vA = S.sb("vA", [128, NT, 64], BF16)
        oA = S.sb("oA", [128, NT, 128], BF16)
        ident, epsc = Z.ident, Z.epsc
        with S.scope():
            N = NormCtx(S, ident, epsc)
            xts = [S.sb(f"xt{k}", [128, D], F32) for k in range(2)]
            hTs = [S.sb(f"hT{k}", [128, 8, 128], BF16) for k in range(2)]
            pQK = S.ps("pQK", [64, 2, 128])
            pGU = S.ps("pGU", [128, 1024])
            pSG = S.ps("pSG", [128, 128])
            xg = S.sb("xg", [128, 576], F32)
            t1 = S.sb("t1", [128, 576], F32)
            gl = S.sb("gl", [128, 576], F32)
            junk = S.sb("junk", [128, 512], F32)
            st = S.sb("st", [128, 4], F32)
            vn = S.sb("vn", [128, 64], BF16)
            for ti in range(NT):
                b, tl = divmod(ti, TPB)
                r = b if tl < n_lat else 2
                xt = xts[ti % 2]
                hT = hTs[ti % 2]
                r0 = xall_row(b, tl)
                S.dma("sp" if ti % 2 == 0 else "act", xt.t[:], x_src[r0:r0 + 128, :], reads=[Z.xall_tok], writes=[xt])
                emit_norm_T(S, nc, N, xt, 128, gain, shift, r, hT, 0)
                for w in range(2):
                    for c in range(8):
                        S.op("pe", lambda: nc.tensor.matmul(pQK.t[:, w, :], lhsT=wqk.t[:, c, w * 64:(w + 1) * 64], rhs=hT.t[:, c, :], start=(c == 0), stop=(c == 7)), reads=[wqk, hT], writes=[pQK])
                S.op("act", lambda: nc.scalar.activation(out=qT.t[:, ti * 128:(ti + 1) * 128], in_=pQK.t[:, 0, :], func=AF.Copy, scale=0.125), reads=[pQK], writes=[qT])
                S.op("act", lambda: nc.scalar.copy(kT.t[:, ti * 128:(ti + 1) * 128], pQK.t[:, 1, :]), reads=[pQK], writes=[kT])
                for (o0, n, wb, wo0) in ((0, 512, wgu, 0), (512, 64, wgu, 512), (576, 64, wv, 0)):
                    for c in range(8):
                        S.op("pe", lambda: nc.tensor.matmul(pGU.t[:, o0:o0 + n], lhsT=hT.t[:, c, :], rhs=wb.t[:, c, wo0:wo0 + n], start=(c == 0), stop=(c == 7)), reads=[wb, hT], writes=[pGU])
                S.op("act", lambda: nc.scalar.copy(vA.t[:, ti, :], pGU.t[:, 576:640]), reads=[pGU], writes=[vA])
                S.op("act", lambda: nc.scalar.copy(xg.t[:], pGU.t[:, 0:576]), reads=[pGU], writes=[xg])
                S.op("dve", lambda: nc.vector.tensor_tensor(out=t1.t[:], in0=xg.t[:], in1=xg.t[:], op=ALU.mult), reads=[xg], writes=[t1])
                S.op("dve", lambda: nc.vector.tensor_scalar(t1.t[:], t1.t[:], 0.044715, 1.0, ALU.mult, ALU.add), reads=[t1], writes=[t1])
                S.op("pool", lambda: nc.gpsimd.tensor_tensor(out=t1.t[:], in0=t1.t[:], in1=xg.t[:], op=ALU.mult), reads=[t1, xg], writes=[t1])
                S.op("act", lambda: nc.scalar.activation(out=t1.t[:], in_=t1.t[:], func=AF.Sigmoid, scale=GELU_C), reads=[t1], writes=[t1])
                S.op("pool", lambda: nc.gpsimd.tensor_tensor(out=gl.t[:], in0=t1.t[:], in1=xg.t[:], op=ALU.mult), reads=[t1, xg], writes=[gl])
                S.op("act", lambda: nc.scalar.activation(out=junk.t[:], in_=gl.t[:, 0:512], func=AF.Identity, accum_out=st.t[:, 0:1]), reads=[gl], writes=[junk, st])
                S.op("act", lambda: nc.scalar.activation(out=junk.t[:], in_=gl.t[:, 0:512], func=AF.Square, accum_out=st.t[:, 1:2]), reads=[gl], writes=[junk, st])
                S.op("dve", lambda: nc.vector.tensor_scalar(st.t[:, 0:2], st.t[:, 0:2], 1.0 / 512, None, ALU.mult), reads=[st], writes=[st])
                S.op("dve", lambda: nc.vector.tensor_tensor(out=st.t[:, 2:3], in0=st.t[:, 0:1], in1=st.t[:, 0:1], op=ALU.mult), reads=[st], writes=[st])
                S.op("dve", lambda: nc.vector.tensor_tensor(out=st.t[:, 2:3], in0=st.t[:, 1:2], in1=st.t[:, 2:3], op=ALU.subtract), reads=[st], writes=[st])
                S.op("act", lambda: nc.scalar.activation(out=st.t[:, 2:3], in_=st.t[:, 2:3], func=AF.Sqrt, bias=epsc.t[:], scale=1.0), reads=[st, epsc], writes=[st])
                S.op("dve", lambda: nc.vector.reciprocal(st.t[:, 2:3], st.t[:, 2:3]), reads=[st], writes=[st])
                S.op("dve", lambda: nc.vector.tensor_scalar(vn.t[:], gl.t[:, 0:64], st.t[:, 0:1], st.t[:, 2:3], ALU.subtract, ALU.mult), reads=[gl, st], writes=[vn])
                S.op("pe", lambda: nc.tensor.matmul(pSG.t[:, 0:64], lhsT=wsT.t[:], rhs=vn.t[:], start=True, stop=True), reads=[wsT, vn], writes=[pSG])
                S.op("dve", lambda: nc.vector.scalar_tensor_tensor(out=oA.t[:, ti, 64:128], in0=pSG.t[:, 0:64], scalar=bs.t[:, 0:1], in1=gl.t[:, 512:576], op0=ALU.add, op1=ALU.mult),
                     reads=[pSG, bs, gl], writes=[oA])
        with S.scope():
            pSs = [S.ps(f"pS{k}", [128, 1024]) for k in range(2)]
            pPTs = [S.ps(f"pPT{k}", [128, 7, 128], BF16) for k in range(2)]
            pOs = [S.ps(f"pO{k}", [128, 64]) for k in range(2)]
            Ts = [S.sb(f"T{k}", [128, 832], F32) for k in range(2)]
            Ps = [S.sb(f"P{k}", [128, 832], BF16) for k in range(2)]
            PTs = [S.sb(f"PT{k}", [128, 7, 128], BF16) for k in range(2)]
            mxs = [S.sb(f"mx{k}", [128, 2], F32) for k in range(2)]
            step = 0
            for b in range(n_b):
                base = b * TPB
                ctx_tiles = [base + n_lat + j for j in range(n_ctx)]
                for tl in range(TPB):
                    ti = base + tl
                    pS, pPT, pO, T, P, PT, mx = [x[step % 2] for x in (pSs, pPTs, pOs, Ts, Ps, PTs, mxs)]
                    step += 1
                    if tl < n_lat:
                        tw = min(max(tl - 2, 0), n_lat - 4)
                        full = [base + tw + k for k in range(4)]
                        extra = (base + tw + 4) if (2 <= tl <= n_lat - 3) else None
                        cls = 0 if tl == 0 else 1 if tl == 1 else 3 if tl == n_lat - 2 else 4 if tl == n_lat - 1 else 2
                        nnb = 512 + (64 if extra is not None else 0)
                    else:
                        full, extra, cls, nnb = [], None, None, 0
                    ncx = n_ctx * 128
                    W = nnb + ncx
                    qs = qT.t[:, ti * 128:(ti + 1) * 128]
                    if full:
                        S.op("pe", lambda: nc.tensor.matmul(pS.t[:, 0:512], lhsT=qs, rhs=kT.t[:, full[0] * 128:(full[0] + 4) * 128], start=True, stop=True), reads=[qT, kT], writes=[pS])
                        if extra is not None:
                            S.op("pe", lambda: nc.tensor.matmul(pS.t[:, 512:576], lhsT=qs, rhs=kT.t[:, extra * 128:extra * 128 + 64], start=True, stop=True), reads=[qT, kT], writes=[pS])
                    c0 = 512 + (64 if extra is not None else 0) if full else 0
                    S.op("pe", lambda: nc.tensor.matmul(pS.t[:, c0:c0 + ncx], lhsT=qs, rhs=kT.t[:, ctx_tiles[0] * 128:ctx_tiles[0] * 128 + ncx], start=True, stop=True), reads=[qT, kT], writes=[pS])
                    if full:
                        S.op("dve", lambda: nc.vector.tensor_tensor(out=T.t[:, 0:nnb], in0=pS.t[:, 0:nnb], in1=biasT.t[:, cls, 0:nnb], op=ALU.add), reads=[pS, biasT], writes=[T])
                    S.op("act", lambda: nc.scalar.copy(T.t[:, nnb:W], pS.t[:, c0:c0 + ncx]), reads=[pS], writes=[T])
                    S.op("dve", lambda: nc.vector.tensor_reduce(out=mx.t[:, 0:1], in_=T.t[:, 0:W], axis=AX.X, op=ALU.max), reads=[T], writes=[mx])
                    S.op("dve", lambda: nc.vector.tensor_scalar(mx.t[:, 0:1], mx.t[:, 0:1], -1.0, None, ALU.mult), reads=[mx], writes=[mx])
                    S.op("act", lambda: nc.scalar.activation(out=P.t[:, 0:W], in_=T.t[:, 0:W], func=AF.Exp, bias=mx.t[:, 0:1], scale=1.0, accum_out=mx.t[:, 1:2]), reads=[T, mx], writes=[P, mx])
                    S.op("dve", lambda: nc.vector.reciprocal(mx.t[:, 1:2], mx.t[:, 1:2]), reads=[mx], writes=[mx])
                    chunks = [(k * 128, 128, full[k]) for k in range(len(full))]
                    if extra is not None:
                        chunks.append((512, 64, extra))
                    chunks += [(nnb + j * 128, 128, ctx_tiles[j]) for j in range(n_ctx)]
                    for ci, (pc0, n, vt) in enumerate(chunks):
                        S.op("pe", lambda: nc.tensor.transpose(pPT.t[:n, ci, :], P.t[:, pc0:pc0 + n], ident.t[:]), reads=[P, ident], writes=[pPT])
                    ncf = len(chunks)
                    S.op("dve", lambda: nc.vector.tensor_copy(PT.t[:, 0:ncf, :], pPT.t[:, 0:ncf, :]), reads=[pPT], writes=[PT])
                    for ci, (pc0, n, vt) in enumerate(chunks):
                        S.op("pe", lambda: nc.tensor.matmul(pO.t[:, :], lhsT=PT.t[:n, ci, :], rhs=vA.t[:n, vt, :], start=(ci == 0), stop=(ci == ncf - 1)), reads=[PT, vA], writes=[pO])
                    S.op("act", lambda: nc.scalar.activation(out=oA.t[:, ti, 0:64], in_=pO.t[:, :], func=AF.Copy, scale=mx.t[:, 1:2]), reads=[pO, mx], writes=[oA])
        emit_stageC(S, nc, Z, oA, [(ti // TPB, ti % TPB, ti) for ti in range(NT)], 128, woh)


def emit_MO_fused(S, nc, Z, i):
    j2 = i // 2
    n_lat, n_ctx, n_b = NLAT, NCTX, NB
    TPB = TPB_
    x_src = Z.xin_d if i == 0 else Z.xall_d
    ident, epsc = Z.ident, Z.epsc
    gS_d, oF_d, gS_tok, oF_tok = Z.gS_d, Z.oF_d, Z.gS_tok, Z.oF_tok
    with S.scope():
        gain, shift, _ = emit_gain_shift(S, nc, Z, i, 0, 1, 0, [0, 1, 2])
        wb = S.sb("wb", [128, 8, 768], BF16)
        woh = S.sb("woh", [128, 2, D], BF16)
        with S.scope():
            stage = S.sb("stage", [128, 8, 768], F32)
            wv_ = Z.mo_w_d[j2].rearrange("(c p) n -> p c n", p=128)
            for c8 in range(8):
                S.dma("sp" if c8 % 2 == 0 else "act", stage.t[:, c8, :], wv_[:, c8, :], writes=[stage], indep=True)
            S.op("dve", lambda: nc.vector.tensor_copy(wb.t[:], stage.t[:]), reads=[stage], writes=[wb])
            stg2 = S.sb("stg2", [128, 4, 512], F32)
            for k4 in range(4):
                S.dma("sp" if k4 % 2 == 0 else "act", stg2.t[:, k4, :], Z.mo_woh_d[j2][(k4 // 2) * 128:(k4 // 2 + 1) * 128, (k4 % 2) * 512:(k4 % 2 + 1) * 512], writes=[stg2], indep=True)
            S.op("dve", lambda: nc.vector.tensor_copy(woh.t[:].rearrange("p k (a n) -> p (k a) n", a=2), stg2.t[:]), reads=[stg2], writes=[woh])
        if getattr(Z, "dump_wb", None) is not None:
            for c8 in range(8):
                S.dma("sp", Z.dump_wb[:, c8, :], wb.t[:, c8, :], reads=[wb], writes=[], out=True)
            S.dma("act", Z.dump_gain, gain.t[:], reads=[gain], writes=[], out=True)
            S.dma("act", Z.dump_shift, shift.t[:], reads=[shift], writes=[], out=True)
        lg = S.sb("lg", [128, 2], F32)
        S.dma("sp", lg.t[:], Z.mo_lg_d[j2], writes=[lg])
        DT = S.sb("DT", [128, 2, 128], F32)
        DQ = S.sb("DQ", [128, 2, 128], F32)
        DK = S.sb("DK", [128, 2], F32)
        GC = S.sb("GC", [128, 2], F32)
        gn = S.sb("gn", [128, 256], F32)
        S.dma("sp", gn.t[:], Z.mo_gn_d[j2], writes=[gn])
        with S.scope():
            cE = S.sb("cE", [128, 2, 128], F32)
            cM = S.sb("cM", [128, 2, 128], F32)
            cQ = S.sb("cQ", [128, 2, 128], F32)
            cK = S.sb("cK", [128, 2], F32)
            c128 = S.sb("c128", [128, 1], F32)
            S.op("dve", lambda: nc.vector.memset(c128.t[:], 128.0), writes=[c128])
            S.dma("sp", cK.t[:], Z.cK_d, writes=[cK])
            for d in range(2):
                S.dma("sp", cE.t[:, d, :], Z.cE_d[d], writes=[cE], indep=True)
                S.dma("act", cM.t[:, d, :], Z.cM_d[d], writes=[cM], indep=True)
                S.dma("sp", cQ.t[:, d, :], Z.cQ_d[d], writes=[cQ], indep=True)
            for d in range(2):
                S.op("act", lambda: nc.scalar.activation(out=DT.t[:, d, :], in_=cE.t[:, d, :], func=AF.Exp, scale=lg.t[:, d:d + 1]), reads=[cE, lg], writes=[DT])
                S.op("dve", lambda: nc.vector.tensor_tensor(out=DT.t[:, d, :], in0=DT.t[:, d, :], in1=cM.t[:, d, :], op=ALU.mult), reads=[DT, cM], writes=[DT])
                S.op("act", lambda: nc.scalar.activation(out=DQ.t[:, d, :], in_=cQ.t[:, d, :], func=AF.Exp, scale=lg.t[:, d:d + 1]), reads=[cQ, lg], writes=[DQ])
                S.op("act", lambda: nc.scalar.activation(out=DK.t[:, d:d + 1], in_=cK.t[:, d:d + 1], func=AF.Exp, scale=lg.t[:, d:d + 1]), reads=[cK, lg], writes=[DK])
                S.op("act", lambda: nc.scalar.activation(out=GC.t[:, d:d + 1], in_=c128.t[:], func=AF.Exp, scale=lg.t[:, d:d + 1]), reads=[c128, lg], writes=[GC])
        for b in range(n_b):
            base = b * TPB
            with S.scope():
                qT = S.sb(f"qT{b}", [128, TPB * 128], BF16)
                kT = S.sb(f"kT{b}", [128, TPB * 128], BF16)
                kK = S.sb(f"kK{b}", [128, TPB, 128], BF16)
                vA = S.sb(f"vA{b}", [128, TPB, 256], BF16)
                with S.scope():
                    N = NormCtx(S, ident, epsc)
                    xts = [S.sb(f"xt{b}_{k}", [128, D], F32) for k in range(2)]
                    hTs = [S.sb(f"hT{b}_{k}", [128, 8, 128], BF16) for k in range(2)]
                    pIn = S.ps(f"pIn{b}", [128, 1024])
                    pT2 = S.ps(f"pT2{b}", [128, 2, 128], BF16)
                    qk = S.sb(f"qk{b}", [128, 256], F32)
                    rp = [S.sb(f"rp{b}_{k}", [128, 256], F32) for k in range(2)]
                    ta = S.sb(f"ta{b}", [128, 2, 64], F32)
                    tb = S.sb(f"tb{b}", [128, 2, 64], F32)
                    rq = S.sb(f"rq{b}", [128, 256], BF16)
                    gs = [S.sb(f"gs{b}_{k}", [128, 256], F32) for k in range(2)]
                    for tl in range(TPB):
                        ti = base + tl
                        is_lat = tl < n_lat
                        r = b if is_lat else 2
                        xt = xts[tl % 2]
                        hT = hTs[tl % 2]
                        r0 = xall_row(b, tl)
                        S.dma("sp" if tl % 2 == 0 else "act", xt.t[:], x_src[r0:r0 + 128, :], reads=[Z.xall_tok], writes=[xt])
                        emit_norm_T(S, nc, N, xt, 128, gain, shift, r, hT, 0)
                        for (o0, n) in ((0, 512), (512, 256)):
                            for c in range(8):
                                S.op("pe", lambda: nc.tensor.matmul(pIn.t[:, o0:o0 + n], lhsT=hT.t[:, c, :], rhs=wb.t[:, c, o0:o0 + n], start=(c == 0), stop=(c == 7)), reads=[wb, hT], writes=[pIn])
                        S.op("act", lambda: nc.scalar.copy(vA.t[:, tl, :], pIn.t[:, 256:512]), reads=[pIn], writes=[vA])
                        g_ = gs[tl % 2]
                        S.op("act", lambda: nc.scalar.activation(out=g_.t[:], in_=pIn.t[:, 512:768], func=AF.Silu), reads=[pIn], writes=[g_])
                        S.dma("act", gS_d[ti * 128:(ti + 1) * 128, :], g_.t[:], reads=[g_], writes=[gS_tok[ti]])
                        if is_lat:
                            S.op("act", lambda: nc.scalar.copy(qk.t[:, 0:128], pIn.t[:, 0:128]), reads=[pIn], writes=[qk])
                            S.op("act", lambda: nc.scalar.activation(out=qk.t[:, 128:256], in_=pIn.t[:, 128:256], func=AF.Copy, scale=128.0 ** -0.5), reads=[pIn], writes=[qk])
                            rpt = rp[tl % 2]
                            S.dma("sp", rpt.t[:], Z.rope_d[tl], writes=[rpt])
                            q4 = qk.t[:].rearrange("p (a h c) -> p a h c", a=2, h=2)
                            o4 = rq.t[:].rearrange("p (a h c) -> p a h c", a=2, h=2)
                            cs = rpt.t[:, 0:128].rearrange("p (a c) -> p a c", a=2)
                            sn = rpt.t[:, 128:256].rearrange("p (a c) -> p a c", a=2)
                            S.op("dve", lambda: nc.vector.tensor_tensor(out=ta.t[:], in0=q4[:, :, 0, :], in1=cs, op=ALU.mult), reads=[qk, rpt], writes=[ta])
                            S.op("pool", lambda: nc.gpsimd.tensor_tensor(out=tb.t[:], in0=q4[:, :, 1, :], in1=sn, op=ALU.mult), reads=[qk, rpt], writes=[tb])
                            S.op("dve", lambda: nc.vector.tensor_tensor(out=o4[:, :, 0, :], in0=ta.t[:], in1=tb.t[:], op=ALU.subtract), reads=[ta, tb], writes=[rq])
                            S.op("dve", lambda: nc.vector.tensor_tensor(out=ta.t[:], in0=q4[:, :, 0, :], in1=sn, op=ALU.mult), reads=[qk, rpt, rq], writes=[ta])
                            S.op("pool", lambda: nc.gpsimd.tensor_tensor(out=tb.t[:], in0=q4[:, :, 1, :], in1=cs, op=ALU.mult), reads=[qk, rpt, rq], writes=[tb])
                            S.op("dve", lambda: nc.vector.tensor_tensor(out=o4[:, :, 1, :], in0=ta.t[:], in1=tb.t[:], op=ALU.add), reads=[ta, tb], writes=[rq])
                        else:
                            S.op("act", lambda: nc.scalar.copy(rq.t[:, 0:128], pIn.t[:, 0:128]), reads=[pIn], writes=[rq])
                            S.op("act", lambda: nc.scalar.activation(out=rq.t[:, 128:256], in_=pIn.t[:, 128:256], func=AF.Copy, scale=128.0 ** -0.5), reads=[pIn], writes=[rq])
                        S.op("pool", lambda: nc.gpsimd.tensor_copy(kK.t[:, tl, :], rq.t[:, 128:256]), reads=[rq], writes=[kK])
                        for w in range(2):
                            S.op("pe", lambda: nc.tensor.transpose(pT2.t[:, w, :], rq.t[:, w * 128:(w + 1) * 128], ident.t[:]), reads=[rq, ident], writes=[pT2])
                        S.op("dve", lambda: nc.vector.tensor_copy(qT.t[:, tl * 128:(tl + 1) * 128], pT2.t[:, 0, :]), reads=[pT2], writes=[qT])
                        S.op("dve", lambda: nc.vector.tensor_copy(kT.t[:, tl * 128:(tl + 1) * 128], pT2.t[:, 1, :]), reads=[pT2], writes=[kT])
                oA = S.sb(f"oA{b}", [128, TPB, 256], BF16)
                with S.scope():
                    Sf = [S.sb(f"Sf{b}_{d}", [128, 256], F32) for d in range(2)]
                    Sb = [S.sb(f"Sb{b}_{d}", [128, 256], BF16) for d in range(2)]
                    for d in range(2):
                        S.op("dve", lambda: nc.vector.memset(Sf[d].t[:], 0.0), writes=[Sf[d]])
                        S.op("dve", lambda: nc.vector.memset(Sb[d].t[:], 0.0), writes=[Sb[d]])
                    pST = [S.ps(f"pST{b}_{k}", [128, 128]) for k in range(2)]
                    pOo = [S.ps(f"pOo{b}_{k}", [128, 256]) for k in range(2)]
                    pDS = [S.ps(f"pDS{b}_{k}", [128, 256]) for k in range(2)]
                    sTm = [S.sb(f"sTm{b}_{k}", [128, 128], BF16) for k in range(2)]
                    qd = [S.sb(f"qd{b}_{k}", [128, 128], BF16) for k in range(2)]
                    kd = [S.sb(f"kd{b}_{k}", [128, 128], BF16) for k in range(2)]
                    of_ = [S.sb(f"of{b}_{k}", [128, 256], F32) for k in range(2)]
                    ofl = [S.sb(f"ofl{b}_{k}", [128, 256], F32) for k in range(2)]
                    gl_ = [S.sb(f"gl{b}_{k}", [128, 256], F32) for k in range(2)]
                    junk = S.sb(f"junk{b}", [128, 256], F32)
                    st = [S.sb(f"st{b}_{k}", [128, 4], F32) for k in range(2)]
                    order = [list(range(n_lat, TPB)) + list(range(n_lat)),
                             list(range(TPB - 1, n_lat - 1, -1)) + list(range(n_lat - 1, -1, -1))]
                    for ii in range(TPB):
                        for d in range(2):
                            tl = order[d][ii]
                            ti = base + tl
                            k2 = d
                            cs_ = slice(tl * 128, (tl + 1) * 128)
                            S.op("pe", lambda: nc.tensor.matmul(pST[d].t[:], lhsT=kT.t[:, cs_], rhs=qT.t[:, cs_], start=True, stop=True), reads=[kT, qT], writes=[pST[d]])
                            S.op("dve", lambda: nc.vector.tensor_tensor(out=sTm[d].t[:], in0=pST[d].t[:], in1=DT.t[:, d, :], op=ALU.mult), reads=[pST[d], DT], writes=[sTm[d]])
                            S.op("pool", lambda: nc.gpsimd.tensor_tensor(out=qd[d].t[:], in0=qT.t[:, cs_], in1=DQ.t[:, d, :], op=ALU.mult), reads=[qT, DQ], writes=[qd[d]])
                            S.op("pool", lambda: nc.gpsimd.tensor_scalar(kd[d].t[:], kK.t[:, tl, :], DK.t[:, d:d + 1], None, ALU.mult), reads=[kK, DK], writes=[kd[d]])
                            S.op("pe", lambda: nc.tensor.matmul(pOo[d].t[:], lhsT=sTm[d].t[:], rhs=vA.t[:, tl, :], start=True, stop=False), reads=[sTm[d], vA], writes=[pOo[d]])
                            S.op("pe", lambda: nc.tensor.matmul(pOo[d].t[:], lhsT=qd[d].t[:], rhs=Sb[d].t[:], start=False, stop=True), reads=[qd[d], Sb[d]], writes=[pOo[d]])
                            S.op("pe", lambda: nc.tensor.matmul(pDS[d].t[:], lhsT=kd[d].t[:], rhs=vA.t[:, tl, :], start=True, stop=True), reads=[kd[d], vA], writes=[pDS[d]])
                            S.op("dve", lambda: nc.vector.scalar_tensor_tensor(out=Sf[d].t[:], in0=Sf[d].t[:], scalar=GC.t[:, d:d + 1], in1=pDS[d].t[:], op0=ALU.mult, op1=ALU.add),
                                 reads=[Sf[d], GC, pDS[d]], writes=[Sf[d]])
                            S.op("act", lambda: nc.scalar.copy(Sb[d].t[:], Sf[d].t[:]), reads=[Sf[d]], writes=[Sb[d]])
                            i_other = order[1 - d].index(tl)
                            if ii < i_other or (ii == i_other and d == 0):
                                o_ = of_[k2]
                                S.op("act", lambda: nc.scalar.copy(o_.t[:], pOo[d].t[:]), reads=[pOo[d]], writes=[o_])
                                S.dma("sp", oF_d[ti * 128:(ti + 1) * 128, :], o_.t[:], reads=[o_], writes=[oF_tok[ti]])
                            else:
                                o_ = ofl[k2]
                                g_ = gl_[k2]
                                s_ = st[k2]
                                S.dma("sp", o_.t[:], oF_d[ti * 128:(ti + 1) * 128, :], reads=[oF_tok[ti]], writes=[o_])
                                S.dma("act", g_.t[:], gS_d[ti * 128:(ti + 1) * 128, :], reads=[gS_tok[ti]], writes=[g_])
                                S.op("dve", lambda: nc.vector.tensor_tensor(out=o_.t[:], in0=pOo[d].t[:], in1=o_.t[:], op=ALU.add), reads=[pOo[d], o_], writes=[o_])
                                S.op("act", lambda: nc.scalar.activation(out=junk.t[:], in_=o_.t[:], func=AF.Identity, accum_out=s_.t[:, 0:1]), reads=[o_], writes=[junk, s_])
                                S.op("act", lambda: nc.scalar.activation(out=junk.t[:], in_=o_.t[:], func=AF.Square, accum_out=s_.t[:, 1:2]), reads=[o_], writes=[junk, s_])
                                S.op("dve", lambda: nc.vector.tensor_scalar(s_.t[:, 0:2], s_.t[:, 0:2], 1.0 / 256, None, ALU.mult), reads=[s_], writes=[s_])
                                S.op("dve", lambda: nc.vector.tensor_tensor(out=s_.t[:, 2:3], in0=s_.t[:, 0:1], in1=s_.t[:, 0:1], op=ALU.mult), reads=[s_], writes=[s_])
                                S.op("dve", lambda: nc.vector.tensor_tensor(out=s_.t[:, 2:3], in0=s_.t[:, 1:2], in1=s_.t[:, 2:3], op=ALU.subtract), reads=[s_], writes=[s_])
                                S.op("act", lambda: nc.scalar.activation(out=s_.t[:, 2:3], in_=s_.t[:, 2:3], func=AF.Sqrt, bias=epsc.t[:], scale=1.0), reads=[s_, epsc], writes=[s_])
                                S.op("dve", lambda: nc.vector.reciprocal(s_.t[:, 2:3], s_.t[:, 2:3]), reads=[s_], writes=[s_])
                                S.op("dve", lambda: nc.vector.tensor_scalar(o_.t[:], o_.t[:], s_.t[:, 0:1], s_.t[:, 2:3], ALU.subtract, ALU.mult), reads=[o_, s_], writes=[o_])
                                S.op("pool", lambda: nc.gpsimd.tensor_tensor(out=g_.t[:], in0=g_.t[:], in1=gn.t[:], op=ALU.mult), reads=[g_, gn], writes=[g_])
                                if getattr(Z, "var", "") == "o_only":
                                    S.op("pool", lambda: nc.gpsimd.tensor_copy(oA.t[:, tl, :], o_.t[:]), reads=[o_, g_], writes=[oA])
                                elif getattr(Z, "var", "") == "prod_dve":
                                    S.op("dve", lambda: nc.vector.tensor_tensor(out=oA.t[:, tl, :], in0=o_.t[:], in1=g_.t[:], op=ALU.mult), reads=[o_, g_], writes=[oA])
                                elif getattr(Z, "var", "") == "prod_tmp":
                                    S.op("pool", lambda: nc.gpsimd.tensor_tensor(out=junk.t[:], in0=o_.t[:], in1=g_.t[:], op=ALU.mult), reads=[o_, g_], writes=[junk])
                                    S.op("dve", lambda: nc.vector.tensor_copy(oA.t[:, tl, :], junk.t[:]), reads=[junk], writes=[oA])
                                elif getattr(Z, "var", "") == "g_only":
                                    S.op("pool", lambda: nc.gpsimd.tensor_copy(oA.t[:, tl, :], g_.t[:]), reads=[o_, g_], writes=[oA])
                                else:
                                    S.op("pool", lambda: nc.gpsimd.tensor_tensor(out=oA.t[:, tl, :], in0=o_.t[:], in1=g_.t[:], op=ALU.mult), reads=[o_, g_], writes=[oA])
                if getattr(Z, "dump_oA", None) is not None and b == 0:
                    for tq in range(TPB):
                        S.dma("sp", Z.dump_oA[:, tq, :], oA.t[:, tq, :], reads=[oA], writes=[], out=True)
                        if getattr(Z, "dump_v", None) is not None:
                            S.dma("act", Z.dump_v[:, tq, :], vA.t[:, tq, :], reads=[vA], writes=[], out=True)
                            S.dma("act", Z.dump_k[:, tq, :], kK.t[:, tq, :], reads=[kK], writes=[], out=True)
                if getattr(Z, "var", "") != "nostagec":
                    emit_stageC(S, nc, Z, oA, [(b, tl, tl) for tl in range(TPB)], 256, woh)


def emit_F_fused(S, nc, Z, i, last):
    ident, epsc = Z.ident, Z.epsc
    xcur_d = Z.xloc_in_d if i == 0 else Z.xloc_d[(i - 1) % 2]
    xcur_tok = Z.xloc_in_tok if i == 0 else Z.xloc_tok[(i - 1) % 2]
    xnew_d, xnew_tok = Z.xloc_d[i % 2], Z.xloc_tok[i % 2]
    xall_src = Z.xin_d if i == 0 else Z.xall_d
    w_in_d, w_out_d = Z.ffn_w_in_d[i], Z.ffn_w_out_d[i]
    with S.scope():
        gain3, shift3, _ = emit_gain_shift(S, nc, Z, i, 2, 4, 3, [0, 1, 2])
        gain = S.sb("gainF", [128, 2, 8], F32)
        shift = S.sb("shiftF", [128, 2, 8], F32)
        wbt = Z.wb
        for (dst, src) in ((gain, gain3), (shift, shift3)):
            S.op("dve", lambda: nc.vector.tensor_scalar(dst.t[:, 0, :], src.t[:, 0, :], wbt.t[:, 0:1], None, ALU.mult), reads=[src, wbt], writes=[dst])
            S.op("dve", lambda: nc.vector.scalar_tensor_tensor(out=dst.t[:, 0, :], in0=src.t[:, 1, :], scalar=wbt.t[:, 1:2], in1=dst.t[:, 0, :], op0=ALU.mult, op1=ALU.add),
                 reads=[src, wbt, dst], writes=[dst])
            S.op("dve", lambda: nc.vector.tensor_copy(dst.t[:, 1, :], src.t[:, 2, :]), reads=[src], writes=[dst])
        convw = S.sb("convw", [128, 2 * NFC, 3], F32)
        convb = S.sb("convb", [128, 2 * NFC], F32)
        S.dma("sp", convw.t[:], Z.convw_d[i], writes=[convw])
        S.dma("sp", convb.t[:], Z.convb_d[i], writes=[convb])
        hmask = Z.hmask
        GB = S.sb("GB", [128, 4, D], F32)
        with S.scope():
            G6 = S.sb("G6", [128, 6, D], F32)
            tn = S.sb("tn", [128, 2, D], F32)
            for r in range(3):
                mod_row_bcast(S, nc, Z, G6, r, i, 2, r)
                mod_row_bcast(S, nc, Z, G6, 3 + r, i, 5, r)
            S.dma("sp", tn.t[:, 0, :], Z.ng_d[i, 1:2, :].partition_broadcast(128), writes=[tn], indep=True)
            S.dma("act", tn.t[:, 1, :], Z.ng_d[i, 3:4, :].partition_broadcast(128), writes=[tn], indep=True)
            for k in range(2):
                S.op("dve", lambda: nc.vector.tensor_scalar(GB.t[:, 2 * k, :], G6.t[:, 3 * k, :], wbt.t[:, 0:1], None, ALU.mult), reads=[G6, wbt], writes=[GB])
                S.op("dve", lambda: nc.vector.scalar_tensor_tensor(out=GB.t[:, 2 * k, :], in0=G6.t[:, 3 * k + 1, :], scalar=wbt.t[:, 1:2], in1=GB.t[:, 2 * k, :], op0=ALU.mult, op1=ALU.add),
                     reads=[G6, wbt, GB], writes=[GB])
                S.op("dve", lambda: nc.vector.tensor_copy(GB.t[:, 2 * k + 1, :], G6.t[:, 3 * k + 2, :]), reads=[G6], writes=[GB])
                for q in range(2):
                    S.op("dve", lambda: nc.vector.tensor_tensor(out=GB.t[:, 2 * k + q, :], in0=GB.t[:, 2 * k + q, :], in1=tn.t[:, k, :], op=ALU.mult), reads=[GB, tn], writes=[GB])
        xh = S.sb("xh", [2, D], F32)
        with S.scope():
            cand = S.sb("cand", [2, 8, D], F32)
            xa3 = xall_src.rearrange("(g n) d -> g n d", n=AGC)
            S.dma("sp", cand.t[0:1, :, :], xa3[56:64, AGC - 1:AGC, :].rearrange("r o d -> o r d"), reads=[Z.xall_tok], writes=[cand], indep=True)
            S.dma("act", cand.t[1:2, :, :], xa3[0:8, 0:1, :].rearrange("r o d -> o r d"), reads=[Z.xall_tok], writes=[cand], indep=True)
            S.op("dve", lambda: nc.vector.tensor_scalar(xh.t[:], cand.t[:, 0, :], Z.wnb.t[:, 0:1], None, ALU.mult), reads=[cand, Z.wnb], writes=[xh])
            for r in range(1, 8):
                S.op("dve", lambda: nc.vector.scalar_tensor_tensor(out=xh.t[:], in0=cand.t[:, r, :], scalar=Z.wnb.t[:, r:r + 1], in1=xh.t[:], op0=ALU.mult, op1=ALU.add),
                     reads=[cand, Z.wnb, xh], writes=[xh])
        N = NormCtx(S, ident, epsc)
        stage = rot_sb(S, "stage", [128, 8, 256], F32, 2)
        pY = rot_ps(S, "pY", [128, D], F32, 2)
        pG = rot_ps(S, "pG", [128, 512], F32, 2)
        for pi in range(2):
            n_main, n_ctx = 8, (NCTX if pi == 0 else 0)
            NTOK = f_ntok(n_main, n_ctx)
            hr = 1 + n_main * 128
            NA = n_main * 128 + 4 + n_ctx * 128
            tiles = [(t * 128, 128, 0, 1 + t * 128, PASS_BASE[pi] + t * 128, pi * 1024 + t * 128, pi * 1024 + t * 128) for t in range(n_main)]
            tiles.append((n_main * 128, 2, 0, None, HALO_BASE[pi], None, None))
            tiles += [(n_main * 128 + 2 + j * 128, 128, 1, hr + 2 + j * 128, CTX_BASE + j * 128, 2048 + j * 128, 2048 + j * 128) for j in range(n_ctx)]
            with S.scope():
                hid = S.sb("hid", [128, NFC, NA], BF16)
                xmid_tok = S.tok(f"xmid{i}_{pi}")
                xmid_d = Z.xmid_d
                with S.scope():
                    hT = S.sb("hT", [128, 8, NTOK], BF16)
                    with S.scope():
                        yts = [S.sb(f"yt{k}", [128, D], F32) for k in range(2)]
                        xts = [S.sb(f"xt{k}", [128, D], F32) for k in range(2)]
                        xms = [S.sb(f"xm{k}", [128, D], F32) for k in range(2)]
                        for ti, (col0, P, r, acol, yrow, xrow, orow) in enumerate(tiles):
                            yt, xt, xm = yts[ti % 2], xts[ti % 2], xms[ti % 2]
                            S.dma("sp", yt.t[:P, :], Z.yloc_d[yrow:yrow + P, :], reads=[Z.yloc_tok], writes=[yt])
                            if xrow is not None:
                                S.dma("act", xt.t[:P, :], xcur_d[xrow:xrow + P, :], reads=[xcur_tok], writes=[xt])
                            else:
                                S.op("dve", lambda: nc.vector.tensor_copy(xt.t[0:2, :], xh.t[:]), reads=[xh], writes=[xt])
                                if pi == 0:
                                    S.dma("act", xt.t[1:2, :], xcur_d[1024:1025, :], reads=[xcur_tok], writes=[xt])
                                else:
                                    S.dma("act", xt.t[0:1, :], xcur_d[1023:1024, :], reads=[xcur_tok], writes=[xt])
                            ss = N.ss.next()
                            sq = N.sq.next()
                            emit_rstd(S, nc, yt.t[:P, :], yt, P, ss, sq, epsc)
                            S.op("dve", lambda: nc.vector.scalar_tensor_tensor(out=xm.t[:P, :], in0=yt.t[:P, :], scalar=ss.t[:P, 0:1], in1=GB.t[:P, r, :], op0=ALU.mult, op1=ALU.mult),
                                 reads=[yt, ss, GB], writes=[xm])
                            S.op("pool", lambda: nc.gpsimd.tensor_tensor(out=xm.t[:P, :], in0=xm.t[:P, :], in1=xt.t[:P, :], op=ALU.add), reads=[xm, xt], writes=[xm])
                            if acol is not None:
                                S.dma("act", xmid_d[col0:col0 + P, :], xm.t[:P, :], reads=[xm], writes=[xmid_tok], indep=True)
                            emit_norm_T(S, nc, N, xm, P, gain, shift, r, hT, col0)
                    with S.scope():
                        wblk = [S.sb(f"wblk{k}", [128, 8, 256], BF16) for k in range(2)]
                        ab = [S.sb("abg", [128, NA], F32), S.sb("abu", [128, NA], F32)]
                        cg = S.sb("cg", [128, NA], F32)
                        cu = S.sb("cu", [128, NA], F32)
                        tp = S.sb("tp", [128, NA], F32)
                        for a in ab:
                            S.op("pool", lambda: nc.gpsimd.memset(a.t[:], 0.0), writes=[a])
                        win_v = w_in_d.rearrange("(c p) n -> p c n", p=128)
                        groups = [(g * 512, 512, 1 + g * 512) for g in range(n_main // 4)]
                        if n_ctx:
                            groups.append((n_main * 128 + 2, n_ctx * 128, hr + 2))
                        wi = 0
                        for jb in range(NFC // 2):
                            blk = []
                            for which in range(2):
                                st = stage.next()
                                wb = wblk[which]
                                c0 = which * D_FF + jb * 256
                                S.dma("sp" if which == 0 else "act", st.t[:], win_v[:, :, c0:c0 + 256], writes=[st])
                                S.op("pool", lambda: nc.gpsimd.tensor_copy(wb.t[:], st.t[:]), reads=[st], writes=[wb])
                                blk.append(wb)
                            for jl in range(2):
                                j = jb * 2 + jl
                                for which in range(2):
                                    wb = blk[which]
                                    a = ab[which]
                                    fc = which * NFC + j
                                    for (tc0, n, ac0) in groups:
                                        pg = pG.next()
                                        for c in range(8):
                                            S.op("pe", lambda: nc.tensor.matmul(pg.t[:, :n], lhsT=wb.t[:, c, jl * 128:(jl + 1) * 128], rhs=hT.t[:, c, tc0:tc0 + n], start=(c == 0), stop=(c == 7)),
                                                 reads=[wb, hT], writes=[pg])
                                        if wi % 2 == 0:
                                            S.op("act", lambda: nc.scalar.copy(a.t[:, ac0:ac0 + n], pg.t[:, :n]), reads=[pg], writes=[a])
                                        else:
                                            S.op("dve", lambda: nc.vector.tensor_copy(a.t[:, ac0:ac0 + n], pg.t[:, :n]), reads=[pg], writes=[a])
                                        wi += 1
                                    pg = pG.next()
                                    hc = n_main * 128
                                    for c in range(8):
                                        S.op("pe", lambda: nc.tensor.matmul(pg.t[:, :2], lhsT=wb.t[:, c, jl * 128:(jl + 1) * 128], rhs=hT.t[:, c, hc:hc + 2], start=(c == 0), stop=(c == 7)),
                                             reads=[wb, hT], writes=[pg])
                                    S.op("dve", lambda: nc.vector.tensor_tensor(out=a.t[:, 0:1], in0=pg.t[:, 0:1], in1=hmask.t[:, 2 * pi:2 * pi + 1], op=ALU.mult), reads=[pg, hmask], writes=[a])
                                    S.op("dve", lambda: nc.vector.tensor_tensor(out=a.t[:, hr:hr + 1], in0=pg.t[:, 1:2], in1=hmask.t[:, 2 * pi + 1:2 * pi + 2], op=ALU.mult), reads=[pg, hmask], writes=[a])
                                    cc = cg if which == 0 else cu
                                    S.op("act", lambda: nc.scalar.activation(out=cc.t[:, 1:NA - 1], in_=a.t[:, 1:NA - 1], func=AF.Identity, bias=convb.t[:, fc:fc + 1], scale=convw.t[:, fc, 1:2]),
                                         reads=[a, convw, convb], writes=[cc])
                                    S.op("dve", lambda: nc.vector.scalar_tensor_tensor(out=cc.t[:, 1:NA - 1], in0=a.t[:, 0:NA - 2], scalar=convw.t[:, fc, 0:1], in1=cc.t[:, 1:NA - 1], op0=ALU.mult, op1=ALU.add),
                                         reads=[a, convw, cc], writes=[cc])
                                    if which == 0:
                                        S.op("dve", lambda: nc.vector.scalar_tensor_tensor(out=cc.t[:, 1:NA - 1], in0=a.t[:, 2:NA], scalar=convw.t[:, fc, 2:3], in1=cc.t[:, 1:NA - 1], op0=ALU.mult, op1=ALU.add),
                                             reads=[a, convw, cc], writes=[cc])
                                    else:
                                        S.op("pool", lambda: nc.gpsimd.tensor_scalar(tp.t[:, 1:NA - 1], a.t[:, 2:NA], convw.t[:, fc, 2:3], None, ALU.mult), reads=[a, convw], writes=[tp])
                                        S.op("pool", lambda: nc.gpsimd.tensor_tensor(out=cc.t[:, 1:NA - 1], in0=cc.t[:, 1:NA - 1], in1=tp.t[:, 1:NA - 1], op=ALU.add), reads=[cc, tp], writes=[cc])
                                S.op("act", lambda: nc.scalar.activation(out=cg.t[:, 1:NA - 1], in_=cg.t[:, 1:NA - 1], func=AF.Silu), reads=[cg], writes=[cg])
                                S.op("pool", lambda: nc.gpsimd.tensor_tensor(out=hid.t[:, j, 1:NA - 1], in0=cg.t[:, 1:NA - 1], in1=cu.t[:, 1:NA - 1], op=ALU.mult), reads=[cg, cu], writes=[hid])
                with S.scope():
                    wout = S.sb("wout", [128, NFC, D], BF16)
                    wout_v = w_out_d.rearrange("(c p) n -> p c n", p=128)
                    for j in range(NFC):
                        for h4 in range(4):
                            st = stage.next()
                            S.dma("sp" if h4 % 2 == 0 else "act", st.t[:, 0, :], wout_v[:, j, h4 * 256:(h4 + 1) * 256], writes=[st])
                            S.op("pool", lambda: nc.gpsimd.tensor_copy(wout.t[:, j, h4 * 256:(h4 + 1) * 256], st.t[:, 0, :]), reads=[st], writes=[wout])
                    xms = [S.sb(f"xm3{k}", [128, D], F32) for k in range(2)]
                    xos = [S.sb(f"xo3{k}", [128, D], F32) for k in range(2)]
                    k3 = 0
                    for (col0, P, r, acol, yrow, xrow, orow) in tiles:
                        if acol is None:
                            continue
                        xm = xms[k3 % 2]
                        xo = xos[k3 % 2]
                        k3 += 1
                        S.dma("sp", xm.t[:], xmid_d[col0:col0 + 128, :], reads=[xmid_tok], writes=[xm])
                        y = pY.next()
                        for half in range(2):
                            for j in range(NFC):
                                S.op("pe", lambda: nc.tensor.matmul(y.t[:, half * 512:(half + 1) * 512], lhsT=hid.t[:, j, acol:acol + 128], rhs=wout.t[:, j, half * 512:(half + 1) * 512],
                                                                    start=(j == 0), stop=(j == NFC - 1)), reads=[hid, wout], writes=[y])
                        ss = N.ss.next()
                        sq = N.sq.next()
                        emit_rstd(S, nc, y.t[:, :], y, 128, ss, sq, epsc)
                        S.op("dve", lambda: nc.vector.scalar_tensor_tensor(out=xo.t[:], in0=y.t[:], scalar=ss.t[:, 0:1], in1=GB.t[:, 2 + r, :], op0=ALU.mult, op1=ALU.mult),
                             reads=[y, ss, GB], writes=[xo])
                        S.op("pool", lambda: nc.gpsimd.tensor_tensor(out=xo.t[:], in0=xo.t[:], in1=xm.t[:], op=ALU.add), reads=[xo, xm], writes=[xo])
                        if last:
                            if r == 0:
                                S.dma("act", Z.out_d[orow:orow + 128, :], xo.t[:], reads=[xo], writes=[], out=True)
                        else:
                            S.dma("act", xnew_d[orow:orow + 128, :], xo.t[:], reads=[xo], writes=[xnew_tok], indep=True)


def build_fused(dbg=False, n_layers=4, test_mo=False):
    nc = bass.Bass("TRN2", target_bir_lowering=False)
    Z = Fz()
    Z.dr = lambda name, shape, dt=F32, kind="ExternalInput": nc.dram_tensor(name, list(shape), dt, kind=kind).ap()
    dr = Z.dr
    S = Sched(nc)
    S.relax_pe = True
    S.relax_war = True
    NTT = NB * TPB_
    Z.ngfm_d = dr("ngfm", [128, 4, 4, 8])
    Z.ng_d = dr("ng", [4, 4, D])
    Z.xin_d = dr("xin", [8 * RB, D])
    Z.xloc_in_d = dr("xloc_in", [RB, D])
    Z.me_wqk_d = dr("me_wqk", [2, D, 128]); Z.me_wgu_d = dr("me_wgu", [2, D, 576]); Z.me_wv_d = dr("me_wv", [2, D, 64])
    Z.me_wsT_d = dr("me_wsT", [2, 128, 128]); Z.me_bs_d = dr("me_bs", [2, 128, 1]); Z.me_bias_d = dr("me_bias", [2, 5, 128, 576])
    Z.me_woh_d = dr("me_woh", [2, 128, D])
    Z.mo_w_d = dr("mo_w", [2, D, 768]); Z.mo_woh_d = dr("mo_woh", [2, 256, D]); Z.mo_lg_d = dr("mo_lg", [2, 128, 2]); Z.mo_gn_d = dr("mo_gn", [2, 128, 256])
    Z.rope_d = dr("rope", [NLAT, 128, 256])
    Z.cE_d = dr("cE", [2, 128, 128]); Z.cM_d = dr("cM", [2, 128, 128]); Z.cQ_d = dr("cQ", [2, 128, 128]); Z.cK_d = dr("cK", [128, 2])
    Z.ffn_w_in_d = dr("ffn_w_in", [4, D, 2 * D_FF]); Z.ffn_w_out_d = dr("ffn_w_out", [4, D_FF, D])
    Z.convw_d = dr("convw", [4, 128, 2 * NFC, 3]); Z.convb_d = dr("convb", [4, 128, 2 * NFC])
    hmask_d = dr("hmask", [128, 4]); wb_d = dr("wb", [128, 2]); wnb_d = dr("wnb", [2, 8]); ident_d = dr("ident", [128, 128])
    Z.out_d = dr("out", [2048, D], kind="ExternalOutput")
    Z.modloc_d = dr("modloc", [12, 768], kind="Internal"); Z.modall_d = dr("modall", [96, 768], kind="Internal")
    Z.xall_d = dr("xall", [8 * RB, D], kind="Internal")
    Z.xloc_d = [dr(f"xloc{k}", [RB, D], kind="Internal") for k in range(2)]
    Z.yin_d = dr("yin", [8 * NSH, D], kind="Internal"); Z.yloc_d = dr("yloc", [NSH, D], kind="Internal")
    Z.xmid_d = dr("xmid", [1282, D], kind="Internal")
    Z.gS_d = dr("gS", [NTT * 128, 256], kind="Internal"); Z.oF_d = dr("oF", [NTT * 128, 256], kind="Internal")
    Z.modloc_tok = S.tok("modloc"); Z.modall_tok = S.tok("modall"); Z.xall_tok = S.tok("xall")
    Z.xloc_tok = [S.tok("xloc0"), S.tok("xloc1")]; Z.xloc_in_tok = S.tok("xlocin")
    Z.yin_tok = S.tok("yin"); Z.yloc_tok = S.tok("yloc")
    Z.gS_tok = S.shared_toks("gS", NTT, 4); Z.oF_tok = S.shared_toks("oF", NTT, 4)
    ident_f = S.sb("ident_f", [128, 128], F32)
    Z.ident = S.sb("ident_b", [128, 128], BF16)
    S.dma("sp", ident_f.t[:], ident_d, writes=[ident_f])
    S.op("dve", lambda: nc.vector.tensor_copy(Z.ident.t[:], ident_f.t[:]), reads=[ident_f], writes=[Z.ident])
    Z.epsc = S.sb("epsc", [128, 1], F32)
    S.op("dve", lambda: nc.vector.memset(Z.epsc.t[:], EPS), writes=[Z.epsc])
    Z.hmask = S.sb("hmask", [128, 4], F32); Z.wb = S.sb("wbsel", [128, 2], F32); Z.wnb = S.sb("wnb", [2, 8], F32)
    S.dma("sp", Z.hmask.t[:], hmask_d, writes=[Z.hmask])
    S.dma("sp", Z.wb.t[:], wb_d, writes=[Z.wb])
    S.dma("sp", Z.wnb.t[:], wnb_d, writes=[Z.wnb])
    zrow = S.sb("zrow", [1, D], F32)
    S.op("dve", lambda: nc.vector.memset(zrow.t[:], 0.0), writes=[zrow])
    yin3 = Z.yin_d.rearrange("(j n) d -> j n d", j=8)
    for b in range(NB):
        S.dma("sp", yin3[b * 4, HALO_BASE[0]:HALO_BASE[0] + 1, :], zrow.t[:], reads=[zrow], writes=[Z.yin_tok], indep=True)
        S.dma("sp", yin3[b * 4 + 3, HALO_BASE[1] + 1:HALO_BASE[1] + 2, :], zrow.t[:], reads=[zrow], writes=[Z.yin_tok], indep=True)
    emit_ADA(S, nc, Z)
    if test_mo:
        Z.xall_d = Z.xin_d
        dbg_yin = dr("dbg_yin", [NSH, D], kind="ExternalOutput")
        Z.dump_oA = dr("dump_oA", [128, 8, 256], BF16, kind="ExternalOutput")
        Z.dump_v = dr("dump_v", [128, 8, 256], BF16, kind="ExternalOutput")
        Z.dump_q = dr("dump_q", [128, 1024], BF16, kind="ExternalOutput")
        emit_MO_fused(S, nc, Z, 1)
        for k in range(0, NSH, 577):
            S.dma("sp", dbg_yin[k:k + 577, :], Z.yin_d[k:k + 577, :], reads=[Z.yin_tok], writes=[], out=True)
        S.finish()
        return nc
    if dbg:
        dbg_mod = dr("dbg_mod", [96, 768], kind="ExternalOutput")
        dbg_yloc = [dr(f"dbg_yloc{k}", [NSH, D], kind="ExternalOutput") for k in range(n_layers)]
        dbg_xloc = [dr(f"dbg_xloc{k}", [RB, D], kind="ExternalOutput") for k in range(n_layers)]
        S.dma("sp", dbg_mod, Z.modall_d, reads=[Z.modall_tok], writes=[], out=True)
    for i in range(n_layers):
        S.renew()
        if i % 2 == 0:
            emit_ME_fused(S, nc, Z, i)
        else:
            emit_MO_fused(S, nc, Z, i)
        S.coll("ReduceScatter", [Z.yin_d], [Z.yloc_d], reads=[Z.yin_tok], writes=[Z.yloc_tok], op=ALU.add)
        if dbg:
            S.dma("sp", dbg_yloc[i], Z.yloc_d, reads=[Z.yloc_tok], writes=[], out=True)
        S.renew()
        emit_F_fused(S, nc, Z, i, last=(i == 3))
        if dbg and i < 3:
            S.dma("sp", dbg_xloc[i], Z.xloc_d[i % 2], reads=[Z.xloc_tok[i % 2]], writes=[], out=True)
        if i < 3:
            for k in range(NAG):
                S.coll("AllGather", [Z.xloc_d[i % 2][k * AGC:(k + 1) * AGC, :]], [Z.xall_d[k * 8 * AGC:(k + 1) * 8 * AGC, :]],
                       reads=[Z.xloc_tok[i % 2]], writes=[Z.xall_tok])
    S.finish()
    return nc


def kernel(x, c, ctx, c_ctx, ada_w, ada_b, norm_g, hyb_w_in, na_rpb, sgu_w, sgu_b, hyb_w_out,
           ret_w_in, ret_log_decay, ret_gn_g, ret_w_out, ffn_w_in, ffn_conv_w, ffn_conv_b, ffn_w_out):
    f32 = lambda a: np.ascontiguousarray(np.asarray(a, dtype=np.float32))
    x, c, ctx, c_ctx, ada_w, ada_b, norm_g = map(f32, (x, c, ctx, c_ctx, ada_w, ada_b, norm_g))
    hyb_w_in, na_rpb, sgu_w, sgu_b, hyb_w_out = map(f32, (hyb_w_in, na_rpb, sgu_w, sgu_b, hyb_w_out))
    ret_w_in, ret_log_decay, ret_gn_g, ret_w_out = map(f32, (ret_w_in, ret_log_decay, ret_gn_g, ret_w_out))
    ffn_w_in, ffn_conv_w, ffn_conv_b, ffn_w_out = map(f32, (ffn_w_in, ffn_conv_w, ffn_conv_b, ffn_w_out))
    B, T, _ = x.shape
    cvec = np.stack([c[0], c[1], c_ctx], 0)
    cT = np.ascontiguousarray(cvec.T.reshape(8, 128, 3).transpose(1, 0, 2))
    xrk = np.stack([np.concatenate([x[r // 4, (r % 4) * 2048:(r % 4 + 1) * 2048], ctx[r // 4]], 0) for r in range(8)], 0)
    xin = np.ascontiguousarray(xrk.reshape(8, NAG, AGC, D).transpose(1, 0, 2, 3).reshape(8 * RB, D))
    ngfm = np.ascontiguousarray(norm_g.reshape(4, 4, 8, 128).transpose(3, 0, 1, 2))
    convw = np.ascontiguousarray(ffn_conv_w.reshape(4, 3, 2 * NFC, 128).transpose(0, 3, 2, 1))
    convb = np.ascontiguousarray(ffn_conv_b.reshape(4, 2 * NFC, 128).transpose(0, 2, 1))
    shared = {"cT": cT, "ngfm": ngfm, "ng": norm_g, "xin": xin, "ffn_w_in": ffn_w_in, "ffn_w_out": ffn_w_out,
              "convw": convw, "convb": convb, "ident": np.eye(128, dtype=np.float32)}
    maps = []
    for core in range(8):
        h = core
        b, q = divmod(core, 4)
        m = dict(shared)
        cs = slice(core * 768, (core + 1) * 768)
        m["ada_w"] = np.ascontiguousarray(ada_w[:, :, cs])
        m["ada_b"] = np.ascontiguousarray(ada_b[:, cs].reshape(4, 6, 128).transpose(2, 0, 1))
        m["xloc_in"] = np.ascontiguousarray(xrk[core])
        hs = slice(h * 64, (h + 1) * 64)
        wqk, wgu, wv, wsT, bs, bias, woh = [], [], [], [], [], [], []
        for j2 in range(2):
            w_in = hyb_w_in[j2]
            g_cols = w_in[:, 2048:2560]
            order = list(range(h * 64, (h + 1) * 64)) + [cc for cc in range(512) if not (h * 64 <= cc < (h + 1) * 64)]
            wqk.append(np.concatenate([w_in[:, 0:512][:, hs], w_in[:, 512:1024][:, hs]], 1))
            wgu.append(np.concatenate([g_cols[:, order], w_in[:, 1536:2048][:, hs]], 1))
            wv.append(w_in[:, 1024:1536][:, hs])
            wsT.append(sgu_w[j2, h].T)
            bs.append(sgu_b[j2, h][:, None])
            bias.append(na_bias_tables(na_rpb[j2, h], NLAT))
            woh.append(np.concatenate([hyb_w_out[j2][hs], hyb_w_out[j2][512 + h * 64:512 + (h + 1) * 64]], 0))
        m["me_wqk"], m["me_wgu"], m["me_wv"] = [np.ascontiguousarray(np.stack(v)) for v in (wqk, wgu, wv)]
        m["me_wsT"], m["me_bs"], m["me_bias"], m["me_woh"] = [np.ascontiguousarray(np.stack(v)) for v in (wsT, bs, bias, woh)]
        mo_w, mo_woh, mo_lg, mo_gn = [], [], [], []
        for j2 in range(2):
            w_in = ret_w_in[j2]
            mo_w.append(np.concatenate([w_in[:, h * 128:(h + 1) * 128], w_in[:, 1024 + h * 128:1024 + (h + 1) * 128],
                                        w_in[:, 2048 + h * 256:2048 + (h + 1) * 256], w_in[:, 4096 + h * 256:4096 + (h + 1) * 256]], 1))
            mo_woh.append(ret_w_out[j2][h * 256:(h + 1) * 256])
            mo_lg.append(np.broadcast_to(ret_log_decay[j2][:, h], (128, 2)))
            mo_gn.append(np.broadcast_to(ret_gn_g[j2][h * 256:(h + 1) * 256], (128, 256)))
        m["mo_w"], m["mo_woh"], m["mo_lg"], m["mo_gn"] = [np.ascontiguousarray(np.stack(v), dtype=np.float32) for v in (mo_w, mo_woh, mo_lg, mo_gn)]
        m.update(_mo_consts())
        hm = np.ones((128, 4), np.float32)
        hm[:, 0] = 1.0 if q > 0 else 0.0
        hm[:, 3] = 1.0 if q < 3 else 0.0
        m["hmask"] = hm
        wbv = np.zeros((128, 2), np.float32)
        wbv[:, b] = 1.0
        m["wb"] = wbv
        wnb = np.zeros((2, 8), np.float32)
        if q > 0:
            wnb[0, core - 1] = 1.0
        if q < 3:
            wnb[1, core + 1] = 1.0
        m["wnb"] = wnb
        maps.append(m)
    if _DBG.get("maps_only"):
        return maps
    res = _run(_prog("fused", build_fused), maps)
    out = np.empty((B, T, D), np.float32)
    for core in range(8):
        b, q = divmod(core, 4)
        out[b, q * 2048:(q + 1) * 2048] = res[core]["out"]
    return out


def _mo_consts():
    t = np.arange(NLAT * 128)
    row = (t // 64).astype(np.float32)
    col = (t % 64).astype(np.float32)
    inv = (10000.0 ** (-np.arange(32, dtype=np.float32) / 32)).astype(np.float32)
    ang = np.concatenate([row[:, None] * inv, col[:, None] * inv], -1).astype(np.float32)
    cos, sin = np.cos(ang).astype(np.float32), np.sin(ang).astype(np.float32)
    rope = np.concatenate([cos, cos, sin, sin], -1).reshape(NLAT, 128, 256)
    pos = np.arange(128, dtype=np.float32)
    kq = pos[None, :] - pos[:, None]
    cE = np.stack([np.where(kq >= 0, kq, 0), np.where(kq <= 0, -kq, 0)]).astype(np.float32)
    cM = np.stack([(kq >= 0), (kq <= 0)]).astype(np.float32)
    cQ = np.stack([np.broadcast_to(pos + 1, (128, 128)), np.broadcast_to(128 - pos, (128, 128))]).astype(np.float32)
    cK = np.stack([127 - pos, pos], 1).astype(np.float32)
    return {"rope": np.ascontiguousarray(rope), "cE": np.ascontiguousarray(cE), "cM": np.ascontiguousarray(cM),
            "cQ": np.ascontiguousarray(cQ), "cK": np.ascontiguousarray(cK)}
```
